# Optimizing a Trainium2 kernel written in Bass

```python
import jax, jax.numpy as jnp
from jax import lax
import numpy as np

D_MODEL = 2048
BATCH = 1
SEQ = 16384
DEPTH = 1

CHUNK = 64
D_MIX = D_MODEL
D_HGRN = D_MIX // 2
HGRN_HEAD_DIM = 128
HGRN_HEADS = D_HGRN // HGRN_HEAD_DIM
D_SSM = D_MIX - D_HGRN
SSM_HEAD_DIM = 64
SSM_HEADS = D_SSM // SSM_HEAD_DIM
SSM_GROUPS = 4
SSM_HPG = SSM_HEADS // SSM_GROUPS
SSM_STATE = 128
SSM_CONV = 4
SSM_CONV_DIM = D_SSM + 2 * SSM_GROUPS * SSM_STATE
D_IN_PROJ = 4 * D_HGRN + D_SSM + SSM_CONV_DIM + SSM_HEADS
D_FF = 5632
FFN_CONV = 3
N_MOD = 6
EPS = 1e-6
DT_MIN = 1e-3
DT_MAX = 1e-1
A_MIN = 1.0
A_MAX = 16.0

kernel_name = 'hybrid_hgrn2_ssd_convffn_adaln'


def rms_norm(x, w):
    xf = x.astype(jnp.float32)
    y = xf * lax.rsqrt(jnp.mean(xf * xf, axis=-1, keepdims=True) + EPS)
    return (y * w.astype(jnp.float32)).astype(x.dtype)


def causal_dwconv(x, w, b):
    width = w.shape[0]
    y = lax.conv_general_dilated(
        x, w[:, None, :].astype(x.dtype), window_strides=(1,), padding=[(width - 1, 0)],
        dimension_numbers=('NWC', 'WIO', 'NWC'), feature_group_count=x.shape[-1])
    return y + b.astype(x.dtype)


def hgrn2_mixer(q, f_pre, v, g, lb, gnorm_w):
    bsz, seq, _ = q.shape
    H, K, C = HGRN_HEADS, HGRN_HEAD_DIM, CHUNK
    nc = seq // C
    f32 = jnp.float32
    qf = jax.nn.silu(q.astype(f32)).reshape(bsz, seq, H, K)
    ff = f_pre.astype(f32).reshape(bsz, seq, H, K)
    vf = v.astype(f32).reshape(bsz, seq, H, K)
    lbh = lb.astype(f32).reshape(H, K)
    log_f = jnp.log(lbh + (1.0 - lbh) * jax.nn.sigmoid(ff))
    kf = (1.0 - lbh) * jax.nn.sigmoid(-ff)

    def to_chunks(t):
        return t.reshape(bsz, nc, C, H, K).transpose(1, 0, 3, 2, 4)

    qc, kc, vc, lc = to_chunks(qf), to_chunks(kf), to_chunks(vf), to_chunks(log_f)
    b = jnp.cumsum(lc, axis=3)
    b_last = b[:, :, :, -1:, :]
    q_dec = qc * jnp.exp(b)
    k_dec = kc * jnp.exp(b_last - b)
    chunk_decay = jnp.exp(b_last[:, :, :, 0, :])
    mask = jnp.tril(jnp.ones((C, C), dtype=bool))[:, :, None]

    def step(state, inp):
        qq, kk, vv, bb, qd, kd, dc = inp
        diff = bb[:, :, :, None, :] - bb[:, :, None, :, :]
        w = jnp.exp(jnp.where(mask, diff, -jnp.inf))
        att = jnp.einsum('bhtk,bhsk,bhtsk->bhts', qq, kk, w)
        out = jnp.einsum('bhts,bhsv->bhtv', att, vv) + jnp.einsum('bhtk,bhkv->bhtv', qd, state)
        state = dc[..., None] * state + jnp.einsum('bhsk,bhsv->bhkv', kd, vv)
        return state, out

    s0 = jnp.zeros((bsz, H, K, K), f32)
    _, o = lax.scan(step, s0, (qc, kc, vc, b, q_dec, k_dec, chunk_decay))
    o = o.transpose(1, 0, 3, 2, 4).reshape(bsz, seq, H, K)
    o = o * lax.rsqrt(jnp.mean(o * o, axis=-1, keepdims=True) + EPS)
    o = o.reshape(bsz, seq, D_HGRN) * gnorm_w.astype(f32) * jax.nn.silu(g.astype(f32))
    return o.astype(q.dtype)


def ssd_mixer(z, xbc, dt_raw, conv_w, conv_b, dt_bias, a_log, d_skip, norm_w):
    bsz, seq, _ = z.shape
    G, R, P, N, C = SSM_GROUPS, SSM_HPG, SSM_HEAD_DIM, SSM_STATE, CHUNK
    nc = seq // C
    f32 = jnp.float32
    xbc = jax.nn.silu(causal_dwconv(xbc, conv_w, conv_b)).astype(f32)
    xs, bm, cm = jnp.split(xbc, [D_SSM, D_SSM + G * N], axis=-1)
    x = xs.reshape(bsz, nc, C, G, R, P)
    bm = bm.reshape(bsz, nc, C, G, N)
    cm = cm.reshape(bsz, nc, C, G, N)
    dt = jax.nn.softplus(dt_raw.astype(f32) + dt_bias.astype(f32)).reshape(bsz, nc, C, G, R)
    a_head = -jnp.exp(a_log.astype(f32)).reshape(G, R)
    acum = jnp.cumsum((dt * a_head).transpose(0, 1, 3, 4, 2), axis=-1)
    xdt = x * dt[..., None]
    mask = jnp.tril(jnp.ones((C, C), dtype=bool))
    seg = jnp.exp(jnp.where(mask, acum[..., :, None] - acum[..., None, :], -jnp.inf))
    cb = jnp.einsum('bctgn,bcsgn->bcgts', cm, bm)
    y_diag = jnp.einsum('bcgts,bcgrts,bcsgrp->bctgrp', cb, seg, xdt)
    decay_states = jnp.exp(acum[..., -1:] - acum)
    states = jnp.einsum('bcsgn,bcgrs,bcsgrp->bcgrpn', bm, decay_states, xdt)
    chunk_decay = jnp.exp(acum[..., -1])

    def step(h, inp):
        st, dc = inp
        return dc[..., None, None] * h + st, h

    h0 = jnp.zeros((bsz, G, R, P, N), f32)
    _, h_prev = lax.scan(step, h0, (states.transpose(1, 0, 2, 3, 4, 5), chunk_decay.transpose(1, 0, 2, 3)))
    h_prev = h_prev.transpose(1, 0, 2, 3, 4, 5)
    y_off = jnp.einsum('bctgn,bcgrpn,bcgrt->bctgrp', cm, h_prev, jnp.exp(acum))
    y = y_diag + y_off + d_skip.astype(f32).reshape(G, R)[:, :, None] * x
    y = y.reshape(bsz, seq, D_SSM) * jax.nn.silu(z.astype(f32))
    yg = y.reshape(bsz, seq, G, D_SSM // G)
    yg = yg * lax.rsqrt(jnp.mean(yg * yg, axis=-1, keepdims=True) + EPS)
    y = yg.reshape(bsz, seq, D_SSM) * norm_w.astype(f32)
    return y.astype(z.dtype)


def setup_inputs(seed: int = 0) -> dict:
    key = jax.random.key(seed)
    ks = jax.random.split(key, 24)

    def nrm(k, shape, scale):
        return jax.random.normal(k, shape, jnp.float32) * scale

    dt0 = jnp.exp(jax.random.uniform(ks[10], (DEPTH, SSM_HEADS), jnp.float32,
                                     np.log(DT_MIN), np.log(DT_MAX)))
    return {
        'x': nrm(ks[0], (BATCH, SEQ, D_MODEL), 1.0),
        'c': nrm(ks[1], (BATCH, D_MODEL), 1.0),
        'w_mod': nrm(ks[2], (DEPTH, D_MODEL, N_MOD * D_MODEL), 0.5 * D_MODEL ** -0.5),
        'b_mod': nrm(ks[3], (DEPTH, N_MOD * D_MODEL), 0.02),
        'norm1_w': 1.0 + nrm(ks[4], (DEPTH, D_MODEL), 0.02),
        'w_in': nrm(ks[5], (DEPTH, D_MODEL, D_IN_PROJ), D_MODEL ** -0.5),
        'hgrn_lb': nrm(ks[6], (DEPTH + 1, D_HGRN), 1.0),
        'hgrn_gnorm_w': 1.0 + nrm(ks[7], (DEPTH, D_HGRN), 0.02),
        'ssd_conv_w': nrm(ks[8], (DEPTH, SSM_CONV, SSM_CONV_DIM), SSM_CONV ** -0.5),
        'ssd_conv_b': nrm(ks[9], (DEPTH, SSM_CONV_DIM), 0.02),
        'ssd_dt_bias': dt0 + jnp.log(-jnp.expm1(-dt0)),
        'ssd_a_log': jnp.log(jax.random.uniform(ks[11], (DEPTH, SSM_HEADS), jnp.float32, A_MIN, A_MAX)),
        'ssd_d': 1.0 + nrm(ks[12], (DEPTH, SSM_HEADS), 0.1),
        'ssd_norm_w': 1.0 + nrm(ks[13], (DEPTH, D_SSM), 0.02),
        'w_out': nrm(ks[14], (DEPTH, D_MIX, D_MODEL), D_MIX ** -0.5),
        'norm2_w': 1.0 + nrm(ks[15], (DEPTH, D_MODEL), 0.02),
        'ffn_w_up': nrm(ks[16], (DEPTH, D_MODEL, 2 * D_FF), D_MODEL ** -0.5),
        'ffn_conv_w': nrm(ks[17], (DEPTH, FFN_CONV, 2 * D_FF), FFN_CONV ** -0.5),
        'ffn_conv_b': nrm(ks[18], (DEPTH, 2 * D_FF), 0.02),
        'ffn_w_down': nrm(ks[19], (DEPTH, D_FF, D_MODEL), D_FF ** -0.5),
        'final_norm_w': 1.0 + nrm(ks[20], (D_MODEL,), 0.02),
    }


def reference(x, c, w_mod, b_mod, norm1_w, w_in, hgrn_lb, hgrn_gnorm_w, ssd_conv_w, ssd_conv_b,
              ssd_dt_bias, ssd_a_log, ssd_d, ssd_norm_w, w_out, norm2_w, ffn_w_up, ffn_conv_w,
              ffn_conv_b, ffn_w_down, final_norm_w):
    split_idx = [D_HGRN, 2 * D_HGRN, 3 * D_HGRN, 4 * D_HGRN, 4 * D_HGRN + D_SSM,
                 4 * D_HGRN + D_SSM + SSM_CONV_DIM]
    lower_bounds = jnp.cumsum(jax.nn.softmax(hgrn_lb.astype(jnp.float32), axis=0), axis=0)
    for l in range(DEPTH):
        mod = (jax.nn.silu(c) @ w_mod[l] + b_mod[l])[:, None, :]
        shift1, scale1, gate1, shift2, scale2, gate2 = jnp.split(mod, N_MOD, axis=-1)

        h = rms_norm(x, norm1_w[l]) * (1.0 + scale1) + shift1
        proj = h @ w_in[l]
        q_h, f_h, i_h, g_h, z_s, xbc_s, dt_s = jnp.split(proj, split_idx, axis=-1)
        o_a = hgrn2_mixer(q_h, f_h, i_h, g_h, lower_bounds[l], hgrn_gnorm_w[l])
        o_b = ssd_mixer(z_s, xbc_s, dt_s, ssd_conv_w[l], ssd_conv_b[l], ssd_dt_bias[l],
                        ssd_a_log[l], ssd_d[l], ssd_norm_w[l])
        mixed = jnp.concatenate([o_a, o_b], axis=-1) @ w_out[l]
        x = x + gate1 * mixed

        h = rms_norm(x, norm2_w[l]) * (1.0 + scale2) + shift2
        u = causal_dwconv(h @ ffn_w_up[l], ffn_conv_w[l], ffn_conv_b[l])
        u_gate, u_val = jnp.split(u, 2, axis=-1)
        x = x + gate2 * ((jax.nn.silu(u_gate) * u_val) @ ffn_w_down[l])
    return rms_norm(x, final_norm_w)
```

```python
import os
import numpy as np
from contextlib import ExitStack
import concourse.bass as bass
import concourse.mybir as mybir
from concourse.bass_utils import run_bass_kernel_spmd

F32 = mybir.dt.float32
BF16 = mybir.dt.bfloat16
ALU = mybir.AluOpType
AF = mybir.ActivationFunctionType

NCORES = 8
D = 2048
T = 2048
TA = 512
NT = T // TA
DIN = 7184
DFF = 5632
EPS = 1e-6

ENGS = ("pe", "act", "dve", "pool", "sp")
NDMA = 12

C_ID = 0
C_SCAN = 128
C_ATT = 640
C_NEG = 1152
C_BLK = 2176
C_SEL = 3200
C_DIAG = 4224
C_LM = 5248
C_W = 5312


class _Op:
    __slots__ = ("eng", "fn", "deps", "signal", "count", "is_dma", "dslot", "dval")

    def __init__(self, eng, fn, is_dma):
        self.eng = eng
        self.fn = fn
        self.deps = []
        self.signal = False
        self.count = 0
        self.is_dma = is_dma
        self.dslot = 0
        self.dval = 0


class Sched:
    def __init__(self):
        self.cbase = {e: 0 for e in ENGS}
        self.ndma = {e: 0 for e in ENGS}
        self.phase = 0
        self.begin()

    def begin(self):
        self.ops = {e: [] for e in ENGS}
        self.res = {}

    def op(self, eng, fn, r=(), w=(), dma=False):
        o = _Op(eng, fn, dma)
        deps = []
        for k in r:
            st = self.res.get(k)
            if st is not None and st[0] is not None:
                deps.append(st[0])
        for k in w:
            st = self.res.get(k)
            if st is not None:
                if st[0] is not None:
                    deps.append(st[0])
                deps.extend(st[1])
        seen = set()
        for d in deps:
            if id(d) in seen or d is o:
                continue
            seen.add(id(d))
            if d.eng == eng and eng == "pe" and not d.is_dma:
                continue
            o.deps.append(d)
            if not d.is_dma:
                d.signal = True
        if dma:
            n = self.ndma[eng]
            o.dslot = n % NDMA
            o.dval = 16 * (n // NDMA + 1)
            self.ndma[eng] = n + 1
        self.ops[eng].append(o)
        for k in r:
            st = self.res.setdefault(k, [None, []])
            st[1].append(o)
        for k in w:
            self.res[k] = [o, []]
        return o

    def emit(self, nc, block, sems, dsems, barsem):
        last_c = {}
        for e in ENGS:
            for o in reversed(self.ops[e]):
                if not o.is_dma:
                    o.signal = True
                    last_c[e] = o
                    break
        for e in ENGS:
            c = self.cbase[e]
            for o in self.ops[e]:
                if o.signal and not o.is_dma:
                    c += 1
                    o.count = c
            self.cbase[e] = c
        sched = self
        ndma_end = dict(self.ndma)
        phase = self.phase

        def run(e, eng):
            waited = {}

            def wait(sem, val, key):
                if waited.get(key, 0) >= val:
                    return
                waited[key] = val
                eng.wait_ge(sem, val)

            if phase > 0:
                eng.wait_ge(barsem, len(ENGS) * phase)
            for o in sched.ops[e]:
                for d in o.deps:
                    if d.is_dma:
                        wait(dsems[d.eng][d.dslot], d.dval, ("d", d.eng, d.dslot))
                    else:
                        wait(sems[d.eng], d.count, ("c", d.eng))
                if o.is_dma:
                    if o.dval > 16:
                        wait(dsems[e][o.dslot], o.dval - 16, ("d", e, o.dslot))
                    ins = o.fn(eng)
                    ins.then_inc(dsems[e][o.dslot], 16)
                else:
                    ins = o.fn(eng)
                    if o.signal:
                        ins.then_inc(sems[e], 1)
            n = ndma_end[e]
            for s in range(min(n, NDMA)):
                last = ((n - 1 - s) // NDMA) * NDMA + s
                wait(dsems[e][s], 16 * (last // NDMA + 1), ("d", e, s))
            if e in last_c:
                wait(sems[e], last_c[e].count, ("c", e))
            eng.sem_inc(barsem, 1)

        block.tensor(lambda eng: run("pe", eng))
        block.scalar(lambda eng: run("act", eng))
        block.vector(lambda eng: run("dve", eng))
        block.gpsimd(lambda eng: run("pool", eng))
        block.sync(lambda eng: run("sp", eng))
        self.phase += 1
        self.begin()


class _Stop(Exception):
    pass


class _Phase(ExitStack):
    def __exit__(self, et, ev, tb):
        r = super().__exit__(et, ev, tb)
        return bool(r) or (et is not None and issubclass(et, _Stop))


class _Lenient(ExitStack):
    truncated = False

    def __exit__(self, et, ev, tb):
        try:
            return super().__exit__(et, ev, tb)
        except AssertionError:
            if not _Lenient.truncated:
                raise
            return False


class _LazyIn:
    def __init__(self, nc, name, shape, used):
        self._nc, self._name, self._shape, self._used, self._apv = nc, name, list(shape), used, None

    def _ap(self):
        if self._apv is None:
            self._apv = self._nc.dram_tensor(self._name, self._shape, F32, kind="ExternalInput").ap()
            self._used.append(self._name)
        return self._apv

    def __getitem__(self, k):
        return self._ap()[k]

    def __getattr__(self, a):
        return getattr(self._ap(), a)


def build_program(upto=99, dbg=None):
    nc = bass.Bass("TRN2", target_bir_lowering=False)
    used_inputs = []
    nc._used_inputs = used_inputs
    dbg_outs = {}
    nc._dbg_outs = dbg_outs

    def din(name, shape):
        return _LazyIn(nc, name, shape, used_inputs)

    def dbg_out(name, shape, dt=F32):
        t = nc.dram_tensor("dbg_" + name, list(shape), dt, kind="ExternalOutput").ap()
        dbg_outs[name] = t
        return t

    x_c = din("x_c", [T + 3, D])
    flags_d = din("flags", [128, 16])
    cst = din("cst", [128, C_W])
    c_in = din("c", [1, D])
    w_mod = din("w_mod", [1, D, 6 * D])
    b_mod = din("b_mod", [1, 6 * D])
    norm1_w = din("norm1_w", [1, D])
    w_in = din("w_in", [1, D, DIN])
    hgrn_lb = din("hgrn_lb", [2, 1024])
    hgrn_gnw = din("hgrn_gnorm_w", [1, 1024])
    ssd_conv_w = din("ssd_conv_w", [1, 4, 2048])
    ssd_conv_b = din("ssd_conv_b", [1, 2048])
    ssd_dt_bias = din("ssd_dt_bias", [1, 16])
    ssd_a_log = din("ssd_a_log", [1, 16])
    ssd_d = din("ssd_d", [1, 16])
    ssd_norm_w = din("ssd_norm_w", [1, 1024])
    w_out = din("w_out", [1, D, D])
    norm2_w = din("norm2_w", [1, D])
    ffn_w_up = din("ffn_w_up", [1, D, 2 * DFF])
    ffn_conv_w = din("ffn_conv_w", [1, 3, 2 * DFF])
    ffn_conv_b = din("ffn_conv_b", [1, 2 * DFF])
    ffn_w_down = din("ffn_w_down", [1, DFF, D])
    final_w = din("final_norm_w", [D])
    y_out = nc.dram_tensor("y_out", [T, D], F32, kind="ExternalOutput").ap() if upto >= 7 else None

    mod_d = nc.dram_tensor("mod_d", [1, 6 * D], F32).ap()
    hT_d = nc.dram_tensor("hT_d", [128, 16, 3 + T], BF16).ap()
    hq_d = nc.dram_tensor("hq_d", [128, 8, T], BF16).ap()
    ho_d = nc.dram_tensor("ho_d", [128, 8, T], BF16).ap()
    hg_d = nc.dram_tensor("hg_d", [128, 8, T], BF16).ap()
    sz_d = nc.dram_tensor("sz_d", [128, 8, T], BF16).ap()
    yl_d = nc.dram_tensor("yl_d", [128, 8, T], BF16).ap()
    x1_d = nc.dram_tensor("x1_d", [T, D], F32).ap()
    h2T_d = nc.dram_tensor("h2T_d", [128, 16, 2 + T], BF16).ap()
    EXW = 3072
    EX_W = [1024, 1024, 1024]
    ex_in = [nc.dram_tensor(f"ex_in{i}", [128, w], F32) for i, w in enumerate(EX_W)]
    ex_out = [nc.dram_tensor(f"ex_out{i}", [NCORES * 128, w], F32) for i, w in enumerate(EX_W)]
    fence_in = nc.dram_tensor("fence_in", [128, 64], F32)
    fence_out = [nc.dram_tensor(f"fence_out{i}", [128, 64], F32) for i in range(2)]
    ex2_in = nc.dram_tensor("ex2_in", [128, 1024], F32)
    ex2_out = nc.dram_tensor("ex2_out", [NCORES * 128, 1024], F32)

    S = Sched()

    uid = [0]

    def sb(stack, name, shape, dt):
        uid[0] += 1
        if stack is None:
            return nc.alloc_sbuf_tensor(f"{name}_u{uid[0]}", list(shape), dt, side="right")
        return stack.enter_context(nc.sbuf_tensor(f"{name}_u{uid[0]}", list(shape), dt))

    def ps(stack, name, shape, dt):
        uid[0] += 1
        return stack.enter_context(nc.psum_tensor(f"{name}_u{uid[0]}", list(shape), dt))

    def ACT(out, in_, func, r, w, bias=None, scale=None, accum_out=None):
        kw = {}
        if bias is not None:
            kw["bias"] = bias
        if scale is not None:
            kw["scale"] = scale
        if accum_out is not None:
            kw["accum_out"] = accum_out
        S.op("act", lambda e: e.activation(out=out, in_=in_, func=func, **kw), r, w)

    def TT(eng, out, in0, in1, op, r, w):
        S.op(eng, lambda e: e.tensor_tensor(out=out, in0=in0, in1=in1, op=op), r, w)

    def TS(eng, out, in0, s1, s2, op0, op1, r, w):
        if s2 is None:
            S.op(eng, lambda e: e.tensor_scalar(out=out, in0=in0, scalar1=s1, scalar2=None, op0=op0), r, w)
        else:
            S.op(eng, lambda e: e.tensor_scalar(out=out, in0=in0, scalar1=s1, scalar2=s2, op0=op0, op1=op1), r, w)

    def STT(out, in0, scalar, in1, op0, op1, r, w):
        S.op("dve", lambda e: e.scalar_tensor_tensor(out=out, in0=in0, scalar=scalar, in1=in1, op0=op0, op1=op1), r, w)

    def CP(eng, out, in_, r, w):
        if eng == "act":
            ACT(out, in_, AF.Copy, r, w)
        else:
            S.op(eng, lambda e: e.tensor_copy(out=out, in_=in_), r, w)

    def MM(out, lhsT, rhs, start, stop, r, w):
        S.op("pe", lambda e: e.matmul(out, lhsT=lhsT, rhs=rhs, start=start, stop=stop), r, w)

    def TR(out, in_, ident, r, w):
        S.op("pe", lambda e: e.transpose(out=out, in_=in_, identity=ident), r, w)

    def DMA(eng, out, in_, r, w):
        S.op(eng, lambda e: e.dma_start(out=out, in_=in_), r, w, dma=True)

    def SCAN(out, d0, d1, init, r, w):
        S.op("dve", lambda e: e.tensor_tensor_scan(out=out, data0=d0, data1=d1, initial=init,
                                                    op0=ALU.mult, op1=ALU.add), r, w)

    def MEMSET(eng, ap, val, w):
        S.op(eng, lambda e: e.memset(ap, val), (), w)

    def RECIP(out, in_, r, w):
        S.op("dve", lambda e: e.reciprocal(out=out, in_=in_), r, w)

    def bc_last(ap, n):
        sh = list(ap.shape)
        return ap.unsqueeze(len(sh)).to_broadcast(sh + [n])

    if True:
     with ExitStack() as G:
        sems = {e: G.enter_context(nc.semaphore("s_" + e)) for e in ENGS}
        dsems = {e: [G.enter_context(nc.semaphore(f"d_{e}_{i}")) for i in range(NDMA)] for e in ENGS}
        barsem = G.enter_context(nc.semaphore("barsem"))
        G.enter_context(nc.allow_non_contiguous_dma(reason="small strided parameter loads"))

        phase_no = [0]

        def emit():
            S.emit(nc, block, sems, dsems, barsem)
            phase_no[0] += 1
            if phase_no[0] > upto:
                stopped[0] = True
                _Lenient.truncated = True
                raise _Stop()

        stopped = [False]

        def stop_check():
            if stopped[0]:
                raise _Stop()

        ident_f = sb(None, "ident_f", [128, 128], F32)
        ident_b = sb(None, "ident_b", [128, 128], BF16)
        ones_b = sb(None, "ones_b", [128, 128], BF16)
        flags = sb(None, "flags_sb", [128, 16], F32)
        one1 = sb(None, "one1", [128, 1], F32)
        epsb = sb(None, "epsb", [128, 1], F32)
        negh = sb(None, "negh", [128, 1], F32)
        modF = sb(None, "modF", [128, 96], F32)
        bmodF = sb(None, "bmodF", [128, 96], F32)
        nw1F = sb(None, "nw1F", [128, 16], F32)
        nw2F = sb(None, "nw2F", [128, 16], F32)
        a1 = sb(None, "a1", [128, 16], F32)
        a2 = sb(None, "a2", [128, 16], F32)
        block = G.enter_context(nc.Block())

        with _Phase() as P:
            stop_check()
            wp = [sb(P, f"wp{i}", [128, 2048], F32) for i in range(4)]
            modrow = [sb(P, f"modrow{i}", [1, 2048], F32) for i in range(2)]
            c_sb = sb(P, "c_sb", [128, 16], F32)
            cE = sb(P, "cE", [128, 16], F32)
            scv = sb(P, "scv", [128, 16], F32)
            psm = [ps(P, f"psm{n}", [128, 512], F32) for n in range(4)]

            DMA("sp", ident_f[:], cst[:, C_ID:C_ID + 128], (), ["ident_f"])
            DMA("pool", ident_b[:], cst[:, C_ID:C_ID + 128], (), ["ident_b"])
            DMA("sp", flags[:], flags_d[:, :], (), ["flags"])
            MEMSET("pool", ones_b[:], 1.0, ["ones_b"])
            MEMSET("pool", one1[:], 1.0, ["one1"])
            MEMSET("pool", epsb[:], EPS, ["epsb"])
            MEMSET("pool", negh[:], -0.5, ["negh"])
            DMA("sp", c_sb[:], c_in.rearrange("o (k p) -> p (o k)", p=128), (), ["c_sb"])
            ACT(cE[:], c_sb[:], AF.Exp, ["c_sb"], ["cE"], scale=-1.0)
            TS("dve", cE[:], cE[:], 1.0, None, ALU.add, None, ["cE"], ["cE"])
            RECIP(cE[:], cE[:], ["cE"], ["cE"])
            TT("dve", scv[:], c_sb[:], cE[:], ALU.mult, ["c_sb", "cE"], ["scv"])
            cnt = 0
            for cg in range(6):
                for k in range(16):
                    b = cnt % 4
                    cnt += 1
                    DMA("sp" if cnt % 2 else "act", wp[b][:], w_mod[0, k * 128:(k + 1) * 128, cg * 2048:(cg + 1) * 2048],
                        (), [("wp", b)])
                    for n in range(4):
                        MM(psm[n][0:1, :], scv[:, k:k + 1], wp[b][:, n * 512:(n + 1) * 512], k == 0, k == 15,
                           ["scv", ("wp", b)], [("psm", n)])
                mr = modrow[cg % 2]
                for n in range(4):
                    CP("act" if n % 2 else "dve", mr[0:1, n * 512:(n + 1) * 512], psm[n][0:1, :], [("psm", n)], [("mr", cg % 2, n)])
                DMA("sp", mod_d[0:1, cg * 2048:(cg + 1) * 2048], mr[0:1, :], [("mr", cg % 2, n) for n in range(4)], ["mod_d"])
            for s in range(6):
                DMA("sp", modF[:, s * 16:(s + 1) * 16],
                    mod_d[0:1, s * 2048:(s + 1) * 2048].rearrange("o (k p) -> p (o k)", p=128), ["mod_d"], ["modF"])
                DMA("act", bmodF[:, s * 16:(s + 1) * 16],
                    b_mod[0:1, s * 2048:(s + 1) * 2048].rearrange("o (k p) -> p (o k)", p=128), (), ["bmodF"])
            DMA("sp", nw1F[:], norm1_w.rearrange("o (k p) -> p (o k)", p=128), (), ["nw1F"])
            DMA("sp", nw2F[:], norm2_w.rearrange("o (k p) -> p (o k)", p=128), (), ["nw2F"])
            TT("dve", modF[:], modF[:], bmodF[:], ALU.add, ["modF", "bmodF"], ["modF"])
            STT(a1[:], modF[:, 16:32], 1.0, nw1F[:], ALU.add, ALU.mult, ["modF", "nw1F"], ["a1"])
            STT(a2[:], modF[:, 64:80], 1.0, nw2F[:], ALU.add, ALU.mult, ["modF", "nw2F"], ["a2"])
            if dbg:
                DMA("sp", dbg_out("modF", [128, 96])[:, :], modF[:], ["modF"], ["dbg0"])
                DMA("sp", dbg_out("a1", [128, 16])[:, :], a1[:], ["a1"], ["dbg1"])
                DMA("sp", dbg_out("a2", [128, 16])[:, :], a2[:], ["a2"], ["dbg2"])
            emit()
        shift1 = modF[:, 0:16]
        shift2 = modF[:, 48:64]

        def norm_to_T(P, tag, nsub, src_loader, avec, shvec, dstT, halo=False, flagmul=False):
            np_ = 3 if halo else 128
            junk = P["junk"]
            ss = P["ss"]
            xn = P["xn"]
            tp = P["tp"]
            for j in range(nsub):
                src, rk = src_loader(j)
                ACT(junk[0:np_, :], src, AF.Square, rk, [("ss", j)], accum_out=ss[0:np_, j:j + 1])
                TS("pool", ss[0:np_, j:j + 1], ss[0:np_, j:j + 1], 1.0 / D, EPS, ALU.mult, ALU.add, [("ss", j)], [("ss", j)])
                TT("pool", ss[0:np_, j:j + 1], ss[0:np_, j:j + 1], negh[0:np_, :], ALU.pow, [("ss", j), "negh"], [("ss", j)])
                ACT(xn[0:np_, j, :], src, AF.Copy, rk + [("ss", j)], [("xn", j)], scale=ss[0:np_, j:j + 1])
            ncol = nsub * np_ if not halo else 3
            for dk in range(16):
                tpb = tp[dk % 2]
                for j in range(nsub):
                    TR(tpb[:, j * np_:(j + 1) * np_] if not halo else tpb[:, 0:3],
                       xn[0:np_, j, dk * 128:(dk + 1) * 128],
                       ident_b[0:np_, 0:np_], [("xn", j), "ident_b"], [("tp", dk % 2)])
                if dk % 2 == 0:
                    ACT(dstT[:, dk, 0:ncol], tpb[:, 0:ncol], AF.Identity, [("tp", dk % 2), "a1", "a2", "modF"], [(tag, dk)],
                        bias=shvec[:, dk:dk + 1], scale=avec[:, dk:dk + 1])
                else:
                    TS("dve", dstT[:, dk, 0:ncol], tpb[:, 0:ncol], avec[:, dk:dk + 1], shvec[:, dk:dk + 1], ALU.mult, ALU.add,
                       [("tp", dk % 2), "a1", "a2", "modF"], [(tag, dk)])
                if flagmul:
                    TS("dve", dstT[:, dk, 0:ncol], dstT[:, dk, 0:ncol], flags[:, 0:1], None, ALU.mult, None,
                       [(tag, dk), "flags"], [(tag, dk)])

        with _Phase() as P:
            stop_check()
            bufs = {
                "junk": sb(P, "junk", [128, 2048], BF16),
                "ss": sb(P, "ss", [128, 4], F32),
                "xn": sb(P, "xn", [128, 4, 2048], BF16),
                "tp": [ps(P, f"tp{i}", [128, 512], BF16) for i in range(2)],
            }
            xt = [sb(P, f"xt{i}", [128, 2048], F32) for i in range(4)]
            hTt = [sb(P, f"hTt{i}", [128, 16, 512], BF16) for i in range(2)]
            hTh = sb(P, "hTh", [128, 16, 4], BF16)
            DMA("sp", xt[0][0:3, :], x_c[0:3, :], (), [("xt", 0)])
            norm_to_T(bufs, "hTh", 1, lambda j: (xt[0][0:3, :], [("xt", 0)]), a1, shift1, hTh, halo=True, flagmul=True)
            DMA("act", hT_d[:, :, 0:3], hTh[:, :, 0:3], [("hTh", dk) for dk in range(16)], ["hT_d"])
            for ti in range(NT):
                def loader(j, ti=ti):
                    return xt[j][:], [("xt", j)]
                for j in range(4):
                    r0 = 3 + ti * TA + j * 128
                    DMA("sp", xt[j][:], x_c[r0:r0 + 128, :], (), [("xt", j)])
                hb = hTt[ti % 2]
                norm_to_T(bufs, ("hTt", ti % 2), 4, loader, a1, shift1, hb)
                DMA("act", hT_d[:, :, 3 + ti * TA:3 + (ti + 1) * TA], hb[:], [(("hTt", ti % 2), dk) for dk in range(16)], ["hT_d"])
            if dbg:
                DMA("sp", dbg_out("hT", [128, 16, 3 + T], BF16)[:, :, :], hT_d, ["hT_d"], ["dbg0"])
            emit()

        w_in_v = w_in[0].rearrange("(k p) c -> p k c", p=128)

        with _Lenient() as M:
            lbv = sb(M, "lbv", [128, 8], F32)
            oml = sb(M, "oml", [128, 8], F32)
            gnw = sb(M, "gnw", [128, 8], F32)
            S_hg = sb(M, "S_hg", [128, 8, 128], F32)
            Bprev = sb(M, "Bprev", [128, 8], F32)
            Gall = sb(M, "Gall", [128, 8, 32], F32)
            Hs = sb(M, "Hs", [128, 1024], F32)
            acumG = sb(M, "acumG", [16, T], F32)
            CFall = sb(M, "CFall", [128, 4, T], BF16)
            Sin_hb = sb(M, "Sin_hb", [128, 8, 128], BF16)
            Sin_sb = sb(M, "Sin_sb", [128, 1024], BF16)
            snw = sb(M, "snw", [128, 8], F32)
            scanmask = sb(M, "scanmask", [128, 512], F32)

            with _Phase() as P:
                stop_check()
                lbr = sb(P, "lbr", [128, 2, 8], F32)
                hTt = [sb(P, f"hTt{i}", [128, 16, 512], BF16) for i in range(2)]
                wb = [sb(P, f"wb{i}", [128, 16, 256], BF16) for i in range(3)]
                qs_all = sb(P, "qs_all", [128, 8, 512], BF16)
                qt_all = sb(P, "qt_all", [128, 8, 512], BF16)
                kt_all = sb(P, "kt_all", [128, 8, 512], BF16)
                t0 = [sb(P, f"t0_{i}", [128, 512], F32) for i in range(2)]
                tE = sb(P, "tE", [128, 512], F32)
                tL1 = sb(P, "tL1", [128, 512], F32)
                tL2 = sb(P, "tL2", [128, 512], F32)
                tb = sb(P, "tb", [128, 512], F32)
                tq = sb(P, "tq", [128, 512], F32)
                tk = sb(P, "tk", [128, 512], F32)
                vF = [sb(P, f"vF{i}", [128, 512], BF16) for i in range(2)]
                kT_sb = sb(P, "kT_sb", [64, 1024], BF16)
                vT_sb = sb(P, "vT_sb", [64, 1024], BF16)
                attm = sb(P, "attm", [64, 512], F32)
                attT_sb = sb(P, "attT_sb", [64, 512], BF16)
                tmpU = sb(P, "tmpU", [128, 128], F32)
                Sp = sb(P, "Sp", [128, 128], BF16)
                o_sb = [sb(P, f"o_sb{i}", [128, 512], BF16) for i in range(2)]
                sgt = [sb(P, f"sgt{i}", [128, 512], BF16) for i in range(2)]
                sc_em = sb(P, "sc_em", [128, 8, 8], F32)
                sc_c1 = sb(P, "sc_c1", [128, 8, 8], F32)
                sc_c2 = sb(P, "sc_c2", [128, 8, 8], F32)
                bcum = sb(P, "bcum", [128, 8], F32)
                gex = sb(P, "gex", [128, 8], F32)
                ones8 = sb(P, "ones8", [128, 8], F32)
                pin = [ps(P, f"pin{i}", [128, 512], F32) for i in range(2)]
                kTp = ps(P, "kTp", [128, 1024], BF16)
                vTp = ps(P, "vTp", [128, 1024], BF16)
                attp = ps(P, "attp", [128, 512], F32)
                Up = ps(P, "Up", [128, 512], F32)
                op_ = ps(P, "op_", [128, 512], F32)

                DMA("sp", lbr[:], hgrn_lb.rearrange("r (h p) -> p r h", p=128), (), ["lbr"])
                DMA("sp", gnw[:], hgrn_gnw.rearrange("o (h p) -> p (o h)", p=128), (), ["gnw"])
                DMA("sp", scanmask[:], cst[:, C_SCAN:C_SCAN + 512], (), ["scanmask"])
                DMA("sp", attm[:], cst[0:64, C_ATT:C_ATT + 512], (), ["attm"])
                TT("dve", lbv[:], lbr[:, 1, :], lbr[:, 0, :], ALU.subtract, ["lbr"], ["lbv"])
                ACT(oml[:], lbv[:], AF.Exp, ["lbv"], ["oml"])
                TS("dve", lbv[:], oml[:], 1.0, None, ALU.add, None, ["oml"], ["lbv"])
                RECIP(lbv[:], lbv[:], ["lbv"], ["lbv"])
                TT("dve", oml[:], oml[:], lbv[:], ALU.mult, ["oml", "lbv"], ["oml"])
                MEMSET("pool", S_hg[:], 0.0, [("S_hg", h) for h in range(8)])
                MEMSET("pool", Bprev[:], 0.0, ["Bprev"])
                MEMSET("pool", ones8[:], 1.0, ["ones8"])

                wcnt = [0]

                def head_pipeline(ti, h):
                    tcol = ti * TA
                    qt = qt_all[:, h, :]
                    kt = kt_all[:, h, :]
                    vf = vF[h % 2]
                    for c in range(8):
                        TR(kTp[0:64, c * 128:(c + 1) * 128], kt[:, c * 64:(c + 1) * 64], ident_b[:], [("kt", h), "ident_b"], ["kTp"])
                    for c in range(8):
                        TR(vTp[0:64, c * 128:(c + 1) * 128], vf[:, c * 64:(c + 1) * 64], ident_b[:], [("vF", h % 2), "ident_b"], ["vTp"])
                    CP("act", kT_sb[:], kTp[0:64, :], ["kTp"], ["kT_sb"])
                    CP("dve", vT_sb[:], vTp[0:64, :], ["vTp"], ["vT_sb"])
                    for c in range(8):
                        MM(attp[0:64, c * 64:(c + 1) * 64], kt[:, c * 64:(c + 1) * 64], qt[:, c * 64:(c + 1) * 64], True, True,
                           [("kt", h), ("qt", h)], ["attp"])
                    TT("dve", attT_sb[:], attp[0:64, :], attm[:], ALU.mult, ["attp", "attm"], ["attT_sb"])
                    ob = o_sb[h % 2]
                    for c in range(8):
                        ACT(Sp[:], S_hg[:, h, :], AF.Copy, [("S_hg", h), ("sc", h)], ["Sp"], scale=sc_em[:, h, c:c + 1])
                        MM(op_[:, c * 64:(c + 1) * 64], vT_sb[:, c * 128:(c + 1) * 128], attT_sb[:, c * 64:(c + 1) * 64], True, False,
                           ["vT_sb", "attT_sb"], [("op", c)])
                        MM(op_[:, c * 64:(c + 1) * 64], Sp[:], qt[:, c * 64:(c + 1) * 64], False, True,
                           ["Sp", ("qt", h)], [("op", c)])
                        us = Up[:, (c % 4) * 128:(c % 4 + 1) * 128]
                        MM(us, kT_sb[:, c * 128:(c + 1) * 128], vT_sb[:, c * 128:(c + 1) * 128], True, True,
                           ["kT_sb", "vT_sb"], [("Up", c % 4)])
                        TS("dve", tmpU[:], us, sc_c2[:, h, c:c + 1], None, ALU.mult, None, [("Up", c % 4), ("sc", h)], ["tmpU"])
                        STT(S_hg[:, h, :], S_hg[:, h, :], sc_c1[:, h, c:c + 1], tmpU[:], ALU.mult, ALU.add,
                            [("S_hg", h), "tmpU", ("sc", h)], [("S_hg", h)])
                    CP("act", ob[:], op_[:], [("op", c) for c in range(8)], [("o_sb", h % 2)])
                    DMA("sp", ho_d[:, h, tcol:tcol + TA], ob[:], [("o_sb", h % 2)], ["ho_d"])
                    DMA("sp", hq_d[:, h, tcol:tcol + TA], qt, [("qt", h)], ["hq_d"])

                for ti in range(NT):
                    hb = hTt[ti % 2]
                    DMA("sp", hb[:], hT_d[:, :, 3 + ti * TA:3 + (ti + 1) * TA], (), [("hTt", ti % 2)])
                    for gi in range(16):
                        wi = wcnt[0] % 3
                        wcnt[0] += 1
                        DMA("pool", wb[wi][:], w_in_v[:, :, gi * 256:(gi + 1) * 256], (), [("wb", wi)])
                        for half in range(2):
                            cc = gi * 2 + half
                            kind, h = cc // 8, cc % 8
                            pp = pin[cc % 2]
                            pk = ("pin", cc % 2)
                            for k in range(16):
                                MM(pp[:], wb[wi][:, k, half * 128:(half + 1) * 128], hb[:, k, :], k == 0, k == 15,
                                   [("wb", wi), ("hTt", ti % 2)], [pk])
                            if kind == 0 or kind == 3:
                                tt = t0[cc % 2]
                                tk0 = ("t0", cc % 2)
                                ACT(tt[:], pp[:], AF.Exp, [pk], [tk0], scale=-1.0)
                                ACT(tt[:], tt[:], AF.Ln, [tk0], [tk0], bias=one1[:])
                                ACT(tt[:], tt[:], AF.Exp, [tk0], [tk0], scale=-1.0)
                                if kind == 0:
                                    TT("dve", qs_all[:, h, :], pp[:], tt[:], ALU.mult, [pk, tk0], [("qs", h)])
                                else:
                                    sg = sgt[h % 2]
                                    STT(sg[:], pp[:], gnw[:, h:h + 1], tt[:], ALU.mult, ALU.mult, [pk, tk0, "gnw"], [("sgt", h % 2)])
                                    DMA("act", hg_d[:, h, ti * TA:(ti + 1) * TA], sg[:], [("sgt", h % 2)], ["hg_d"])
                            elif kind == 1:
                                ACT(tE[:], pp[:], AF.Exp, [pk], ["tE"], scale=-1.0)
                                ACT(tL2[:], tE[:], AF.Ln, ["tE"], ["tL2"], bias=one1[:])
                                ACT(tL1[:], tE[:], AF.Ln, ["tE", "lbv"], ["tL1"], bias=one1[:], scale=lbv[:, h:h + 1])
                                TT("pool", tL1[:], tL1[:], tL2[:], ALU.subtract, ["tL1", "tL2"], ["tL1"])
                                SCAN(tb[:], scanmask[:], tL1[:], 0.0, ["scanmask", "tL1"], ["tb"])
                                tb3 = tb[:].rearrange("p (c t) -> p c t", t=64)
                                TT("dve", tL1[:].rearrange("p (c t) -> p c t", t=64), tb3,
                                   tb3[:, :, 31:32].to_broadcast([128, 8, 64]), ALU.subtract, ["tb"], ["tL1"])
                                ACT(tq[:], tL1[:], AF.Exp, ["tL1"], ["tq"])
                                ACT(tk[:], tL1[:], AF.Exp, ["tL1"], ["tk"], scale=-1.0)
                                ACT(tL2[:], tL2[:], AF.Exp, ["tL2"], ["tL2"], scale=-1.0)
                                STT(tE[:], tE[:], oml[:, h:h + 1], tL2[:], ALU.mult, ALU.mult, ["tE", "tL2", "oml"], ["tE"])
                                TT("dve", qt_all[:, h, :], qs_all[:, h, :], tq[:], ALU.mult, [("qs", h), "tq"], [("qt", h)])
                                TT("pool", kt_all[:, h, :], tE[:], tk[:], ALU.mult, ["tE", "tk"], [("kt", h)])
                                m_v = tb3[:, :, 31]
                                bl_v = tb3[:, :, 63]
                                ACT(sc_em[:, h, :], m_v, AF.Exp, ["tb"], [("sc", h)])
                                ACT(sc_c1[:, h, :], bl_v, AF.Exp, ["tb"], [("sc", h)])
                                TT("pool", gex[:], bl_v, m_v, ALU.subtract, ["tb"], ["gex"])
                                ACT(sc_c2[:, h, :], gex[:], AF.Exp, ["gex"], [("sc", h)])
                                SCAN(bcum[:], ones8[:], bl_v, Bprev[:, h:h + 1], ["ones8", "tb", ("Bprev", h), "Bprev"], ["bcum"])
                                TT("pool", gex[:], bcum[:], bl_v, ALU.subtract, ["bcum", "tb"], ["gex"])
                                TT("pool", gex[:], gex[:], m_v, ALU.add, ["gex", "tb"], ["gex"])
                                ACT(Gall[:, h, ti * 8:(ti + 1) * 8], gex[:], AF.Exp, ["gex"], ["Gall"])
                                CP("pool", Bprev[:, h:h + 1], bcum[:, 7:8], ["bcum"], [("Bprev", h)])
                            else:
                                CP("act", vF[h % 2][:], pp[:], [pk], [("vF", h % 2)])
                                head_pipeline(ti, h)
                if dbg:
                    DMA("sp", dbg_out("ho", [128, 8, T], BF16)[:, :, :], ho_d, ["ho_d"], ["dbg0"])
                    DMA("sp", dbg_out("hq", [128, 8, T], BF16)[:, :, :], hq_d, ["hq_d"], ["dbg1"])
                    DMA("sp", dbg_out("hg", [128, 8, T], BF16)[:, :, :], hg_d, ["hg_d"], ["dbg2"])
                    DMA("sp", dbg_out("S_hg", [128, 1024])[:, :], S_hg[:].rearrange("p h v -> p (h v)"), [("S_hg", h) for h in range(8)], ["dbg3"])
                    DMA("sp", dbg_out("Gall", [128, 256])[:, :], Gall[:].rearrange("p h c -> p (h c)"), ["Gall"], ["dbg4"])
                    DMA("sp", dbg_out("Bprev", [128, 8])[:, :], Bprev[:], [("Bprev", h) for h in range(8)], ["dbg5"])
                emit()

            with _Phase() as P:
                stop_check()
                hTt = [sb(P, f"hTt{i}", [128, 16, 512], BF16) for i in range(2)]
                hTh = sb(P, "hTh", [128, 16, 4], BF16)
                wb = [sb(P, f"wb{i}", [128, 16, 256], BF16) for i in range(3)]
                wdt = sb(P, "wdt", [128, 16, 16], BF16)
                convw = sb(P, "convw", [128, 16, 4], F32)
                convb = sb(P, "convb", [128, 16], F32)
                dtb = sb(P, "dtb", [16, 1], F32)
                a_h = sb(P, "a_h", [16, 1], F32)
                dB = sb(P, "dB", [128, 16], F32)
                dgm = sb(P, "dgm", [128, 1024], F32)
                blockm = sb(P, "blockm", [16, 1024], F32)
                ones16 = sb(P, "ones16", [16, 128], F32)
                rhs_m = sb(P, "rhs_m", [96, 1024], F32)
                lhsT_m = sb(P, "lhsT_m", [96, 64], F32)
                xhalo = sb(P, "xhalo", [128, 16, 3], F32)
                xr = [sb(P, f"xr{i}", [128, 515], F32) for i in range(2)]
                acc = [sb(P, f"acc{i}", [128, 512], F32) for i in range(2)]
                t0 = [sb(P, f"t0_{i}", [128, 512], F32) for i in range(2)]
                szt = [sb(P, f"szt{i}", [128, 512], BF16) for i in range(2)]
                XF = sb(P, "XF", [128, 8, 512], BF16)
                BFm = sb(P, "BFm", [128, 4, 512], BF16)
                dtE = sb(P, "dtE", [16, 512], F32)
                dtv = sb(P, "dtv", [16, 512], F32)
                dAv = sb(P, "dAv", [16, 512], F32)
                acum = sb(P, "acum", [16, 512], F32)
                ones512 = sb(P, "ones512", [16, 512], F32)
                acT = sb(P, "acT", [64, 16], F32)
                dtx = sb(P, "dtx", [128, 16], F32)
                edt = sb(P, "edt", [64, 16], F32)
                eatot = sb(P, "eatot", [128, 16], F32)
                diff = sb(P, "diff", [64, 1024], F32)
                Maug = sb(P, "Maug", [128, 1024], BF16)
                xaug = sb(P, "xaug", [128, 1024], BF16)
                xdec = sb(P, "xdec", [64, 1024], BF16)
                btok = sb(P, "btok", [64, 512], BF16)
                eR = sb(P, "eR", [128, 1024], BF16)
                Ct = sb(P, "Ct", [128, 1024], BF16)
                Hb = sb(P, "Hb", [128, 1024], BF16)
                y_sb = [sb(P, f"y_sb{i}", [128, 8, 512], BF16) for i in range(2)]
                pinU = ps(P, "pinU", [128, 1024], F32)
                repm = ps(P, "repm", [128, 512], F32)
                rep1 = ps(P, "rep1", [128, 512], F32)
                xTp = ps(P, "xTp", [128, 1024], BF16)
                yp = ps(P, "yp", [128, 512], F32)
                psml = ps(P, "psml", [128, 512], F32)
                bTp = ps(P, "bTp", [128, 1024], BF16)
                pin = [pinU[:, 0:512], pinU[:, 512:1024]]

                SETUPN = int(os.environ.get("K_A3_SETUP", 999))
                if SETUPN > 0:
                    for j in range(4):
                        DMA("sp", convw[:, :, j], ssd_conv_w[0, j:j + 1, :].rearrange("o (c p) -> p (o c)", p=128), (), ["convw"])
                if SETUPN > 1:
                    DMA("sp", convb[:], ssd_conv_b.rearrange("o (c p) -> p (o c)", p=128), (), ["convb"])
                if SETUPN > 2:
                    DMA("sp", dtb[:], ssd_dt_bias.rearrange("o h -> h o"), (), ["dtb"])
                if SETUPN > 3:
                    DMA("sp", a_h[:], ssd_a_log.rearrange("o h -> h o"), (), ["a_h"])
                if SETUPN > 4:
                    ACT(a_h[:], a_h[:], AF.Exp, ["a_h"], ["a_h"])
                if SETUPN > 5:
                    TS("dve", a_h[:], a_h[:], -1.0, None, ALU.mult, None, ["a_h"], ["a_h"])
                if SETUPN > 6:
                    DMA("sp", dB[:], ssd_d[0:1, :].partition_broadcast(128), (), ["dB"])
                if SETUPN > 7:
                    DMA("sp", dgm[64:128, :], cst[0:64, C_DIAG:C_DIAG + 1024], (), ["dgm"])
                if SETUPN > 8:
                    DMA("sp", blockm[:], cst[0:16, C_BLK:C_BLK + 1024], (), ["blockm"])
                if SETUPN > 9:
                    MEMSET("pool", ones16[:], 1.0, ["ones16"])
                if SETUPN > 10:
                    MEMSET("pool", ones512[:], 1.0, ["ones512"])
                if SETUPN > 11:
                    MEMSET("pool", rhs_m[:], 0.0, ["rhs_m"])
                if SETUPN > 12:
                    DMA("sp", rhs_m[32:96, :], cst[0:64, C_NEG:C_NEG + 1024], ["rhs_m"], ["rhs_m"])
                if SETUPN > 13:
                    DMA("sp", lhsT_m[:], cst[0:96, C_LM:C_LM + 64], (), ["lhsT_m"])
                if SETUPN > 14:
                    DMA("pool", wdt[:], w_in_v[:, :, 7168:7184], (), ["wdt"])
                if SETUPN > 15:
                    DMA("sp", hTh[:, :, 0:3], hT_d[:, :, 0:3], (), ["hTh"])
                if SETUPN > 16:
                    DMA("sp", snw[:], ssd_norm_w.rearrange("o (h p) -> p (o h)", p=128), (), ["snw"])
                if SETUPN > 17:
                    TT("dve", Maug[64:128, :].rearrange("p (h t) -> p h t", t=64), dgm[64:128, :].rearrange("p (h t) -> p h t", t=64),
                       bc_last(dB[64:128, :], 64), ALU.mult, ["dgm", "dB"], ["MaugC"])
                if SETUPN > 18:
                    MEMSET("pool", Hs[:], 0.0, ["Hs"])
                if SETUPN > 19:
                    MEMSET("pool", Hb[:], 0.0, ["Hb"])
                if SETUPN > 20:
                    MEMSET("pool", dtx[64:128, :], 1.0, ["dtxC"])


                wcnt = [0]
                A3_TILES = int(os.environ.get("K_A3_TILES", NT))
                A3_CHUNKS = int(os.environ.get("K_A3_CHUNKS", 8))
                A3_STAGE = int(os.environ.get("K_A3_STAGE", 99))

                def silu_evac(src, srck, tt, ttk, dst, dstk, eng="dve"):
                    ACT(tt, src, AF.Exp, [srck], [ttk], scale=-1.0)
                    ACT(tt, tt, AF.Ln, [ttk], [ttk], bias=one1[:])
                    ACT(tt, tt, AF.Exp, [ttk], [ttk], scale=-1.0)
                    TT(eng, dst, src, tt, ALU.mult, [srck, ttk], [dstk])

                for ti in range(A3_TILES):
                    tcol = ti * TA
                    hb = hTt[ti % 2]
                    DMA("sp", hb[:], hT_d[:, :, 3 + tcol:3 + tcol + TA], (), [("hTt", ti % 2)])
                    ys = y_sb[ti % 2]
                    for gi in range(16, 28):
                        wi = wcnt[0] % 3
                        wcnt[0] += 1
                        DMA("pool", wb[wi][:], w_in_v[:, :, gi * 256:(gi + 1) * 256], (), [("wb", wi)])
                        for half in range(2):
                            cc = gi * 2 + half
                            pp = pin[cc % 2]
                            pk = ("pin", cc % 2)
                            for k in range(16):
                                MM(pp, wb[wi][:, k, half * 128:(half + 1) * 128], hb[:, k, :], k == 0, k == 15,
                                   [("wb", wi), ("hTt", ti % 2)], [pk])
                            if cc < 40:
                                j = cc - 32
                                silu_evac(pp, pk, t0[cc % 2][:], ("t0", cc % 2), szt[cc % 2][:], ("szt", cc % 2))
                                DMA("act", sz_d[:, j, tcol:tcol + TA], szt[cc % 2][:], [("szt", cc % 2)], ["sz_d"])
                            else:
                                j = cc - 40
                                xx = xr[j % 2]
                                xk = ("xr", j % 2)
                                if ti == 0:
                                    for k in range(16):
                                        MM(psml[:, 400:403], wb[wi][:, k, half * 128:(half + 1) * 128], hTh[:, k, 0:3], k == 0, k == 15,
                                           [("wb", wi), "hTh"], ["psml_h"])
                                    CP("act", xx[:, 0:3], psml[:, 400:403], ["psml_h"], [xk])
                                else:
                                    CP("pool", xx[:, 0:3], xhalo[:, j, :], [("xhalo", j)], [xk])
                                ACT(xx[:, 3:515], pp, AF.Copy, [pk, xk], [xk])
                                CP("pool", xhalo[:, j, :], xx[:, 512:515], [xk], [("xhalo", j)])
                                ac = acc[j % 2]
                                ak = ("acc", j % 2)
                                TS("dve", ac[:], xx[:, 3:515], convw[:, j, 3:4], convb[:, j:j + 1], ALU.mult, ALU.add, [xk, "convw", "convb"], [ak])
                                for tap in (2, 1, 0):
                                    STT(ac[:], xx[:, tap:tap + 512], convw[:, j, tap:tap + 1], ac[:], ALU.mult, ALU.add, [xk, ak, "convw"], [ak])
                                if j < 8:
                                    dst, dk_ = XF[:, j, :], ("XF", j)
                                elif j < 12:
                                    dst, dk_ = BFm[:, j - 8, :], ("BF", j - 8)
                                else:
                                    dst, dk_ = CFall[:, j - 12, tcol:tcol + TA], ("CF", j - 12)
                                silu_evac(ac[:], ak, t0[cc % 2][:], ("t0", cc % 2), dst, dk_, eng="pool")
                    for k in range(16):
                        MM(repm[0:16, :], wdt[:, k, :], hb[:, k, :], k == 0, k == 15, ["wdt", ("hTt", ti % 2)], ["repm"])
                    ACT(dtE[:], repm[0:16, :], AF.Exp, ["repm", "dtb"], ["dtE"], bias=dtb[:])
                    ACT(dtv[:], dtE[:], AF.Ln, ["dtE"], ["dtv"], bias=one1[0:16, :])
                    TS("dve", dAv[:], dtv[:], a_h[:, 0:1], None, ALU.mult, None, ["dtv", "a_h"], ["dAv"])
                    SCAN(acum[:], scanmask[0:16, :], dAv[:], 0.0, ["scanmask", "dAv"], ["acum"])
                    SCAN(acumG[:, tcol:tcol + TA], ones512[:], dAv[:], 0.0 if ti == 0 else acumG[:, tcol - 1:tcol],
                         ["ones512", "dAv", "acumG"], ["acumG"])
                    for c in range(A3_CHUNKS):
                        cs = c * 64
                        TT("pool", rhs_m[0:16, :].rearrange("p (h t) -> p h t", t=64),
                           blockm[:].rearrange("p (h t) -> p h t", t=64),
                           acum[:, cs:cs + 64].unsqueeze(1).to_broadcast([16, 16, 64]), ALU.mult,
                           ["blockm", "acum"], ["Dg"])
                        MM(yp[0:64, 0:16], acum[:, cs:cs + 64], ident_f[0:16, 0:16], True, True, ["acum", "ident_f"], ["yp"])
                        MM(rep1[0:64, 0:16], dtv[:, cs:cs + 64], ident_f[0:16, 0:16], True, True, ["dtv", "ident_f"], ["rep1"])
                        CP("act", acT[:], yp[0:64, 0:16], ["yp"], ["acT"])
                        CP("act", dtx[0:64, :], rep1[0:64, 0:16], ["rep1"], ["dtx"])
                        for g in range(4):
                            MM(psml[0:64, 64 + g * 64:64 + (g + 1) * 64], BFm[:, g, cs:cs + 64], CFall[:, g, tcol + cs:tcol + cs + 64], True, True,
                               [("BF", g), ("CF", g)], ["psml_cb"])
                        for j in range(8):
                            TR(xTp[0:64, j * 128:(j + 1) * 128], XF[:, j, cs:cs + 64], ident_b[:], [("XF", j), "ident_b"], ["xTp"])
                            TR(xTp[64:128, j * 128:(j + 1) * 128], XF[:, j, cs:cs + 64], ident_b[:], [("XF", j), "ident_b"], ["xTp"])
                        TT("dve", xaug[:].rearrange("p (h q) -> p h q", q=64), xTp[:].rearrange("p (h q) -> p h q", q=64),
                           bc_last(dtx[:], 64), ALU.mult, ["xTp", "dtx", "dtxC"], ["xaug"])
                        for g in range(4):
                            TR(bTp[0:64, g * 128:(g + 1) * 128], BFm[:, g, cs:cs + 64], ident_b[:], [("BF", g), "ident_b"], ["bTp"])
                        CP("act", btok[:], bTp[0:64, 0:512], ["bTp"], ["btok"])
                        for hf in range(2):
                            hs = slice(hf * 512, (hf + 1) * 512)
                            MM(repm[0:64, :], lhsT_m[:], rhs_m[:, hs], True, True, ["lhsT_m", "rhs_m", "Dg"], ["repm"])
                            MM(rep1[:], ones16[:], rhs_m[0:16, hs], True, True, ["ones16", "Dg"], ["rep1"])
                            r3 = repm[0:64, :].rearrange("p (h t) -> p h t", t=64)
                            TT("dve", diff[:, hs].rearrange("p (h t) -> p h t", t=64), r3,
                               bc_last(acT[:, hf * 8:(hf + 1) * 8], 64), ALU.subtract, ["repm", "acT"], [("diff", hf)])
                            TT("dve", edt[:, hf * 8:(hf + 1) * 8], r3[:, :, 63], acT[:, hf * 8:(hf + 1) * 8], ALU.subtract,
                               ["repm", "acT"], ["edt"])
                            ACT(diff[:, hs], diff[:, hs], AF.Exp, [("diff", hf)], [("diff", hf)])
                            ACT(eR[:, hs], rep1[:], AF.Exp, ["rep1"], [("eR", hf)])
                            ACT(eatot[:, hf * 8:(hf + 1) * 8], rep1[:].rearrange("p (h t) -> p h t", t=64)[:, :, 63], AF.Exp,
                                ["rep1"], [("eatot", hf)])
                            TT("dve", Maug[0:64, hs].rearrange("p (g r t) -> p g r t", r=4, t=64),
                               diff[:, hs].rearrange("p (g r t) -> p g r t", r=4, t=64),
                               psml[0:64, 64 + hf * 128:64 + (hf + 1) * 128].rearrange("p (g t) -> p g t", t=64).unsqueeze(2).to_broadcast([64, 2, 4, 64]),
                               ALU.mult, [("diff", hf), "psml_cb"], [("Maug", hf)])
                            TT("pool", Ct[:, hs].rearrange("p (g r t) -> p g r t", r=4, t=64),
                               eR[:, hs].rearrange("p (g r t) -> p g r t", r=4, t=64),
                               CFall[:, hf * 2:(hf + 1) * 2, tcol + cs:tcol + cs + 64].unsqueeze(2).to_broadcast([128, 2, 4, 64]),
                               ALU.mult, [("eR", hf), ("CF", 2 * hf), ("CF", 2 * hf + 1)], [("Ct", hf)])
                        ACT(edt[:], edt[:], AF.Exp, ["edt"], ["edt"])
                        TT("pool", xdec[:].rearrange("p (h q) -> p h q", q=64), xaug[0:64, :].rearrange("p (h q) -> p h q", q=64),
                           bc_last(edt[:], 64), ALU.mult, ["xaug", "edt"], ["xdec"])
                        for h in range(16):
                            jp, hh = h // 2, h % 2
                            o_ap = yp[hh * 64:(hh + 1) * 64, jp * 64:(jp + 1) * 64]
                            MM(o_ap, xaug[:, h * 64:(h + 1) * 64], Maug[:, h * 64:(h + 1) * 64], True, False,
                               ["xaug", ("Maug", h // 8), "MaugC"], ["yp"])
                            MM(o_ap, Hb[:, h * 64:(h + 1) * 64], Ct[:, h * 64:(h + 1) * 64], False, True,
                               ["Hb", ("Ct", h // 8)], ["yp"])
                        CP("act", ys[:, :, cs:cs + 64], yp[:].rearrange("p (j t) -> p j t", t=64), ["yp"], [("y_sb", ti % 2)])
                        for g in range(4):
                            MM(pinU[:, g * 256:(g + 1) * 256], btok[:, g * 128:(g + 1) * 128], xdec[:, g * 256:(g + 1) * 256], True, True,
                               ["btok", "xdec"], [("pin", g // 2)])
                        TT("dve", Hs[:].rearrange("p (h q) -> p h q", q=64), Hs[:].rearrange("p (h q) -> p h q", q=64),
                           bc_last(eatot[:], 64), ALU.mult, ["Hs", ("eatot", 0), ("eatot", 1)], ["Hs"])
                        TT("dve", Hs[:], Hs[:], pinU[:], ALU.add, ["Hs", ("pin", 0), ("pin", 1)], ["Hs"])
                        CP("act", Hb[:], Hs[:], ["Hs"], ["Hb"])
                    DMA("sp", yl_d[:, :, tcol:tcol + TA], ys[:], [("y_sb", ti % 2)], ["yl_d"])
                if dbg:
                    DMA("sp", dbg_out("yl", [128, 8, T], BF16)[:, :, :], yl_d, ["yl_d"], ["dbg0"])
                    DMA("sp", dbg_out("sz", [128, 8, T], BF16)[:, :, :], sz_d, ["sz_d"], ["dbg1"])
                    DMA("sp", dbg_out("Hs", [128, 1024])[:, :], Hs[:], ["Hs"], ["dbg2"])
                    DMA("sp", dbg_out("acumG", [16, T])[:, :], acumG[:], ["acumG"], ["dbg3"])
                    DMA("sp", dbg_out("CF", [128, 4, T], BF16)[:, :, :], CFall[:], [("CF", g) for g in range(4)], ["dbg4"])
                    if A3_CHUNKS == 1:
                        DMA("sp", dbg_out("acT", [64, 16])[:, :], acT[:], ["acT"], ["dbg5"])
                        DMA("sp", dbg_out("dtx", [128, 16])[:, :], dtx[:], ["dtx"], ["dbg6"])
                        DMA("sp", dbg_out("Wm", [64, 1024])[:, :], diff[:], [("diff", 0), ("diff", 1)], ["dbg7"])
                        DMA("sp", dbg_out("Maug", [128, 1024], BF16)[:, :], Maug[:], [("Maug", 0), ("Maug", 1)], ["dbg8"])
                        DMA("sp", dbg_out("xaug", [128, 1024], BF16)[:, :], xaug[:], ["xaug"], ["dbg9"])
                        DMA("sp", dbg_out("xdec", [64, 1024], BF16)[:, :], xdec[:], ["xdec"], ["dbg10"])
                        DMA("sp", dbg_out("btok", [64, 512], BF16)[:, :], btok[:], ["btok"], ["dbg11"])
                        DMA("sp", dbg_out("Ct", [128, 1024], BF16)[:, :], Ct[:], [("Ct", 0), ("Ct", 1)], ["dbg12"])
                        DMA("sp", dbg_out("eatot", [128, 16])[:, :], eatot[:], [("eatot", 0), ("eatot", 1)], ["dbg13"])
                        DMA("sp", dbg_out("edt", [64, 16])[:, :], edt[:], ["edt"], ["dbg14"])
                        DMA("sp", dbg_out("acum", [16, 512])[:, :], acum[:], ["acum"], ["dbg15"])
                        DMA("sp", dbg_out("dtv", [16, 512])[:, :], dtv[:], ["dtv"], ["dbg16"])
                emit()

            with _Phase() as P:
                stop_check()
                xbuf = sb(P, "xbuf", [128, EXW], F32)
                rblk = sb(P, "rblk", [128, 7, EXW], F32)
                Sin_h = sb(P, "Sin_h", [128, 1024], F32)
                Sin_s = sb(P, "Sin_s", [128, 1024], F32)
                Tm = sb(P, "Tm", [128, 1024], F32)
                Tm2 = sb(P, "Tm2", [128, 1024], F32)
                agl = sb(P, "agl", [16, 16], F32)
                ones16 = sb(P, "ones16x", [16, 128], F32)
                pr = ps(P, "pr", [128, 512], F32)

                MEMSET("pool", xbuf[:, 2072:3072], 0.0, ["xbuf5"])
                CP("dve", xbuf[:, 0:1024], S_hg[:].rearrange("p h v -> p (h v)"), (), ["xbuf"])
                CP("pool", xbuf[:, 1024:2048], Hs[:], (), ["xbuf2"])
                ACT(xbuf[:, 2048:2056], Bprev[:], AF.Exp, (), ["xbuf3"])
                MEMSET("pool", ones16[:], 1.0, ["ones16x"])
                TT("dve", agl[:], ident_f[0:16, 0:16], acumG[:, T - 1:T].to_broadcast([16, 16]), ALU.mult, (), ["agl"])
                MM(pr[:, 0:16], ones16[:], agl[:], True, True, ["ones16x", "agl"], ["pr"])
                ACT(xbuf[:, 2056:2072], pr[:, 0:16], AF.Exp, ["pr"], ["xbuf4"])
                NOX = int(os.environ.get("K_NOX", 0))
                if not NOX:
                    off = 0
                    for i, w in enumerate(EX_W):
                        DMA("sp", ex_in[i].ap(), xbuf[:, off:off + w], ["xbuf", "xbuf2", "xbuf3", "xbuf4", "xbuf5"], [("ex_in", i)])

                        def cc1(e, i=i):
                            return e.collective_compute("AllGather", ALU.bypass, replica_groups=[list(range(NCORES))],
                                                        ins=[ex_in[i].ap().opt()], outs=[ex_out[i].ap().opt()])
                        S.op("pool", cc1, [("ex_in", i)], [("ex_out", i)])
                        off += w
                    DMA("sp", fence_in.ap(), xbuf[:, 2072:2136], ["xbuf5"], ["fence_in"])

                    def ccf(e):
                        return e.collective_compute("AllReduce", ALU.add, replica_groups=[list(range(NCORES))],
                                                    ins=[fence_in.ap().opt()], outs=[fence_out[0].ap().opt()])
                    S.op("pool", ccf, ["fence_in"] + [("ex_out", i) for i in range(3)], ["fence"])
                    off = 0
                    for i, w in enumerate(EX_W):
                        DMA("sp", rblk[:, :, off:off + w], ex_out[i].ap()[0:7 * 128, :].rearrange("(r p) w -> p r w", p=128),
                            [("ex_out", i), "fence"], ["rblk"])
                        off += w
                else:
                    MEMSET("pool", rblk[:], 0.0, ["rblk"])
                MEMSET("pool", Sin_h[:], 0.0, ["Sin_h"])
                MEMSET("pool", Sin_s[:], 0.0, ["Sin_s"])
                for r in range(7):
                    mcol = flags[:, 1 + r:2 + r]
                    TT("dve", Tm[:].rearrange("p (h v) -> p h v", v=128), Sin_h[:].rearrange("p (h v) -> p h v", v=128),
                       bc_last(rblk[:, r, 2048:2056], 128), ALU.mult, ["Sin_h", "rblk"], ["Tm"])
                    TT("dve", Tm[:], Tm[:], rblk[:, r, 0:1024], ALU.add, ["Tm", "rblk"], ["Tm"])
                    TT("dve", Tm[:], Tm[:], Sin_h[:], ALU.subtract, ["Tm", "Sin_h"], ["Tm"])
                    STT(Sin_h[:], Tm[:], mcol, Sin_h[:], ALU.mult, ALU.add, ["Tm", "Sin_h", "flags"], ["Sin_h"])
                    TT("pool", Tm2[:].rearrange("p (h q) -> p h q", q=64), Sin_s[:].rearrange("p (h q) -> p h q", q=64),
                       bc_last(rblk[:, r, 2056:2072], 64), ALU.mult, ["Sin_s", "rblk"], ["Tm2"])
                    TT("pool", Tm2[:], Tm2[:], rblk[:, r, 1024:2048], ALU.add, ["Tm2", "rblk"], ["Tm2"])
                    TT("pool", Tm2[:], Tm2[:], Sin_s[:], ALU.subtract, ["Tm2", "Sin_s"], ["Tm2"])
                    TS("pool", Tm2[:], Tm2[:], mcol, None, ALU.mult, None, ["Tm2", "flags"], ["Tm2"])
                    TT("pool", Sin_s[:], Sin_s[:], Tm2[:], ALU.add, ["Tm2", "Sin_s"], ["Sin_s"])
                CP("act", Sin_hb[:].rearrange("p h v -> p (h v)"), Sin_h[:], ["Sin_h"], ["Sin_hb"])
                CP("act", Sin_sb[:], Sin_s[:], ["Sin_s"], ["Sin_sb"])
                if dbg:
                    DMA("sp", dbg_out("rblk", [128, 7, EXW])[:, :, :], rblk[:], ["rblk"], ["dbg0"])
                    DMA("sp", dbg_out("Sin_h", [128, 1024])[:, :], Sin_h[:], ["Sin_h"], ["dbg1"])
                    DMA("sp", dbg_out("Sin_s", [128, 1024])[:, :], Sin_s[:], ["Sin_s"], ["dbg2"])
                    DMA("sp", dbg_out("xbuf", [128, EXW])[:, :], xbuf[:], ["xbuf", "xbuf2", "xbuf3", "xbuf4"], ["dbg3"])
                emit()

            with _Phase() as P:
                stop_check()
                bufs = {
                    "junk": sb(P, "junk", [128, 2048], BF16),
                    "ss": sb(P, "ss", [128, 4], F32),
                    "xn": sb(P, "xn", [128, 4, 2048], BF16),
                    "tp": [ps(P, f"tp{i}", [128, 512], BF16) for i in range(2)],
                }
                g1b = sb(P, "g1b", [128, 2048], F32)
                selm = sb(P, "selm", [16, 1024], F32)
                hq = [sb(P, f"hq{i}", [128, 512], BF16) for i in range(2)]
                ho = [sb(P, f"ho{i}", [128, 512], BF16) for i in range(2)]
                hg = [sb(P, f"hg{i}", [128, 512], BF16) for i in range(2)]
                yl = [sb(P, f"yl{i}", [128, 512], BF16) for i in range(2)]
                szl = [sb(P, f"szl{i}", [128, 512], BF16) for i in range(2)]
                Qg = [sb(P, f"Qg{i}", [128, 512], BF16) for i in range(2)]
                tO = [sb(P, f"tO{i}", [128, 512], F32) for i in range(4)]
                sq = [sb(P, f"sq{i}", [128, 512], BF16) for i in range(4)]
                rs = [sb(P, f"rs{i}", [128, 512], F32) for i in range(2)]
                eP = [sb(P, f"eP{i}", [128, 512], F32) for i in range(2)]
                mixT = sb(P, "mixT", [128, 16, 512], BF16)
                wo = [sb(P, f"wo{i}", [128, 16, 256], BF16) for i in range(2)]
                x1t = sb(P, "x1t", [128, 4, 2048], F32)
                tmpx = [sb(P, f"tmpx{i}", [128, 256], F32) for i in range(2)]
                h2Tt = sb(P, "h2Tt", [128, 16, 512], BF16)
                h2l = sb(P, "h2l", [128, 1024], F32)
                pc = [ps(P, f"pc{i}", [128, 512], F32) for i in range(2)]
                pss = [ps(P, f"pss{i}", [128, 512], F32) for i in range(2)]
                prp = ps(P, "prp", [128, 512], F32)
                w_out_v = w_out[0].rearrange("(k p) d -> p k d", p=128)

                DMA("sp", g1b[:], mod_d[0:1, 2 * 2048:3 * 2048].partition_broadcast(128), (), ["g1b"])
                DMA("act", x1t[:, 0, :], b_mod[0:1, 2 * 2048:3 * 2048].partition_broadcast(128), (), [("x1t", 0)])
                TT("pool", g1b[:], g1b[:], x1t[:, 0, :], ALU.add, ["g1b", ("x1t", 0)], ["g1b"])
                DMA("sp", selm[:], cst[0:16, C_SEL:C_SEL + 1024], (), ["selm"])
                wcnt = 0
                for ti in range(NT):
                    tcol = ti * TA
                    for j in range(4):
                        r0 = 3 + tcol + j * 128
                        DMA("act", x1t[:, j, :], x_c[r0:r0 + 128, :], (), [("x1t", j)])
                    for h in range(8):
                        b2 = h % 2
                        DMA("sp", hq[b2][:], hq_d[:, h, tcol:tcol + TA], (), [("hq", b2)])
                        DMA("sp", ho[b2][:], ho_d[:, h, tcol:tcol + TA], (), [("ho", b2)])
                        DMA("sp", hg[b2][:], hg_d[:, h, tcol:tcol + TA], (), [("hg", b2)])
                        q_ = Qg[b2]
                        TT("pool", q_[:].rearrange("p (c t) -> p c t", t=64), hq[b2][:].rearrange("p (c t) -> p c t", t=64),
                           bc_last(Gall[:, h, ti * 8:(ti + 1) * 8], 64), ALU.mult, [("hq", b2)], [("Qg", b2)])
                        MM(pc[b2][:], Sin_hb[:, h, :], q_[:], True, True, [("Qg", b2)], [("pc", b2)])
                        to = tO[h % 4]
                        tok = ("tO", h % 4)
                        TT("dve", to[:], pc[b2][:], ho[b2][:], ALU.add, [("pc", b2), ("ho", b2)], [tok])
                        ACT(sq[h % 4][:], to[:], AF.Square, [tok], [("sq", h % 4)])
                        MM(pss[b2][:], ones_b[:], sq[h % 4][:], True, True, [("sq", h % 4)], [("pss", b2)])
                        r_ = rs[b2]
                        ACT(r_[:], pss[b2][:], AF.Ln, [("pss", b2)], [("rs", b2)], bias=epsb[:], scale=1.0 / 128)
                        ACT(r_[:], r_[:], AF.Exp, [("rs", b2)], [("rs", b2)], scale=-0.5)
                        TT("dve", to[:], to[:], r_[:], ALU.mult, [tok, ("rs", b2)], [tok])
                        TT("pool", mixT[:, h, :], to[:], hg[b2][:], ALU.mult, [tok, ("hg", b2)], [("mixT", h)])
                    for g in range(4):
                        for jj in range(2):
                            j = 2 * g + jj
                            DMA("sp", yl[jj][:], yl_d[:, j, tcol:tcol + TA], (), [("yl", jj)])
                            DMA("sp", szl[jj][:], sz_d[:, j, tcol:tcol + TA], (), [("szl", jj)])
                            for hh in range(2):
                                hd = 2 * j + hh
                                MM(pc[jj][hh * 64:(hh + 1) * 64, :], Sin_sb[:, hd * 64:(hd + 1) * 64], CFall[:, g, tcol:tcol + TA], True, True,
                                   (), [("pc", jj)])
                            MM(prp[:], selm[:, j * 128:(j + 1) * 128], acumG[:, tcol:tcol + TA], True, True, ["selm"], ["prp"])
                            ACT(eP[jj][:], prp[:], AF.Exp, ["prp"], [("eP", jj)])
                            to = tO[jj]
                            tok = ("tO", jj)
                            TT("dve", to[:], pc[jj][:], eP[jj][:], ALU.mult, [("pc", jj), ("eP", jj)], [tok])
                            TT("pool", to[:], to[:], yl[jj][:], ALU.add, [tok, ("yl", jj)], [tok])
                            TT("pool", to[:], to[:], szl[jj][:], ALU.mult, [tok, ("szl", jj)], [tok])
                            ACT(sq[jj][:], to[:], AF.Square, [tok], [("sq", jj)])
                            MM(pss[0][:], ones_b[:], sq[jj][:], jj == 0, jj == 1, [("sq", jj)], [("pss", 0)])
                        r_ = rs[0]
                        ACT(r_[:], pss[0][:], AF.Ln, [("pss", 0)], [("rs", 0)], bias=epsb[:], scale=1.0 / 256)
                        ACT(r_[:], r_[:], AF.Exp, [("rs", 0)], [("rs", 0)], scale=-0.5)
                        for jj in range(2):
                            j = 2 * g + jj
                            STT(mixT[:, 8 + j, :], tO[jj][:], snw[:, j:j + 1], r_[:], ALU.mult, ALU.mult,
                                [("tO", jj), ("rs", 0), "snw"], [("mixT", 8 + j)])
                    for dg in range(8):
                        wi = wcnt % 2
                        wcnt += 1
                        DMA("pool", wo[wi][:], w_out_v[:, :, dg * 256:(dg + 1) * 256], (), [("wo", wi)])
                        for sub in range(4):
                            pq = pc[sub % 2]
                            for k in range(16):
                                MM(pq[:, 0:256], mixT[:, k, sub * 128:(sub + 1) * 128], wo[wi][:, k, :], k == 0, k == 15,
                                   [("mixT", k), ("wo", wi)], [("pc", sub % 2)])
                            tx = tmpx[sub % 2]
                            TT("dve", tx[:], pq[:, 0:256], g1b[:, dg * 256:(dg + 1) * 256], ALU.mult, [("pc", sub % 2), "g1b"], [("tmpx", sub % 2)])
                            TT("pool", x1t[:, sub, dg * 256:(dg + 1) * 256], x1t[:, sub, dg * 256:(dg + 1) * 256], tx[:], ALU.add,
                               [("tmpx", sub % 2), ("x1t", sub)], [("x1t", sub)])
                    for j in range(4):
                        DMA("sp", x1_d[tcol + j * 128:tcol + (j + 1) * 128, :], x1t[:, j, :], [("x1t", j)], ["x1_d"])
                    norm_to_T(bufs, "h2Tt", 4, lambda j: (x1t[:, j, :], [("x1t", j)]), a2, shift2, h2Tt)
                    DMA("act", h2T_d[:, :, 2 + tcol:2 + tcol + TA], h2Tt[:], [("h2Tt", dk) for dk in range(16)], ["h2T_d"])
                    if ti == NT - 1:
                        MEMSET("pool", h2l[:], 0.0, ["h2l"])
                        CP("dve", h2l[:, 0:32].rearrange("p (k t) -> p k t", t=2), h2Tt[:, :, 510:512], [("h2Tt", dk) for dk in range(16)] + ["h2l"], ["h2l"])
                        DMA("sp", ex2_in.ap(), h2l[:], ["h2l"], ["ex2_in"])
                if dbg:
                    DMA("sp", dbg_out("x1", [T, D])[:, :], x1_d, ["x1_d"], ["dbg0"])
                    DMA("sp", dbg_out("h2T", [128, 16, 2 + T], BF16)[:, :, :], h2T_d, ["h2T_d"], ["dbg1"])
                emit()

        with _Phase() as P:
            stop_check()
            r2 = sb(P, "r2", [128, 7, 32], F32)
            hacc = sb(P, "hacc", [128, 32], F32)
            hout = sb(P, "hout", [128, 16, 2], BF16)

            def cc2(e):
                return e.collective_compute("AllGather", ALU.bypass, replica_groups=[list(range(NCORES))],
                                            ins=[ex2_in.ap().opt()], outs=[ex2_out.ap().opt()])
            if not int(os.environ.get("K_NOX", 0)):
                S.op("pool", cc2, (), ["ex2_out"])

                def ccf2(e):
                    return e.collective_compute("AllReduce", ALU.add, replica_groups=[list(range(NCORES))],
                                                ins=[fence_in.ap().opt()], outs=[fence_out[1].ap().opt()])
                S.op("pool", ccf2, ["ex2_out"], ["fence2"])
                DMA("sp", r2[:], ex2_out.ap()[0:7 * 128, 0:32].rearrange("(r p) w -> p r w", p=128), ["ex2_out", "fence2"], ["r2"])
            else:
                MEMSET("pool", r2[:], 0.0, ["r2"])
            MEMSET("pool", hacc[:], 0.0, ["hacc"])
            for r in range(7):
                STT(hacc[:], r2[:, r, :], flags[:, 8 + r:9 + r], hacc[:], ALU.mult, ALU.add, ["r2", "hacc"], ["hacc"])
            if int(os.environ.get("K_NOX", 0)) == 2:
                DMA("sp", hacc[:], ex2_in.ap()[:, 0:32], ["hacc"], ["hacc"])
            CP("dve", hout[:].rearrange("p k t -> p (k t)"), hacc[:], ["hacc"], ["hout"])
            DMA("sp", h2T_d[:, :, 0:2], hout[:], ["hout"], ["h2T_d"])
            if dbg:
                DMA("sp", dbg_out("r2", [128, 7, 32])[:, :, :], r2[:], ["r2"], ["dbg0"])
                DMA("sp", dbg_out("hacc", [128, 32])[:, :], hacc[:], ["hacc"], ["dbg1"])
                DMA("sp", dbg_out("h2Tb", [128, 16, 2 + T], BF16)[:, :, :], h2T_d, ["h2T_d"], ["dbg2"])
            emit()

        with _Phase() as P:
            stop_check()
            g2b = sb(P, "g2b", [128, 2048], F32)
            fwb = sb(P, "fwb", [128, 2048], F32)
            fcw = sb(P, "fcw", [128, 88, 3], F32)
            fcb = sb(P, "fcb", [128, 88], F32)
            uhalo = sb(P, "uhalo", [128, 88, 2], F32)
            h2Tt = [sb(P, "h2Tt0", [128, 16, 514], BF16)] * 2
            a_half = sb(P, "a_half", [128, 22, 512], BF16)
            wu = [sb(P, f"wu{i}", [128, 16, 2, 256], BF16) for i in range(2)]
            wd = [sb(P, f"wd{i}", [128, 22, 512], BF16) for i in range(2)]
            ur = [sb(P, f"ur{i}", [128, 514], F32) for i in range(2)]
            ac2 = [sb(P, f"ac2_{i}", [128, 512], F32) for i in range(2)]
            sgf = sb(P, "sgf", [128, 512], F32)
            x2t = sb(P, "x2t", [128, 4, 2048], F32)
            tmpx = [sb(P, f"tmpx{i}", [128, 512], F32) for i in range(2)]
            ssf = sb(P, "ssf", [128, 4], F32)
            junk = sb(P, "junk", [128, 2048], BF16)
            ppu = [ps(P, f"ppu{i}", [128, 512], F32) for i in range(4)]
            ppd = [ps(P, f"ppd{i}", [128, 512], F32) for i in range(2)]
            pph = ps(P, "pph", [128, 512], F32)
            w_up_v = ffn_w_up[0].rearrange("(k p) c -> p k c", p=128)
            w_dn_v = ffn_w_down[0].rearrange("(i p) d -> p i d", p=128)

            DMA("sp", g2b[:], mod_d[0:1, 5 * 2048:6 * 2048].partition_broadcast(128), (), ["g2b"])
            DMA("act", fwb[:], b_mod[0:1, 5 * 2048:6 * 2048].partition_broadcast(128), (), ["fwb"])
            TT("pool", g2b[:], g2b[:], fwb[:], ALU.add, ["g2b", "fwb"], ["g2b"])
            DMA("sp", fwb[:], final_w.rearrange("(o d) -> o d", o=1).partition_broadcast(128), ["fwb"], ["fwb"])
            for j in range(3):
                DMA("sp", fcw[:, :, j], ffn_conv_w[0, j:j + 1, :].rearrange("o (c p) -> p (o c)", p=128), (), ["fcw"])
            DMA("sp", fcb[:], ffn_conv_b.rearrange("o (c p) -> p (o c)", p=128), (), ["fcb"])
            wcu = 0
            wcd = 0
            for ti in range(NT):
                tcol = ti * TA
                hb = h2Tt[0]
                hk = ("h2Tt", 0)
                DMA("sp", hb[:], h2T_d[:, :, tcol:tcol + 514], (), [hk])
                for j in range(4):
                    DMA("act", x2t[:, j, :], x1_d[tcol + j * 128:tcol + (j + 1) * 128, :], (), [("x2t", j)])
                for hf in range(2):
                    for gi in range(11):
                        wi = wcu % 2
                        wcu += 1
                        f0 = (hf * 22 + gi * 2) * 128
                        DMA("pool", wu[wi][:, :, 0, :], w_up_v[:, :, f0:f0 + 256], (), [("wu", wi)])
                        DMA("pool", wu[wi][:, :, 1, :], w_up_v[:, :, DFF + f0:DFF + f0 + 256], (), [("wu", wi)])
                        for half in range(2):
                            fl = gi * 2 + half
                            fc = hf * 22 + fl
                            accs = []
                            for gv in range(2):
                                ch = gv * 44 + fc
                                pp = ppu[(fl % 2) * 2 + gv]
                                pk = ("ppu", (fl % 2) * 2 + gv)
                                for k in range(16):
                                    MM(pp[:], wu[wi][:, k, gv, half * 128:(half + 1) * 128], hb[:, k, 2:514], k == 0, k == 15,
                                       [("wu", wi), hk], [pk])
                                u = ur[gv]
                                uk = ("ur", gv)
                                if ti == 0:
                                    for k in range(16):
                                        MM(pph[:, gv * 2:gv * 2 + 2], wu[wi][:, k, gv, half * 128:(half + 1) * 128], hb[:, k, 0:2], k == 0, k == 15,
                                           [("wu", wi), hk], [("pph", gv)])
                                    CP("act", u[:, 0:2], pph[:, gv * 2:gv * 2 + 2], [("pph", gv)], [uk])
                                else:
                                    CP("pool", u[:, 0:2], uhalo[:, ch, :], [("uhalo", ch)], [uk])
                                ACT(u[:, 2:514], pp[:], AF.Copy, [pk, uk], [uk])
                                CP("pool", uhalo[:, ch, :], u[:, 512:514], [uk], [("uhalo", ch)])
                                ac = ac2[gv]
                                ak = ("ac2", gv)
                                TS("dve", ac[:], u[:, 2:514], fcw[:, ch, 2:3], fcb[:, ch:ch + 1], ALU.mult, ALU.add, [uk, "fcw", "fcb"], [ak])
                                STT(ac[:], u[:, 1:513], fcw[:, ch, 1:2], ac[:], ALU.mult, ALU.add, [uk, ak, "fcw"], [ak])
                                STT(ac[:], u[:, 0:512], fcw[:, ch, 0:1], ac[:], ALU.mult, ALU.add, [uk, ak, "fcw"], [ak])
                                accs.append((ac, ak))
                            ACT(sgf[:], accs[0][0][:], AF.Silu, [accs[0][1]], ["sgf"])
                            TT("pool", a_half[:, fl, :], sgf[:], accs[1][0][:], ALU.mult, ["sgf", accs[1][1]], [("a_half", fl)])
                    for dg in range(4):
                        wi = wcd % 2
                        wcd += 1
                        DMA("pool", wd[wi][:], w_dn_v[:, hf * 22:(hf + 1) * 22, dg * 512:(dg + 1) * 512], (), [("wd", wi)])
                        for sub in range(4):
                            pq = ppd[sub % 2]
                            for i in range(22):
                                MM(pq[:], a_half[:, i, sub * 128:(sub + 1) * 128], wd[wi][:, i, :], i == 0, i == 21,
                                   [("a_half", i), ("wd", wi)], [("ppd", sub % 2)])
                            tx = tmpx[sub % 2]
                            TT("dve", tx[:], pq[:], g2b[:, dg * 512:(dg + 1) * 512], ALU.mult, [("ppd", sub % 2), "g2b"], [("tmpx", sub % 2)])
                            TT("pool", x2t[:, sub, dg * 512:(dg + 1) * 512], x2t[:, sub, dg * 512:(dg + 1) * 512], tx[:], ALU.add,
                               [("tmpx", sub % 2), ("x2t", sub)], [("x2t", sub)])
                for j in range(4):
                    ACT(junk[:], x2t[:, j, :], AF.Square, [("x2t", j)], [("ssf", j)], accum_out=ssf[:, j:j + 1])
                    TS("pool", ssf[:, j:j + 1], ssf[:, j:j + 1], 1.0 / D, EPS, ALU.mult, ALU.add, [("ssf", j)], [("ssf", j)])
                    TT("pool", ssf[:, j:j + 1], ssf[:, j:j + 1], negh[:], ALU.pow, [("ssf", j)], [("ssf", j)])
                    STT(x2t[:, j, :], x2t[:, j, :], ssf[:, j:j + 1], fwb[:], ALU.mult, ALU.mult, [("x2t", j), ("ssf", j), "fwb"], [("x2t", j)])
                    DMA("sp", y_out[tcol + j * 128:tcol + (j + 1) * 128, :], x2t[:, j, :], [("x2t", j)], ["y_out"])
            emit()
    return nc


def _consts():
    c = np.zeros((128, C_W), np.float32)
    c[:, C_ID:C_ID + 128] = np.eye(128, dtype=np.float32)
    sm = np.ones(512, np.float32)
    sm[::64] = 0.0
    c[:, C_SCAN:C_SCAN + 512] = sm[None, :]
    s = np.arange(64)[:, None]
    t = np.arange(64)[None, :]
    m01 = (t >= s).astype(np.float32)
    c[0:64, C_ATT:C_ATT + 512] = np.tile(m01, (1, 8))
    neg = np.where(t >= s, 0.0, -30000.0).astype(np.float32)
    c[0:64, C_NEG:C_NEG + 1024] = np.tile(neg, (1, 16))
    for h in range(16):
        c[h, C_BLK + h * 64:C_BLK + (h + 1) * 64] = 1.0
        c[h, C_SEL + h * 64:C_SEL + (h + 1) * 64] = 1.0
    c[0:64, C_DIAG:C_DIAG + 1024] = np.tile(np.eye(64, dtype=np.float32), (1, 16))
    lm = np.zeros((96, 64), np.float32)
    lm[0:16, :] = 1.0
    lm[32:96, :] = np.eye(64, dtype=np.float32)
    c[0:96, C_LM:C_LM + 64] = lm
    return c


_NC_CACHE = {}


def kernel(**inputs):
    x = np.ascontiguousarray(np.asarray(inputs["x"], dtype=np.float32))[0]
    if "nc" not in _NC_CACHE:
        _NC_CACHE["nc"] = build_program()
    nc = _NC_CACHE["nc"]
    cst = _consts()
    shared = {}
    for k in ("c", "w_mod", "b_mod", "norm1_w", "w_in", "hgrn_lb", "hgrn_gnorm_w", "ssd_conv_w", "ssd_conv_b",
              "ssd_dt_bias", "ssd_a_log", "ssd_d", "ssd_norm_w", "w_out", "norm2_w", "ffn_w_up", "ffn_conv_w",
              "ffn_conv_b", "ffn_w_down", "final_norm_w"):
        shared[k] = np.ascontiguousarray(np.asarray(inputs[k], dtype=np.float32))
    in_maps = []
    for r in range(NCORES):
        xc = np.zeros((T + 3, D), np.float32)
        if r > 0:
            xc[0:3] = x[r * T - 3:r * T]
        xc[3:] = x[r * T:(r + 1) * T]
        fl = np.zeros((128, 16), np.float32)
        fl[:, 0] = 1.0 if r > 0 else 0.0
        for q in range(7):
            fl[:, 1 + q] = 1.0 if q < r else 0.0
            fl[:, 8 + q] = 1.0 if q == r - 1 else 0.0
        m = dict(shared)
        m["x_c"] = xc
        m["flags"] = fl
        m["cst"] = cst
        in_maps.append(m)
    res = run_bass_kernel_spmd(nc, in_maps, core_ids=list(range(NCORES)))
    out = np.concatenate([res.results[r]["y_out"] for r in range(NCORES)], axis=0)
    return out.reshape(1, NCORES * T, D).astype(np.float32)
```

```python
import os
import numpy as np
from contextlib import ExitStack
import concourse.bass as bass
import concourse.mybir as mybir
from concourse.bass_utils import run_bass_kernel_spmd

F32 = mybir.dt.float32
BF16 = mybir.dt.bfloat16
ALU = mybir.AluOpType
AF = mybir.ActivationFunctionType

NCORES = 8
D = 2048
T = 2048
TA = 512
NT = T // TA
DIN = 7184
DFF = 5632
EPS = 1e-6

ENGS = ("pe", "act", "dve", "pool", "sp")
NDMA = 12
NBG = 16

C_ID = 0
C_SCAN = 128
C_ATT = 640
C_NEG = 1152
C_BLK = 2176
C_SEL = 3200
C_DIAG = 4224
C_LM = 5248
C_W = 5312


class _Op:
    __slots__ = ("eng", "fn", "deps", "signal", "count", "is_dma", "dslot", "dval", "bg")

    def __init__(self, eng, fn, is_dma):
        self.eng = eng
        self.fn = fn
        self.deps = []
        self.signal = False
        self.count = 0
        self.is_dma = is_dma
        self.dslot = 0
        self.dval = 0
        self.bg = False


class Sched:
    def __init__(self):
        self.cbase = {e: 0 for e in ENGS}
        self.ndma = {e: 0 for e in ENGS}
        self.nbg = 0
        self.drain_bg = False
        self.phase = 0
        self.begin()

    def begin(self):
        self.ops = {e: [] for e in ENGS}
        self.res = {}

    def op(self, eng, fn, r=(), w=(), dma=False, bg=False):
        o = _Op(eng, fn, dma)
        o.bg = bg
        deps = []
        for k in r:
            st = self.res.get(k)
            if st is not None and st[0] is not None:
                deps.append(st[0])
        for k in w:
            st = self.res.get(k)
            if st is not None:
                if st[0] is not None:
                    deps.append(st[0])
                deps.extend(st[1])
        seen = set()
        for d in deps:
            if id(d) in seen or d is o:
                continue
            seen.add(id(d))
            if d.eng == eng and eng == "pe" and not d.is_dma:
                continue
            o.deps.append(d)
            if not d.is_dma:
                d.signal = True
        if dma and bg:
            n = self.nbg
            o.dslot = n % NBG
            o.dval = 16 * (n // NBG + 1)
            self.nbg = n + 1
        elif dma:
            n = self.ndma[eng]
            o.dslot = n % NDMA
            o.dval = 16 * (n // NDMA + 1)
            self.ndma[eng] = n + 1
        self.ops[eng].append(o)
        for k in r:
            st = self.res.setdefault(k, [None, []])
            st[1].append(o)
        for k in w:
            self.res[k] = [o, []]
        return o

    def emit(self, nc, block, sems, dsems, barsem, bgsems=None):
        last_c = {}
        for e in ENGS:
            for o in reversed(self.ops[e]):
                if not o.is_dma:
                    o.signal = True
                    last_c[e] = o
                    break
        for e in ENGS:
            c = self.cbase[e]
            for o in self.ops[e]:
                if o.signal and not o.is_dma:
                    c += 1
                    o.count = c
            self.cbase[e] = c
        sched = self
        ndma_end = dict(self.ndma)
        nbg_end = self.nbg
        drain_bg = self.drain_bg
        self.drain_bg = False
        phase = self.phase

        def run(e, eng):
            waited = {}

            def wait(sem, val, key):
                if waited.get(key, 0) >= val:
                    return
                waited[key] = val
                eng.wait_ge(sem, val)

            if phase > 0:
                eng.wait_ge(barsem, len(ENGS) * phase)
            for o in sched.ops[e]:
                for d in o.deps:
                    if d.is_dma:
                        wait(dsems[d.eng][d.dslot], d.dval, ("d", d.eng, d.dslot))
                    else:
                        wait(sems[d.eng], d.count, ("c", d.eng))
                if o.is_dma and o.bg:
                    if o.dval > 16:
                        wait(bgsems[o.dslot], o.dval - 16, ("b", o.dslot))
                    ins = o.fn(eng)
                    ins.then_inc(bgsems[o.dslot], 16)
                elif o.is_dma:
                    if o.dval > 16:
                        wait(dsems[e][o.dslot], o.dval - 16, ("d", e, o.dslot))
                    ins = o.fn(eng)
                    ins.then_inc(dsems[e][o.dslot], 16)
                else:
                    ins = o.fn(eng)
                    if o.signal:
                        ins.then_inc(sems[e], 1)
            if drain_bg:
                for s_ in range(min(nbg_end, NBG)):
                    last = ((nbg_end - 1 - s_) // NBG) * NBG + s_
                    wait(bgsems[s_], 16 * (last // NBG + 1), ("b", s_))
            n = ndma_end[e]
            for s in range(min(n, NDMA)):
                last = ((n - 1 - s) // NDMA) * NDMA + s
                wait(dsems[e][s], 16 * (last // NDMA + 1), ("d", e, s))
            if e in last_c:
                wait(sems[e], last_c[e].count, ("c", e))
            eng.sem_inc(barsem, 1)

        block.tensor(lambda eng: run("pe", eng))
        block.scalar(lambda eng: run("act", eng))
        block.vector(lambda eng: run("dve", eng))
        block.gpsimd(lambda eng: run("pool", eng))
        block.sync(lambda eng: run("sp", eng))
        self.phase += 1
        self.begin()


class _Stop(Exception):
    pass


class _Phase(ExitStack):
    def __exit__(self, et, ev, tb):
        r = super().__exit__(et, ev, tb)
        return bool(r) or (et is not None and issubclass(et, _Stop))


class _Lenient(ExitStack):
    truncated = False

    def __exit__(self, et, ev, tb):
        try:
            return super().__exit__(et, ev, tb)
        except AssertionError:
            if not _Lenient.truncated:
                raise
            return False


class _LazyIn:
    def __init__(self, nc, name, shape, used):
        self._nc, self._name, self._shape, self._used, self._apv = nc, name, list(shape), used, None

    def _ap(self):
        if self._apv is None:
            self._apv = self._nc.dram_tensor(self._name, self._shape, F32, kind="ExternalInput").ap()
            self._used.append(self._name)
        return self._apv

    def __getitem__(self, k):
        return self._ap()[k]

    def __getattr__(self, a):
        return getattr(self._ap(), a)


def build_program(upto=99, dbg=None):
    nc = bass.Bass("TRN2", target_bir_lowering=False)
    used_inputs = []
    nc._used_inputs = used_inputs
    dbg_outs = {}
    nc._dbg_outs = dbg_outs

    def din(name, shape):
        return _LazyIn(nc, name, shape, used_inputs)

    def dbg_out(name, shape, dt=F32):
        t = nc.dram_tensor("dbg_" + name, list(shape), dt, kind="ExternalOutput").ap()
        dbg_outs[name] = t
        return t

    x_c = din("x_c", [T + 3, D])
    flags_d = din("flags", [128, 16])
    cst = din("cst", [128, C_W])
    c_in = din("c", [1, D])
    w_mod = din("w_mod", [1, D, 6 * D])
    b_mod = din("b_mod", [1, 6 * D])
    norm1_w = din("norm1_w", [1, D])
    w_in = din("w_in", [1, D, DIN])
    hgrn_lb = din("hgrn_lb", [2, 1024])
    hgrn_gnw = din("hgrn_gnorm_w", [1, 1024])
    ssd_conv_w = din("ssd_conv_w", [1, 4, 2048])
    ssd_conv_b = din("ssd_conv_b", [1, 2048])
    ssd_dt_bias = din("ssd_dt_bias", [1, 16])
    ssd_a_log = din("ssd_a_log", [1, 16])
    ssd_d = din("ssd_d", [1, 16])
    ssd_norm_w = din("ssd_norm_w", [1, 1024])
    w_out = din("w_out", [1, D, D])
    norm2_w = din("norm2_w", [1, D])
    ffn_w_up = din("ffn_w_up", [1, D, 2 * DFF])
    ffn_conv_w = din("ffn_conv_w", [1, 3, 2 * DFF])
    ffn_conv_b = din("ffn_conv_b", [1, 2 * DFF])
    ffn_w_down = din("ffn_w_down", [1, DFF, D])
    final_w = din("final_norm_w", [D])
    y_out = nc.dram_tensor("y_out", [T, D], F32, kind="ExternalOutput").ap() if upto >= 7 else None

    mod_d = nc.dram_tensor("mod_d", [1, 6 * D], F32).ap()
    hT_d = nc.dram_tensor("hT_d", [128, 16, 3 + T], BF16).ap()
    hq_d = nc.dram_tensor("hq_d", [128, 8, T], BF16).ap()
    ho_d = nc.dram_tensor("ho_d", [128, 8, T], BF16).ap()
    hg_d = nc.dram_tensor("hg_d", [128, 8, T], BF16).ap()
    sz_d = nc.dram_tensor("sz_d", [128, 8, T], BF16).ap()
    yl_d = nc.dram_tensor("yl_d", [128, 8, T], BF16).ap()
    x1_d = nc.dram_tensor("x1_d", [T, D], F32).ap()
    h2T_d = nc.dram_tensor("h2T_d", [128, 16, 2 + T], BF16).ap()
    EXW = 3072
    EX_W = [1024, 1024, 1024]
    ex_in = [nc.dram_tensor(f"ex_in{i}", [128, w], F32) for i, w in enumerate(EX_W)]
    ex_out = [nc.dram_tensor(f"ex_out{i}", [NCORES * 128, w], F32) for i, w in enumerate(EX_W)]
    fence_in = nc.dram_tensor("fence_in", [128, 64], F32)
    fence_out = [nc.dram_tensor(f"fence_out{i}", [128, 64], F32) for i in range(2)]
    ex2_in = nc.dram_tensor("ex2_in", [128, 1024], F32)
    ex2_out = nc.dram_tensor("ex2_out", [NCORES * 128, 1024], F32)

    wup_t = nc.dram_tensor("wup_t", [22, 128, 16 * 2 * 256], BF16).ap()
    wdn_t = nc.dram_tensor("wdn_t", [16, 128, 11 * 512], BF16).ap()
    S = Sched()
    cast_jobs = []

    uid = [0]

    def sb(stack, name, shape, dt):
        uid[0] += 1
        if stack is None:
            return nc.alloc_sbuf_tensor(f"{name}_u{uid[0]}", list(shape), dt, side="right")
        return stack.enter_context(nc.sbuf_tensor(f"{name}_u{uid[0]}", list(shape), dt))

    def ps(stack, name, shape, dt):
        uid[0] += 1
        return stack.enter_context(nc.psum_tensor(f"{name}_u{uid[0]}", list(shape), dt))

    def ACT(out, in_, func, r, w, bias=None, scale=None, accum_out=None):
        kw = {}
        if bias is not None:
            kw["bias"] = bias
        if scale is not None:
            kw["scale"] = scale
        if accum_out is not None:
            kw["accum_out"] = accum_out
        S.op("act", lambda e: e.activation(out=out, in_=in_, func=func, **kw), r, w)

    def TT(eng, out, in0, in1, op, r, w):
        S.op(eng, lambda e: e.tensor_tensor(out=out, in0=in0, in1=in1, op=op), r, w)

    def TS(eng, out, in0, s1, s2, op0, op1, r, w):
        if s2 is None:
            S.op(eng, lambda e: e.tensor_scalar(out=out, in0=in0, scalar1=s1, scalar2=None, op0=op0), r, w)
        else:
            S.op(eng, lambda e: e.tensor_scalar(out=out, in0=in0, scalar1=s1, scalar2=s2, op0=op0, op1=op1), r, w)

    def STT(out, in0, scalar, in1, op0, op1, r, w):
        S.op("dve", lambda e: e.scalar_tensor_tensor(out=out, in0=in0, scalar=scalar, in1=in1, op0=op0, op1=op1), r, w)

    def CP(eng, out, in_, r, w):
        if eng == "act":
            ACT(out, in_, AF.Copy, r, w)
        else:
            S.op(eng, lambda e: e.tensor_copy(out=out, in_=in_), r, w)

    def MM(out, lhsT, rhs, start, stop, r, w):
        S.op("pe", lambda e: e.matmul(out, lhsT=lhsT, rhs=rhs, start=start, stop=stop), r, w)

    def TR(out, in_, ident, r, w):
        S.op("pe", lambda e: e.transpose(out=out, in_=in_, identity=ident), r, w)

    def DMA(eng, out, in_, r, w):
        S.op(eng, lambda e: e.dma_start(out=out, in_=in_), r, w, dma=True)

    def SCAN(out, d0, d1, init, r, w):
        S.op("dve", lambda e: e.tensor_tensor_scan(out=out, data0=d0, data1=d1, initial=init,
                                                    op0=ALU.mult, op1=ALU.add), r, w)

    def MEMSET(eng, ap, val, w):
        S.op(eng, lambda e: e.memset(ap, val), (), w)

    def RECIP(out, in_, r, w):
        S.op("dve", lambda e: e.reciprocal(out=out, in_=in_), r, w)

    def bc_last(ap, n):
        sh = list(ap.shape)
        return ap.unsqueeze(len(sh)).to_broadcast(sh + [n])

    if True:
     def make_cast_jobs():
        w_up_v = ffn_w_up[0].rearrange("(k p) c -> p k c", p=128)
        w_dn_v = ffn_w_down[0].rearrange("(i p) d -> p i d", p=128)
        for g in range(22):
            f0 = g * 256
            dst = wup_t[g].rearrange("p (k gv c) -> p k gv c", gv=2, c=256)
            for gv in range(2):
                cast_jobs.append((dst[:, :, gv, :], w_up_v[:, :, gv * DFF + f0:gv * DFF + f0 + 256]))
        for hf in range(2):
            for dg in range(4):
                for j in range(2):
                    blk = (hf * 4 + dg) * 2 + j
                    i0 = hf * 22 + j * 11
                    dst = wdn_t[blk].rearrange("p (i d) -> p i d", d=512)
                    cast_jobs.append((dst, w_dn_v[:, i0:i0 + 11, dg * 512:(dg + 1) * 512]))

    def issue_cast(n=1):
        for _ in range(n):
            if cast_jobs:
                dst, src = cast_jobs.pop(0)
                S.op("pool", lambda e, dst=dst, src=src: e.dma_start(out=dst, in_=src), (), (), dma=True, bg=True)

    with ExitStack() as G:
        sems = {e: G.enter_context(nc.semaphore("s_" + e)) for e in ENGS}
        dsems = {e: [G.enter_context(nc.semaphore(f"d_{e}_{i}")) for i in range(NDMA)] for e in ENGS}
        barsem = G.enter_context(nc.semaphore("barsem"))
        bgsems = [G.enter_context(nc.semaphore(f"bg_{i}")) for i in range(NBG)]
        G.enter_context(nc.allow_non_contiguous_dma(reason="small strided parameter loads"))

        phase_no = [0]

        def emit():
            S.emit(nc, block, sems, dsems, barsem, bgsems)
            phase_no[0] += 1
            if phase_no[0] > upto:
                stopped[0] = True
                _Lenient.truncated = True
                raise _Stop()

        stopped = [False]

        def stop_check():
            if stopped[0]:
                raise _Stop()

        ident_f = sb(None, "ident_f", [128, 128], F32)
        ident_b = sb(None, "ident_b", [128, 128], BF16)
        ones_b = sb(None, "ones_b", [128, 128], BF16)
        flags = sb(None, "flags_sb", [128, 16], F32)
        one1 = sb(None, "one1", [128, 1], F32)
        epsb = sb(None, "epsb", [128, 1], F32)
        negh = sb(None, "negh", [128, 1], F32)
        modF = sb(None, "modF", [128, 96], F32)
        bmodF = sb(None, "bmodF", [128, 96], F32)
        nw1F = sb(None, "nw1F", [128, 16], F32)
        nw2F = sb(None, "nw2F", [128, 16], F32)
        a1 = sb(None, "a1", [128, 16], F32)
        a2 = sb(None, "a2", [128, 16], F32)
        block = G.enter_context(nc.Block())

        with _Phase() as P:
            stop_check()
            wp = [sb(P, f"wp{i}", [128, 2048], F32) for i in range(4)]
            modrow = [sb(P, f"modrow{i}", [1, 2048], F32) for i in range(2)]
            c_sb = sb(P, "c_sb", [128, 16], F32)
            cE = sb(P, "cE", [128, 16], F32)
            scv = sb(P, "scv", [128, 16], F32)
            psm = [ps(P, f"psm{n}", [128, 512], F32) for n in range(4)]

            DMA("sp", ident_f[:], cst[:, C_ID:C_ID + 128], (), ["ident_f"])
            DMA("pool", ident_b[:], cst[:, C_ID:C_ID + 128], (), ["ident_b"])
            DMA("sp", flags[:], flags_d[:, :], (), ["flags"])
            MEMSET("pool", ones_b[:], 1.0, ["ones_b"])
            MEMSET("pool", one1[:], 1.0, ["one1"])
            MEMSET("pool", epsb[:], EPS, ["epsb"])
            MEMSET("pool", negh[:], -0.5, ["negh"])
            DMA("sp", c_sb[:], c_in.rearrange("o (k p) -> p (o k)", p=128), (), ["c_sb"])
            ACT(cE[:], c_sb[:], AF.Exp, ["c_sb"], ["cE"], scale=-1.0)
            TS("dve", cE[:], cE[:], 1.0, None, ALU.add, None, ["cE"], ["cE"])
            RECIP(cE[:], cE[:], ["cE"], ["cE"])
            TT("dve", scv[:], c_sb[:], cE[:], ALU.mult, ["c_sb", "cE"], ["scv"])
            cnt = 0
            for cg in range(6):
                for k in range(16):
                    b = cnt % 4
                    cnt += 1
                    DMA("sp" if cnt % 2 else "act", wp[b][:], w_mod[0, k * 128:(k + 1) * 128, cg * 2048:(cg + 1) * 2048],
                        (), [("wp", b)])
                    for n in range(4):
                        MM(psm[n][0:1, :], scv[:, k:k + 1], wp[b][:, n * 512:(n + 1) * 512], k == 0, k == 15,
                           ["scv", ("wp", b)], [("psm", n)])
                mr = modrow[cg % 2]
                for n in range(4):
                    CP("act" if n % 2 else "dve", mr[0:1, n * 512:(n + 1) * 512], psm[n][0:1, :], [("psm", n)], [("mr", cg % 2, n)])
                DMA("sp", mod_d[0:1, cg * 2048:(cg + 1) * 2048], mr[0:1, :], [("mr", cg % 2, n) for n in range(4)], ["mod_d"])
            for s in range(6):
                DMA("sp", modF[:, s * 16:(s + 1) * 16],
                    mod_d[0:1, s * 2048:(s + 1) * 2048].rearrange("o (k p) -> p (o k)", p=128), ["mod_d"], ["modF"])
                DMA("act", bmodF[:, s * 16:(s + 1) * 16],
                    b_mod[0:1, s * 2048:(s + 1) * 2048].rearrange("o (k p) -> p (o k)", p=128), (), ["bmodF"])
            DMA("sp", nw1F[:], norm1_w.rearrange("o (k p) -> p (o k)", p=128), (), ["nw1F"])
            DMA("sp", nw2F[:], norm2_w.rearrange("o (k p) -> p (o k)", p=128), (), ["nw2F"])
            TT("dve", modF[:], modF[:], bmodF[:], ALU.add, ["modF", "bmodF"], ["modF"])
            STT(a1[:], modF[:, 16:32], 1.0, nw1F[:], ALU.add, ALU.mult, ["modF", "nw1F"], ["a1"])
            STT(a2[:], modF[:, 64:80], 1.0, nw2F[:], ALU.add, ALU.mult, ["modF", "nw2F"], ["a2"])
            if dbg:
                DMA("sp", dbg_out("modF", [128, 96])[:, :], modF[:], ["modF"], ["dbg0"])
                DMA("sp", dbg_out("a1", [128, 16])[:, :], a1[:], ["a1"], ["dbg1"])
                DMA("sp", dbg_out("a2", [128, 16])[:, :], a2[:], ["a2"], ["dbg2"])
            emit()
        shift1 = modF[:, 0:16]
        shift2 = modF[:, 48:64]

        def norm_to_T(P, tag, nsub, src_loader, avec, shvec, dstT, halo=False, flagmul=False):
            np_ = 3 if halo else 128
            junk = P["junk"]
            ss = P["ss"]
            xn = P["xn"]
            tp = P["tp"]
            for j in range(nsub):
                src, rk = src_loader(j)
                ACT(junk[0:np_, :], src, AF.Square, rk, [("ss", j)], accum_out=ss[0:np_, j:j + 1])
                TS("pool", ss[0:np_, j:j + 1], ss[0:np_, j:j + 1], 1.0 / D, EPS, ALU.mult, ALU.add, [("ss", j)], [("ss", j)])
                TT("pool", ss[0:np_, j:j + 1], ss[0:np_, j:j + 1], negh[0:np_, :], ALU.pow, [("ss", j), "negh"], [("ss", j)])
                ACT(xn[0:np_, j, :], src, AF.Copy, rk + [("ss", j)], [("xn", j)], scale=ss[0:np_, j:j + 1])
            ncol = nsub * np_ if not halo else 3
            for dk in range(16):
                tpb = tp[dk % 2]
                for j in range(nsub):
                    TR(tpb[:, j * np_:(j + 1) * np_] if not halo else tpb[:, 0:3],
                       xn[0:np_, j, dk * 128:(dk + 1) * 128],
                       ident_b[0:np_, 0:np_], [("xn", j), "ident_b"], [("tp", dk % 2)])
                if dk % 2 == 0:
                    ACT(dstT[:, dk, 0:ncol], tpb[:, 0:ncol], AF.Identity, [("tp", dk % 2), "a1", "a2", "modF"], [(tag, dk)],
                        bias=shvec[:, dk:dk + 1], scale=avec[:, dk:dk + 1])
                else:
                    TS("dve", dstT[:, dk, 0:ncol], tpb[:, 0:ncol], avec[:, dk:dk + 1], shvec[:, dk:dk + 1], ALU.mult, ALU.add,
                       [("tp", dk % 2), "a1", "a2", "modF"], [(tag, dk)])
                if flagmul:
                    TS("dve", dstT[:, dk, 0:ncol], dstT[:, dk, 0:ncol], flags[:, 0:1], None, ALU.mult, None,
                       [(tag, dk), "flags"], [(tag, dk)])

        with _Phase() as P:
            stop_check()
            bufs = {
                "junk": sb(P, "junk", [128, 2048], BF16),
                "ss": sb(P, "ss", [128, 4], F32),
                "xn": sb(P, "xn", [128, 4, 2048], BF16),
                "tp": [ps(P, f"tp{i}", [128, 512], BF16) for i in range(2)],
            }
            xt = [sb(P, f"xt{i}", [128, 2048], F32) for i in range(4)]
            hTt = [sb(P, f"hTt{i}", [128, 16, 512], BF16) for i in range(2)]
            hTh = sb(P, "hTh", [128, 16, 4], BF16)
            DMA("sp", xt[0][0:3, :], x_c[0:3, :], (), [("xt", 0)])
            norm_to_T(bufs, "hTh", 1, lambda j: (xt[0][0:3, :], [("xt", 0)]), a1, shift1, hTh, halo=True, flagmul=True)
            DMA("act", hT_d[:, :, 0:3], hTh[:, :, 0:3], [("hTh", dk) for dk in range(16)], ["hT_d"])
            for ti in range(NT):
                def loader(j, ti=ti):
                    return xt[j][:], [("xt", j)]
                for j in range(4):
                    r0 = 3 + ti * TA + j * 128
                    DMA("sp", xt[j][:], x_c[r0:r0 + 128, :], (), [("xt", j)])
                hb = hTt[ti % 2]
                norm_to_T(bufs, ("hTt", ti % 2), 4, loader, a1, shift1, hb)
                DMA("act", hT_d[:, :, 3 + ti * TA:3 + (ti + 1) * TA], hb[:], [(("hTt", ti % 2), dk) for dk in range(16)], ["hT_d"])
            if dbg:
                DMA("sp", dbg_out("hT", [128, 16, 3 + T], BF16)[:, :, :], hT_d, ["hT_d"], ["dbg0"])
            emit()

        w_in_v = w_in[0].rearrange("(k p) c -> p k c", p=128)

        with _Lenient() as M:
            lbv = sb(M, "lbv", [128, 8], F32)
            oml = sb(M, "oml", [128, 8], F32)
            gnw = sb(M, "gnw", [128, 8], F32)
            S_hg = sb(M, "S_hg", [128, 8, 128], F32)
            Bprev = sb(M, "Bprev", [128, 8], F32)
            Gall = sb(M, "Gall", [128, 8, 32], F32)
            Hs = sb(M, "Hs", [128, 1024], F32)
            acumG = sb(M, "acumG", [16, T], F32)
            CFall = sb(M, "CFall", [128, 4, T], BF16)
            Sin_hb = sb(M, "Sin_hb", [128, 8, 128], BF16)
            Sin_sb = sb(M, "Sin_sb", [128, 1024], BF16)
            snw = sb(M, "snw", [128, 8], F32)
            scanmask = sb(M, "scanmask", [128, 512], F32)

            with _Phase() as P:
                stop_check()
                lbr = sb(P, "lbr", [128, 2, 8], F32)
                hTt = [sb(P, f"hTt{i}", [128, 16, 512], BF16) for i in range(2)]
                wb = [sb(P, f"wb{i}", [128, 16, 256], BF16) for i in range(3)]
                qs_all = sb(P, "qs_all", [128, 8, 512], BF16)
                qt_all = sb(P, "qt_all", [128, 8, 512], BF16)
                kt_all = sb(P, "kt_all", [128, 8, 512], BF16)
                t0 = [sb(P, f"t0_{i}", [128, 512], F32) for i in range(2)]
                tE = sb(P, "tE", [128, 512], F32)
                tL1 = sb(P, "tL1", [128, 512], F32)
                tL2 = sb(P, "tL2", [128, 512], F32)
                tb = sb(P, "tb", [128, 512], F32)
                tq = sb(P, "tq", [128, 512], F32)
                tk = sb(P, "tk", [128, 512], F32)
                vF = [sb(P, f"vF{i}", [128, 512], BF16) for i in range(2)]
                kT_sb = sb(P, "kT_sb", [64, 1024], BF16)
                vT_sb = sb(P, "vT_sb", [64, 1024], BF16)
                attm = sb(P, "attm", [64, 512], F32)
                attT_sb = sb(P, "attT_sb", [64, 512], BF16)
                tmpU = sb(P, "tmpU", [128, 128], F32)
                Sp = sb(P, "Sp", [128, 128], BF16)
                o_sb = [sb(P, f"o_sb{i}", [128, 512], BF16) for i in range(2)]
                sgt = [sb(P, f"sgt{i}", [128, 512], BF16) for i in range(2)]
                sc_em = sb(P, "sc_em", [128, 8, 8], F32)
                sc_c1 = sb(P, "sc_c1", [128, 8, 8], F32)
                sc_c2 = sb(P, "sc_c2", [128, 8, 8], F32)
                bcum = sb(P, "bcum", [128, 8], F32)
                gex = sb(P, "gex", [128, 8], F32)
                ones8 = sb(P, "ones8", [128, 8], F32)
                pin = [ps(P, f"pin{i}", [128, 512], F32) for i in range(2)]
                kTp = ps(P, "kTp", [128, 1024], BF16)
                vTp = ps(P, "vTp", [128, 1024], BF16)
                attp = ps(P, "attp", [128, 512], F32)
                Up = ps(P, "Up", [128, 512], F32)
                op_ = ps(P, "op_", [128, 512], F32)

                DMA("sp", lbr[:], hgrn_lb.rearrange("r (h p) -> p r h", p=128), (), ["lbr"])
                DMA("sp", gnw[:], hgrn_gnw.rearrange("o (h p) -> p (o h)", p=128), (), ["gnw"])
                DMA("sp", scanmask[:], cst[:, C_SCAN:C_SCAN + 512], (), ["scanmask"])
                DMA("sp", attm[:], cst[0:64, C_ATT:C_ATT + 512], (), ["attm"])
                TT("dve", lbv[:], lbr[:, 1, :], lbr[:, 0, :], ALU.subtract, ["lbr"], ["lbv"])
                ACT(oml[:], lbv[:], AF.Exp, ["lbv"], ["oml"])
                TS("dve", lbv[:], oml[:], 1.0, None, ALU.add, None, ["oml"], ["lbv"])
                RECIP(lbv[:], lbv[:], ["lbv"], ["lbv"])
                TT("dve", oml[:], oml[:], lbv[:], ALU.mult, ["oml", "lbv"], ["oml"])
                MEMSET("pool", S_hg[:], 0.0, [("S_hg", h) for h in range(8)])
                MEMSET("pool", Bprev[:], 0.0, ["Bprev"])
                MEMSET("pool", ones8[:], 1.0, ["ones8"])

                wcnt = [0]
                make_cast_jobs()

                def head_pipeline(ti, h):
                    tcol = ti * TA
                    qt = qt_all[:, h, :]
                    kt = kt_all[:, h, :]
                    vf = vF[h % 2]
                    for c in range(8):
                        TR(kTp[0:64, c * 128:(c + 1) * 128], kt[:, c * 64:(c + 1) * 64], ident_b[:], [("kt", h), "ident_b"], ["kTp"])
                    for c in range(8):
                        TR(vTp[0:64, c * 128:(c + 1) * 128], vf[:, c * 64:(c + 1) * 64], ident_b[:], [("vF", h % 2), "ident_b"], ["vTp"])
                    CP("act", kT_sb[:], kTp[0:64, :], ["kTp"], ["kT_sb"])
                    CP("dve", vT_sb[:], vTp[0:64, :], ["vTp"], ["vT_sb"])
                    for c in range(8):
                        MM(attp[0:64, c * 64:(c + 1) * 64], kt[:, c * 64:(c + 1) * 64], qt[:, c * 64:(c + 1) * 64], True, True,
                           [("kt", h), ("qt", h)], ["attp"])
                    TT("dve", attT_sb[:], attp[0:64, :], attm[:], ALU.mult, ["attp", "attm"], ["attT_sb"])
                    ob = o_sb[h % 2]
                    for c in range(8):
                        ACT(Sp[:], S_hg[:, h, :], AF.Copy, [("S_hg", h), ("sc", h)], ["Sp"], scale=sc_em[:, h, c:c + 1])
                        MM(op_[:, c * 64:(c + 1) * 64], vT_sb[:, c * 128:(c + 1) * 128], attT_sb[:, c * 64:(c + 1) * 64], True, False,
                           ["vT_sb", "attT_sb"], [("op", c)])
                        MM(op_[:, c * 64:(c + 1) * 64], Sp[:], qt[:, c * 64:(c + 1) * 64], False, True,
                           ["Sp", ("qt", h)], [("op", c)])
                        us = Up[:, (c % 4) * 128:(c % 4 + 1) * 128]
                        MM(us, kT_sb[:, c * 128:(c + 1) * 128], vT_sb[:, c * 128:(c + 1) * 128], True, True,
                           ["kT_sb", "vT_sb"], [("Up", c % 4)])
                        TS("dve", tmpU[:], us, sc_c2[:, h, c:c + 1], None, ALU.mult, None, [("Up", c % 4), ("sc", h)], ["tmpU"])
                        STT(S_hg[:, h, :], S_hg[:, h, :], sc_c1[:, h, c:c + 1], tmpU[:], ALU.mult, ALU.add,
                            [("S_hg", h), "tmpU", ("sc", h)], [("S_hg", h)])
                    CP("act", ob[:], op_[:], [("op", c) for c in range(8)], [("o_sb", h % 2)])
                    DMA("sp", ho_d[:, h, tcol:tcol + TA], ob[:], [("o_sb", h % 2)], ["ho_d"])
                    DMA("sp", hq_d[:, h, tcol:tcol + TA], qt, [("qt", h)], ["hq_d"])

                for ti in range(NT):
                    hb = hTt[ti % 2]
                    DMA("sp", hb[:], hT_d[:, :, 3 + ti * TA:3 + (ti + 1) * TA], (), [("hTt", ti % 2)])
                    for gi in range(16):
                        wi = wcnt[0] % 3
                        wcnt[0] += 1
                        DMA("pool", wb[wi][:], w_in_v[:, :, gi * 256:(gi + 1) * 256], (), [("wb", wi)])
                        issue_cast(1)
                        for half in range(2):
                            cc = gi * 2 + half
                            kind, h = cc // 8, cc % 8
                            pp = pin[cc % 2]
                            pk = ("pin", cc % 2)
                            for k in range(16):
                                MM(pp[:], wb[wi][:, k, half * 128:(half + 1) * 128], hb[:, k, :], k == 0, k == 15,
                                   [("wb", wi), ("hTt", ti % 2)], [pk])
                            if kind == 0 or kind == 3:
                                tt = t0[cc % 2]
                                tk0 = ("t0", cc % 2)
                                ACT(tt[:], pp[:], AF.Exp, [pk], [tk0], scale=-1.0)
                                ACT(tt[:], tt[:], AF.Ln, [tk0], [tk0], bias=one1[:])
                                ACT(tt[:], tt[:], AF.Exp, [tk0], [tk0], scale=-1.0)
                                if kind == 0:
                                    TT("dve", qs_all[:, h, :], pp[:], tt[:], ALU.mult, [pk, tk0], [("qs", h)])
                                else:
                                    sg = sgt[h % 2]
                                    STT(sg[:], pp[:], gnw[:, h:h + 1], tt[:], ALU.mult, ALU.mult, [pk, tk0, "gnw"], [("sgt", h % 2)])
                                    DMA("act", hg_d[:, h, ti * TA:(ti + 1) * TA], sg[:], [("sgt", h % 2)], ["hg_d"])
                            elif kind == 1:
                                ACT(tE[:], pp[:], AF.Exp, [pk], ["tE"], scale=-1.0)
                                ACT(tL2[:], tE[:], AF.Ln, ["tE"], ["tL2"], bias=one1[:])
                                ACT(tL1[:], tE[:], AF.Ln, ["tE", "lbv"], ["tL1"], bias=one1[:], scale=lbv[:, h:h + 1])
                                TT("pool", tL1[:], tL1[:], tL2[:], ALU.subtract, ["tL1", "tL2"], ["tL1"])
                                SCAN(tb[:], scanmask[:], tL1[:], 0.0, ["scanmask", "tL1"], ["tb"])
                                tb3 = tb[:].rearrange("p (c t) -> p c t", t=64)
                                TT("dve", tL1[:].rearrange("p (c t) -> p c t", t=64), tb3,
                                   tb3[:, :, 31:32].to_broadcast([128, 8, 64]), ALU.subtract, ["tb"], ["tL1"])
                                ACT(tq[:], tL1[:], AF.Exp, ["tL1"], ["tq"])
                                ACT(tk[:], tL1[:], AF.Exp, ["tL1"], ["tk"], scale=-1.0)
                                ACT(tL2[:], tL2[:], AF.Exp, ["tL2"], ["tL2"], scale=-1.0)
                                STT(tE[:], tE[:], oml[:, h:h + 1], tL2[:], ALU.mult, ALU.mult, ["tE", "tL2", "oml"], ["tE"])
                                TT("dve", qt_all[:, h, :], qs_all[:, h, :], tq[:], ALU.mult, [("qs", h), "tq"], [("qt", h)])
                                TT("pool", kt_all[:, h, :], tE[:], tk[:], ALU.mult, ["tE", "tk"], [("kt", h)])
                                m_v = tb3[:, :, 31]
                                bl_v = tb3[:, :, 63]
                                ACT(sc_em[:, h, :], m_v, AF.Exp, ["tb"], [("sc", h)])
                                ACT(sc_c1[:, h, :], bl_v, AF.Exp, ["tb"], [("sc", h)])
                                TT("pool", gex[:], bl_v, m_v, ALU.subtract, ["tb"], ["gex"])
                                ACT(sc_c2[:, h, :], gex[:], AF.Exp, ["gex"], [("sc", h)])
                                SCAN(bcum[:], ones8[:], bl_v, Bprev[:, h:h + 1], ["ones8", "tb", ("Bprev", h), "Bprev"], ["bcum"])
                                TT("pool", gex[:], bcum[:], bl_v, ALU.subtract, ["bcum", "tb"], ["gex"])
                                TT("pool", gex[:], gex[:], m_v, ALU.add, ["gex", "tb"], ["gex"])
                                ACT(Gall[:, h, ti * 8:(ti + 1) * 8], gex[:], AF.Exp, ["gex"], ["Gall"])
                                CP("pool", Bprev[:, h:h + 1], bcum[:, 7:8], ["bcum"], [("Bprev", h)])
                            else:
                                CP("act", vF[h % 2][:], pp[:], [pk], [("vF", h % 2)])
                                head_pipeline(ti, h)
                if dbg:
                    DMA("sp", dbg_out("ho", [128, 8, T], BF16)[:, :, :], ho_d, ["ho_d"], ["dbg0"])
                    DMA("sp", dbg_out("hq", [128, 8, T], BF16)[:, :, :], hq_d, ["hq_d"], ["dbg1"])
                    DMA("sp", dbg_out("hg", [128, 8, T], BF16)[:, :, :], hg_d, ["hg_d"], ["dbg2"])
                    DMA("sp", dbg_out("S_hg", [128, 1024])[:, :], S_hg[:].rearrange("p h v -> p (h v)"), [("S_hg", h) for h in range(8)], ["dbg3"])
                    DMA("sp", dbg_out("Gall", [128, 256])[:, :], Gall[:].rearrange("p h c -> p (h c)"), ["Gall"], ["dbg4"])
                    DMA("sp", dbg_out("Bprev", [128, 8])[:, :], Bprev[:], [("Bprev", h) for h in range(8)], ["dbg5"])
                emit()

            with _Phase() as P:
                stop_check()
                hTt = [sb(P, f"hTt{i}", [128, 16, 512], BF16) for i in range(2)]
                hTh = sb(P, "hTh", [128, 16, 4], BF16)
                wb = [sb(P, f"wb{i}", [128, 16, 256], BF16) for i in range(3)]
                wdt = sb(P, "wdt", [128, 16, 16], BF16)
                convw = sb(P, "convw", [128, 16, 4], F32)
                convb = sb(P, "convb", [128, 16], F32)
                dtb = sb(P, "dtb", [16, 1], F32)
                a_h = sb(P, "a_h", [16, 1], F32)
                dB = sb(P, "dB", [128, 16], F32)
                dgm = sb(P, "dgm", [128, 1024], F32)
                blockm = sb(P, "blockm", [16, 1024], F32)
                ones16 = sb(P, "ones16", [16, 128], F32)
                rhs_m = sb(P, "rhs_m", [96, 1024], F32)
                lhsT_m = sb(P, "lhsT_m", [96, 64], F32)
                xhalo = sb(P, "xhalo", [128, 16, 3], F32)
                xr = [sb(P, f"xr{i}", [128, 515], F32) for i in range(2)]
                acc = [sb(P, f"acc{i}", [128, 512], F32) for i in range(2)]
                t0 = [sb(P, f"t0_{i}", [128, 512], F32) for i in range(2)]
                szt = [sb(P, f"szt{i}", [128, 512], BF16) for i in range(2)]
                XF = sb(P, "XF", [128, 8, 512], BF16)
                BFm = sb(P, "BFm", [128, 4, 512], BF16)
                dtE = sb(P, "dtE", [16, 512], F32)
                dtv = sb(P, "dtv", [16, 512], F32)
                dAv = sb(P, "dAv", [16, 512], F32)
                acum = sb(P, "acum", [16, 512], F32)
                ones512 = sb(P, "ones512", [16, 512], F32)
                acT = sb(P, "acT", [64, 16], F32)
                dtx = sb(P, "dtx", [128, 16], F32)
                edt = sb(P, "edt", [64, 16], F32)
                eatot = sb(P, "eatot", [128, 16], F32)
                diff = sb(P, "diff", [64, 1024], F32)
                Maug = sb(P, "Maug", [128, 1024], BF16)
                xaug = sb(P, "xaug", [128, 1024], BF16)
                xdec = sb(P, "xdec", [64, 1024], BF16)
                btok = sb(P, "btok", [64, 512], BF16)
                eR = sb(P, "eR", [128, 1024], BF16)
                Ct = sb(P, "Ct", [128, 1024], BF16)
                Hb = sb(P, "Hb", [128, 1024], BF16)
                y_sb = [sb(P, f"y_sb{i}", [128, 8, 512], BF16) for i in range(2)]
                pinU = ps(P, "pinU", [128, 1024], F32)
                repm = ps(P, "repm", [128, 512], F32)
                rep1 = ps(P, "rep1", [128, 512], F32)
                xTp = ps(P, "xTp", [128, 1024], BF16)
                yp = ps(P, "yp", [128, 512], F32)
                psml = ps(P, "psml", [128, 512], F32)
                bTp = ps(P, "bTp", [128, 1024], BF16)
                pin = [pinU[:, 0:512], pinU[:, 512:1024]]

                SETUPN = int(os.environ.get("K_A3_SETUP", 999))
                if SETUPN > 0:
                    for j in range(4):
                        DMA("sp", convw[:, :, j], ssd_conv_w[0, j:j + 1, :].rearrange("o (c p) -> p (o c)", p=128), (), ["convw"])
                if SETUPN > 1:
                    DMA("sp", convb[:], ssd_conv_b.rearrange("o (c p) -> p (o c)", p=128), (), ["convb"])
                if SETUPN > 2:
                    DMA("sp", dtb[:], ssd_dt_bias.rearrange("o h -> h o"), (), ["dtb"])
                if SETUPN > 3:
                    DMA("sp", a_h[:], ssd_a_log.rearrange("o h -> h o"), (), ["a_h"])
                if SETUPN > 4:
                    ACT(a_h[:], a_h[:], AF.Exp, ["a_h"], ["a_h"])
                if SETUPN > 5:
                    TS("dve", a_h[:], a_h[:], -1.0, None, ALU.mult, None, ["a_h"], ["a_h"])
                if SETUPN > 6:
                    DMA("sp", dB[:], ssd_d[0:1, :].partition_broadcast(128), (), ["dB"])
                if SETUPN > 7:
                    DMA("sp", dgm[64:128, :], cst[0:64, C_DIAG:C_DIAG + 1024], (), ["dgm"])
                if SETUPN > 8:
                    DMA("sp", blockm[:], cst[0:16, C_BLK:C_BLK + 1024], (), ["blockm"])
                if SETUPN > 9:
                    MEMSET("pool", ones16[:], 1.0, ["ones16"])
                if SETUPN > 10:
                    MEMSET("pool", ones512[:], 1.0, ["ones512"])
                if SETUPN > 11:
                    MEMSET("pool", rhs_m[:], 0.0, ["rhs_m"])
                if SETUPN > 12:
                    DMA("sp", rhs_m[32:96, :], cst[0:64, C_NEG:C_NEG + 1024], ["rhs_m"], ["rhs_m"])
                if SETUPN > 13:
                    DMA("sp", lhsT_m[:], cst[0:96, C_LM:C_LM + 64], (), ["lhsT_m"])
                if SETUPN > 14:
                    DMA("pool", wdt[:], w_in_v[:, :, 7168:7184], (), ["wdt"])
                if SETUPN > 15:
                    DMA("sp", hTh[:, :, 0:3], hT_d[:, :, 0:3], (), ["hTh"])
                if SETUPN > 16:
                    DMA("sp", snw[:], ssd_norm_w.rearrange("o (h p) -> p (o h)", p=128), (), ["snw"])
                if SETUPN > 17:
                    TT("dve", Maug[64:128, :].rearrange("p (h t) -> p h t", t=64), dgm[64:128, :].rearrange("p (h t) -> p h t", t=64),
                       bc_last(dB[64:128, :], 64), ALU.mult, ["dgm", "dB"], ["MaugC"])
                if SETUPN > 18:
                    MEMSET("pool", Hs[:], 0.0, ["Hs"])
                if SETUPN > 19:
                    MEMSET("pool", Hb[:], 0.0, ["Hb"])
                if SETUPN > 20:
                    MEMSET("pool", dtx[64:128, :], 1.0, ["dtxC"])


                wcnt = [0]
                A3_TILES = int(os.environ.get("K_A3_TILES", NT))
                A3_CHUNKS = int(os.environ.get("K_A3_CHUNKS", 8))
                A3_STAGE = int(os.environ.get("K_A3_STAGE", 99))

                def silu_evac(src, srck, tt, ttk, dst, dstk, eng="dve"):
                    ACT(tt, src, AF.Exp, [srck], [ttk], scale=-1.0)
                    ACT(tt, tt, AF.Ln, [ttk], [ttk], bias=one1[:])
                    ACT(tt, tt, AF.Exp, [ttk], [ttk], scale=-1.0)
                    TT(eng, dst, src, tt, ALU.mult, [srck, ttk], [dstk])

                for ti in range(A3_TILES):
                    tcol = ti * TA
                    hb = hTt[ti % 2]
                    DMA("sp", hb[:], hT_d[:, :, 3 + tcol:3 + tcol + TA], (), [("hTt", ti % 2)])
                    ys = y_sb[ti % 2]
                    for gi in range(16, 28):
                        wi = wcnt[0] % 3
                        wcnt[0] += 1
                        DMA("pool", wb[wi][:], w_in_v[:, :, gi * 256:(gi + 1) * 256], (), [("wb", wi)])
                        issue_cast(1)
                        for half in range(2):
                            cc = gi * 2 + half
                            pp = pin[cc % 2]
                            pk = ("pin", cc % 2)
                            for k in range(16):
                                MM(pp, wb[wi][:, k, half * 128:(half + 1) * 128], hb[:, k, :], k == 0, k == 15,
                                   [("wb", wi), ("hTt", ti % 2)], [pk])
                            if cc < 40:
                                j = cc - 32
                                silu_evac(pp, pk, t0[cc % 2][:], ("t0", cc % 2), szt[cc % 2][:], ("szt", cc % 2))
                                DMA("act", sz_d[:, j, tcol:tcol + TA], szt[cc % 2][:], [("szt", cc % 2)], ["sz_d"])
                            else:
                                j = cc - 40
                                xx = xr[j % 2]
                                xk = ("xr", j % 2)
                                if ti == 0:
                                    for k in range(16):
                                        MM(psml[:, 400:403], wb[wi][:, k, half * 128:(half + 1) * 128], hTh[:, k, 0:3], k == 0, k == 15,
                                           [("wb", wi), "hTh"], ["psml_h"])
                                    CP("act", xx[:, 0:3], psml[:, 400:403], ["psml_h"], [xk])
                                else:
                                    CP("pool", xx[:, 0:3], xhalo[:, j, :], [("xhalo", j)], [xk])
                                ACT(xx[:, 3:515], pp, AF.Copy, [pk, xk], [xk])
                                CP("pool", xhalo[:, j, :], xx[:, 512:515], [xk], [("xhalo", j)])
                                ac = acc[j % 2]
                                ak = ("acc", j % 2)
                                TS("dve", ac[:], xx[:, 3:515], convw[:, j, 3:4], convb[:, j:j + 1], ALU.mult, ALU.add, [xk, "convw", "convb"], [ak])
                                for tap in (2, 1, 0):
                                    STT(ac[:], xx[:, tap:tap + 512], convw[:, j, tap:tap + 1], ac[:], ALU.mult, ALU.add, [xk, ak, "convw"], [ak])
                                if j < 8:
                                    dst, dk_ = XF[:, j, :], ("XF", j)
                                elif j < 12:
                                    dst, dk_ = BFm[:, j - 8, :], ("BF", j - 8)
                                else:
                                    dst, dk_ = CFall[:, j - 12, tcol:tcol + TA], ("CF", j - 12)
                                silu_evac(ac[:], ak, t0[cc % 2][:], ("t0", cc % 2), dst, dk_, eng="pool")
                    for k in range(16):
                        MM(repm[0:16, :], wdt[:, k, :], hb[:, k, :], k == 0, k == 15, ["wdt", ("hTt", ti % 2)], ["repm"])
                    ACT(dtE[:], repm[0:16, :], AF.Exp, ["repm", "dtb"], ["dtE"], bias=dtb[:])
                    ACT(dtv[:], dtE[:], AF.Ln, ["dtE"], ["dtv"], bias=one1[0:16, :])
                    TS("dve", dAv[:], dtv[:], a_h[:, 0:1], None, ALU.mult, None, ["dtv", "a_h"], ["dAv"])
                    SCAN(acum[:], scanmask[0:16, :], dAv[:], 0.0, ["scanmask", "dAv"], ["acum"])
                    SCAN(acumG[:, tcol:tcol + TA], ones512[:], dAv[:], 0.0 if ti == 0 else acumG[:, tcol - 1:tcol],
                         ["ones512", "dAv", "acumG"], ["acumG"])
                    for c in range(A3_CHUNKS):
                        cs = c * 64
                        TT("pool", rhs_m[0:16, :].rearrange("p (h t) -> p h t", t=64),
                           blockm[:].rearrange("p (h t) -> p h t", t=64),
                           acum[:, cs:cs + 64].unsqueeze(1).to_broadcast([16, 16, 64]), ALU.mult,
                           ["blockm", "acum"], ["Dg"])
                        MM(yp[0:64, 0:16], acum[:, cs:cs + 64], ident_f[0:16, 0:16], True, True, ["acum", "ident_f"], ["yp"])
                        MM(rep1[0:64, 0:16], dtv[:, cs:cs + 64], ident_f[0:16, 0:16], True, True, ["dtv", "ident_f"], ["rep1"])
                        CP("act", acT[:], yp[0:64, 0:16], ["yp"], ["acT"])
                        CP("act", dtx[0:64, :], rep1[0:64, 0:16], ["rep1"], ["dtx"])
                        for g in range(4):
                            MM(psml[0:64, 64 + g * 64:64 + (g + 1) * 64], BFm[:, g, cs:cs + 64], CFall[:, g, tcol + cs:tcol + cs + 64], True, True,
                               [("BF", g), ("CF", g)], ["psml_cb"])
                        for j in range(8):
                            TR(xTp[0:64, j * 128:(j + 1) * 128], XF[:, j, cs:cs + 64], ident_b[:], [("XF", j), "ident_b"], ["xTp"])
                            TR(xTp[64:128, j * 128:(j + 1) * 128], XF[:, j, cs:cs + 64], ident_b[:], [("XF", j), "ident_b"], ["xTp"])
                        TT("dve", xaug[:].rearrange("p (h q) -> p h q", q=64), xTp[:].rearrange("p (h q) -> p h q", q=64),
                           bc_last(dtx[:], 64), ALU.mult, ["xTp", "dtx", "dtxC"], ["xaug"])
                        for g in range(4):
                            TR(bTp[0:64, g * 128:(g + 1) * 128], BFm[:, g, cs:cs + 64], ident_b[:], [("BF", g), "ident_b"], ["bTp"])
                        CP("act", btok[:], bTp[0:64, 0:512], ["bTp"], ["btok"])
                        for hf in range(2):
                            hs = slice(hf * 512, (hf + 1) * 512)
                            MM(repm[0:64, :], lhsT_m[:], rhs_m[:, hs], True, True, ["lhsT_m", "rhs_m", "Dg"], ["repm"])
                            MM(rep1[:], ones16[:], rhs_m[0:16, hs], True, True, ["ones16", "Dg"], ["rep1"])
                            r3 = repm[0:64, :].rearrange("p (h t) -> p h t", t=64)
                            TT("dve", diff[:, hs].rearrange("p (h t) -> p h t", t=64), r3,
                               bc_last(acT[:, hf * 8:(hf + 1) * 8], 64), ALU.subtract, ["repm", "acT"], [("diff", hf)])
                            TT("dve", edt[:, hf * 8:(hf + 1) * 8], r3[:, :, 63], acT[:, hf * 8:(hf + 1) * 8], ALU.subtract,
                               ["repm", "acT"], ["edt"])
                            ACT(diff[:, hs], diff[:, hs], AF.Exp, [("diff", hf)], [("diff", hf)])
                            ACT(eR[:, hs], rep1[:], AF.Exp, ["rep1"], [("eR", hf)])
                            ACT(eatot[:, hf * 8:(hf + 1) * 8], rep1[:].rearrange("p (h t) -> p h t", t=64)[:, :, 63], AF.Exp,
                                ["rep1"], [("eatot", hf)])
                            TT("dve", Maug[0:64, hs].rearrange("p (g r t) -> p g r t", r=4, t=64),
                               diff[:, hs].rearrange("p (g r t) -> p g r t", r=4, t=64),
                               psml[0:64, 64 + hf * 128:64 + (hf + 1) * 128].rearrange("p (g t) -> p g t", t=64).unsqueeze(2).to_broadcast([64, 2, 4, 64]),
                               ALU.mult, [("diff", hf), "psml_cb"], [("Maug", hf)])
                            TT("pool", Ct[:, hs].rearrange("p (g r t) -> p g r t", r=4, t=64),
                               eR[:, hs].rearrange("p (g r t) -> p g r t", r=4, t=64),
                               CFall[:, hf * 2:(hf + 1) * 2, tcol + cs:tcol + cs + 64].unsqueeze(2).to_broadcast([128, 2, 4, 64]),
                               ALU.mult, [("eR", hf), ("CF", 2 * hf), ("CF", 2 * hf + 1)], [("Ct", hf)])
                        ACT(edt[:], edt[:], AF.Exp, ["edt"], ["edt"])
                        TT("pool", xdec[:].rearrange("p (h q) -> p h q", q=64), xaug[0:64, :].rearrange("p (h q) -> p h q", q=64),
                           bc_last(edt[:], 64), ALU.mult, ["xaug", "edt"], ["xdec"])
                        for h in range(16):
                            jp, hh = h // 2, h % 2
                            o_ap = yp[hh * 64:(hh + 1) * 64, jp * 64:(jp + 1) * 64]
                            MM(o_ap, xaug[:, h * 64:(h + 1) * 64], Maug[:, h * 64:(h + 1) * 64], True, False,
                               ["xaug", ("Maug", h // 8), "MaugC"], ["yp"])
                            MM(o_ap, Hb[:, h * 64:(h + 1) * 64], Ct[:, h * 64:(h + 1) * 64], False, True,
                               ["Hb", ("Ct", h // 8)], ["yp"])
                        CP("act", ys[:, :, cs:cs + 64], yp[:].rearrange("p (j t) -> p j t", t=64), ["yp"], [("y_sb", ti % 2)])
                        for g in range(4):
                            MM(pinU[:, g * 256:(g + 1) * 256], btok[:, g * 128:(g + 1) * 128], xdec[:, g * 256:(g + 1) * 256], True, True,
                               ["btok", "xdec"], [("pin", g // 2)])
                        TT("dve", Hs[:].rearrange("p (h q) -> p h q", q=64), Hs[:].rearrange("p (h q) -> p h q", q=64),
                           bc_last(eatot[:], 64), ALU.mult, ["Hs", ("eatot", 0), ("eatot", 1)], ["Hs"])
                        TT("dve", Hs[:], Hs[:], pinU[:], ALU.add, ["Hs", ("pin", 0), ("pin", 1)], ["Hs"])
                        CP("act", Hb[:], Hs[:], ["Hs"], ["Hb"])
                    DMA("sp", yl_d[:, :, tcol:tcol + TA], ys[:], [("y_sb", ti % 2)], ["yl_d"])
                if dbg:
                    DMA("sp", dbg_out("yl", [128, 8, T], BF16)[:, :, :], yl_d, ["yl_d"], ["dbg0"])
                    DMA("sp", dbg_out("sz", [128, 8, T], BF16)[:, :, :], sz_d, ["sz_d"], ["dbg1"])
                    DMA("sp", dbg_out("Hs", [128, 1024])[:, :], Hs[:], ["Hs"], ["dbg2"])
                    DMA("sp", dbg_out("acumG", [16, T])[:, :], acumG[:], ["acumG"], ["dbg3"])
                    DMA("sp", dbg_out("CF", [128, 4, T], BF16)[:, :, :], CFall[:], [("CF", g) for g in range(4)], ["dbg4"])
                    if A3_CHUNKS == 1:
                        DMA("sp", dbg_out("acT", [64, 16])[:, :], acT[:], ["acT"], ["dbg5"])
                        DMA("sp", dbg_out("dtx", [128, 16])[:, :], dtx[:], ["dtx"], ["dbg6"])
                        DMA("sp", dbg_out("Wm", [64, 1024])[:, :], diff[:], [("diff", 0), ("diff", 1)], ["dbg7"])
                        DMA("sp", dbg_out("Maug", [128, 1024], BF16)[:, :], Maug[:], [("Maug", 0), ("Maug", 1)], ["dbg8"])
                        DMA("sp", dbg_out("xaug", [128, 1024], BF16)[:, :], xaug[:], ["xaug"], ["dbg9"])
                        DMA("sp", dbg_out("xdec", [64, 1024], BF16)[:, :], xdec[:], ["xdec"], ["dbg10"])
                        DMA("sp", dbg_out("btok", [64, 512], BF16)[:, :], btok[:], ["btok"], ["dbg11"])
                        DMA("sp", dbg_out("Ct", [128, 1024], BF16)[:, :], Ct[:], [("Ct", 0), ("Ct", 1)], ["dbg12"])
                        DMA("sp", dbg_out("eatot", [128, 16])[:, :], eatot[:], [("eatot", 0), ("eatot", 1)], ["dbg13"])
                        DMA("sp", dbg_out("edt", [64, 16])[:, :], edt[:], ["edt"], ["dbg14"])
                        DMA("sp", dbg_out("acum", [16, 512])[:, :], acum[:], ["acum"], ["dbg15"])
                        DMA("sp", dbg_out("dtv", [16, 512])[:, :], dtv[:], ["dtv"], ["dbg16"])
                emit()

            with _Phase() as P:
                stop_check()
                xbuf = sb(P, "xbuf", [128, EXW], F32)
                rblk = sb(P, "rblk", [128, 7, EXW], F32)
                Sin_h = sb(P, "Sin_h", [128, 1024], F32)
                Sin_s = sb(P, "Sin_s", [128, 1024], F32)
                Tm = sb(P, "Tm", [128, 1024], F32)
                Tm2 = sb(P, "Tm2", [128, 1024], F32)
                agl = sb(P, "agl", [16, 16], F32)
                ones16 = sb(P, "ones16x", [16, 128], F32)
                pr = ps(P, "pr", [128, 512], F32)

                MEMSET("pool", xbuf[:, 2072:3072], 0.0, ["xbuf5"])
                CP("dve", xbuf[:, 0:1024], S_hg[:].rearrange("p h v -> p (h v)"), (), ["xbuf"])
                CP("pool", xbuf[:, 1024:2048], Hs[:], (), ["xbuf2"])
                ACT(xbuf[:, 2048:2056], Bprev[:], AF.Exp, (), ["xbuf3"])
                MEMSET("pool", ones16[:], 1.0, ["ones16x"])
                TT("dve", agl[:], ident_f[0:16, 0:16], acumG[:, T - 1:T].to_broadcast([16, 16]), ALU.mult, (), ["agl"])
                MM(pr[:, 0:16], ones16[:], agl[:], True, True, ["ones16x", "agl"], ["pr"])
                ACT(xbuf[:, 2056:2072], pr[:, 0:16], AF.Exp, ["pr"], ["xbuf4"])
                NOX = int(os.environ.get("K_NOX", 0))
                if not NOX:
                    off = 0
                    for i, w in enumerate(EX_W):
                        DMA("sp", ex_in[i].ap(), xbuf[:, off:off + w], ["xbuf", "xbuf2", "xbuf3", "xbuf4", "xbuf5"], [("ex_in", i)])

                        def cc1(e, i=i):
                            return e.collective_compute("AllGather", ALU.bypass, replica_groups=[list(range(NCORES))],
                                                        ins=[ex_in[i].ap().opt()], outs=[ex_out[i].ap().opt()])
                        S.op("pool", cc1, [("ex_in", i)], [("ex_out", i)])
                        off += w
                    DMA("sp", fence_in.ap(), xbuf[:, 2072:2136], ["xbuf5"], ["fence_in"])

                    def ccf(e):
                        return e.collective_compute("AllReduce", ALU.add, replica_groups=[list(range(NCORES))],
                                                    ins=[fence_in.ap().opt()], outs=[fence_out[0].ap().opt()])
                    S.op("pool", ccf, ["fence_in"] + [("ex_out", i) for i in range(3)], ["fence"])
                    off = 0
                    for i, w in enumerate(EX_W):
                        DMA("sp", rblk[:, :, off:off + w], ex_out[i].ap()[0:7 * 128, :].rearrange("(r p) w -> p r w", p=128),
                            [("ex_out", i), "fence"], ["rblk"])
                        off += w
                else:
                    MEMSET("pool", rblk[:], 0.0, ["rblk"])
                MEMSET("pool", Sin_h[:], 0.0, ["Sin_h"])
                MEMSET("pool", Sin_s[:], 0.0, ["Sin_s"])
                for r in range(7):
                    mcol = flags[:, 1 + r:2 + r]
                    TT("dve", Tm[:].rearrange("p (h v) -> p h v", v=128), Sin_h[:].rearrange("p (h v) -> p h v", v=128),
                       bc_last(rblk[:, r, 2048:2056], 128), ALU.mult, ["Sin_h", "rblk"], ["Tm"])
                    TT("dve", Tm[:], Tm[:], rblk[:, r, 0:1024], ALU.add, ["Tm", "rblk"], ["Tm"])
                    TT("dve", Tm[:], Tm[:], Sin_h[:], ALU.subtract, ["Tm", "Sin_h"], ["Tm"])
                    STT(Sin_h[:], Tm[:], mcol, Sin_h[:], ALU.mult, ALU.add, ["Tm", "Sin_h", "flags"], ["Sin_h"])
                    TT("pool", Tm2[:].rearrange("p (h q) -> p h q", q=64), Sin_s[:].rearrange("p (h q) -> p h q", q=64),
                       bc_last(rblk[:, r, 2056:2072], 64), ALU.mult, ["Sin_s", "rblk"], ["Tm2"])
                    TT("pool", Tm2[:], Tm2[:], rblk[:, r, 1024:2048], ALU.add, ["Tm2", "rblk"], ["Tm2"])
                    TT("pool", Tm2[:], Tm2[:], Sin_s[:], ALU.subtract, ["Tm2", "Sin_s"], ["Tm2"])
                    TS("pool", Tm2[:], Tm2[:], mcol, None, ALU.mult, None, ["Tm2", "flags"], ["Tm2"])
                    TT("pool", Sin_s[:], Sin_s[:], Tm2[:], ALU.add, ["Tm2", "Sin_s"], ["Sin_s"])
                CP("act", Sin_hb[:].rearrange("p h v -> p (h v)"), Sin_h[:], ["Sin_h"], ["Sin_hb"])
                CP("act", Sin_sb[:], Sin_s[:], ["Sin_s"], ["Sin_sb"])
                if dbg:
                    DMA("sp", dbg_out("rblk", [128, 7, EXW])[:, :, :], rblk[:], ["rblk"], ["dbg0"])
                    DMA("sp", dbg_out("Sin_h", [128, 1024])[:, :], Sin_h[:], ["Sin_h"], ["dbg1"])
                    DMA("sp", dbg_out("Sin_s", [128, 1024])[:, :], Sin_s[:], ["Sin_s"], ["dbg2"])
                    DMA("sp", dbg_out("xbuf", [128, EXW])[:, :], xbuf[:], ["xbuf", "xbuf2", "xbuf3", "xbuf4"], ["dbg3"])
                emit()

            with _Phase() as P:
                stop_check()
                bufs = {
                    "junk": sb(P, "junk", [128, 2048], BF16),
                    "ss": sb(P, "ss", [128, 4], F32),
                    "xn": sb(P, "xn", [128, 4, 2048], BF16),
                    "tp": [ps(P, f"tp{i}", [128, 512], BF16) for i in range(2)],
                }
                g1b = sb(P, "g1b", [128, 2048], F32)
                selm = sb(P, "selm", [16, 1024], F32)
                hq = [sb(P, f"hq{i}", [128, 512], BF16) for i in range(2)]
                ho = [sb(P, f"ho{i}", [128, 512], BF16) for i in range(2)]
                hg = [sb(P, f"hg{i}", [128, 512], BF16) for i in range(2)]
                yl = [sb(P, f"yl{i}", [128, 512], BF16) for i in range(2)]
                szl = [sb(P, f"szl{i}", [128, 512], BF16) for i in range(2)]
                Qg = [sb(P, f"Qg{i}", [128, 512], BF16) for i in range(2)]
                tO = [sb(P, f"tO{i}", [128, 512], F32) for i in range(4)]
                sq = [sb(P, f"sq{i}", [128, 512], BF16) for i in range(4)]
                rs = [sb(P, f"rs{i}", [128, 512], F32) for i in range(2)]
                eP = [sb(P, f"eP{i}", [128, 512], F32) for i in range(2)]
                mixT = sb(P, "mixT", [128, 16, 512], BF16)
                wo = [sb(P, f"wo{i}", [128, 16, 256], BF16) for i in range(2)]
                x1t = sb(P, "x1t", [128, 4, 2048], F32)
                tmpx = [sb(P, f"tmpx{i}", [128, 256], F32) for i in range(2)]
                h2Tt = sb(P, "h2Tt", [128, 16, 512], BF16)
                h2l = sb(P, "h2l", [128, 1024], F32)
                pc = [ps(P, f"pc{i}", [128, 512], F32) for i in range(2)]
                pss = [ps(P, f"pss{i}", [128, 512], F32) for i in range(2)]
                prp = ps(P, "prp", [128, 512], F32)
                w_out_v = w_out[0].rearrange("(k p) d -> p k d", p=128)

                DMA("sp", g1b[:], mod_d[0:1, 2 * 2048:3 * 2048].partition_broadcast(128), (), ["g1b"])
                DMA("act", x1t[:, 0, :], b_mod[0:1, 2 * 2048:3 * 2048].partition_broadcast(128), (), [("x1t", 0)])
                TT("pool", g1b[:], g1b[:], x1t[:, 0, :], ALU.add, ["g1b", ("x1t", 0)], ["g1b"])
                DMA("sp", selm[:], cst[0:16, C_SEL:C_SEL + 1024], (), ["selm"])
                wcnt = 0
                for ti in range(NT):
                    tcol = ti * TA
                    for j in range(4):
                        r0 = 3 + tcol + j * 128
                        DMA("act", x1t[:, j, :], x_c[r0:r0 + 128, :], (), [("x1t", j)])
                    for h in range(8):
                        b2 = h % 2
                        DMA("sp", hq[b2][:], hq_d[:, h, tcol:tcol + TA], (), [("hq", b2)])
                        DMA("sp", ho[b2][:], ho_d[:, h, tcol:tcol + TA], (), [("ho", b2)])
                        DMA("sp", hg[b2][:], hg_d[:, h, tcol:tcol + TA], (), [("hg", b2)])
                        q_ = Qg[b2]
                        TT("pool", q_[:].rearrange("p (c t) -> p c t", t=64), hq[b2][:].rearrange("p (c t) -> p c t", t=64),
                           bc_last(Gall[:, h, ti * 8:(ti + 1) * 8], 64), ALU.mult, [("hq", b2)], [("Qg", b2)])
                        MM(pc[b2][:], Sin_hb[:, h, :], q_[:], True, True, [("Qg", b2)], [("pc", b2)])
                        to = tO[h % 4]
                        tok = ("tO", h % 4)
                        TT("dve", to[:], pc[b2][:], ho[b2][:], ALU.add, [("pc", b2), ("ho", b2)], [tok])
                        ACT(sq[h % 4][:], to[:], AF.Square, [tok], [("sq", h % 4)])
                        MM(pss[b2][:], ones_b[:], sq[h % 4][:], True, True, [("sq", h % 4)], [("pss", b2)])
                        r_ = rs[b2]
                        ACT(r_[:], pss[b2][:], AF.Ln, [("pss", b2)], [("rs", b2)], bias=epsb[:], scale=1.0 / 128)
                        ACT(r_[:], r_[:], AF.Exp, [("rs", b2)], [("rs", b2)], scale=-0.5)
                        TT("dve", to[:], to[:], r_[:], ALU.mult, [tok, ("rs", b2)], [tok])
                        TT("pool", mixT[:, h, :], to[:], hg[b2][:], ALU.mult, [tok, ("hg", b2)], [("mixT", h)])
                    for g in range(4):
                        for jj in range(2):
                            j = 2 * g + jj
                            DMA("sp", yl[jj][:], yl_d[:, j, tcol:tcol + TA], (), [("yl", jj)])
                            DMA("sp", szl[jj][:], sz_d[:, j, tcol:tcol + TA], (), [("szl", jj)])
                            for hh in range(2):
                                hd = 2 * j + hh
                                MM(pc[jj][hh * 64:(hh + 1) * 64, :], Sin_sb[:, hd * 64:(hd + 1) * 64], CFall[:, g, tcol:tcol + TA], True, True,
                                   (), [("pc", jj)])
                            MM(prp[:], selm[:, j * 128:(j + 1) * 128], acumG[:, tcol:tcol + TA], True, True, ["selm"], ["prp"])
                            ACT(eP[jj][:], prp[:], AF.Exp, ["prp"], [("eP", jj)])
                            to = tO[jj]
                            tok = ("tO", jj)
                            TT("dve", to[:], pc[jj][:], eP[jj][:], ALU.mult, [("pc", jj), ("eP", jj)], [tok])
                            TT("pool", to[:], to[:], yl[jj][:], ALU.add, [tok, ("yl", jj)], [tok])
                            TT("pool", to[:], to[:], szl[jj][:], ALU.mult, [tok, ("szl", jj)], [tok])
                            ACT(sq[jj][:], to[:], AF.Square, [tok], [("sq", jj)])
                            MM(pss[0][:], ones_b[:], sq[jj][:], jj == 0, jj == 1, [("sq", jj)], [("pss", 0)])
                        r_ = rs[0]
                        ACT(r_[:], pss[0][:], AF.Ln, [("pss", 0)], [("rs", 0)], bias=epsb[:], scale=1.0 / 256)
                        ACT(r_[:], r_[:], AF.Exp, [("rs", 0)], [("rs", 0)], scale=-0.5)
                        for jj in range(2):
                            j = 2 * g + jj
                            STT(mixT[:, 8 + j, :], tO[jj][:], snw[:, j:j + 1], r_[:], ALU.mult, ALU.mult,
                                [("tO", jj), ("rs", 0), "snw"], [("mixT", 8 + j)])
                    for dg in range(8):
                        wi = wcnt % 2
                        wcnt += 1
                        DMA("pool", wo[wi][:], w_out_v[:, :, dg * 256:(dg + 1) * 256], (), [("wo", wi)])
                        for sub in range(4):
                            pq = pc[sub % 2]
                            for k in range(16):
                                MM(pq[:, 0:256], mixT[:, k, sub * 128:(sub + 1) * 128], wo[wi][:, k, :], k == 0, k == 15,
                                   [("mixT", k), ("wo", wi)], [("pc", sub % 2)])
                            tx = tmpx[sub % 2]
                            TT("dve", tx[:], pq[:, 0:256], g1b[:, dg * 256:(dg + 1) * 256], ALU.mult, [("pc", sub % 2), "g1b"], [("tmpx", sub % 2)])
                            TT("pool", x1t[:, sub, dg * 256:(dg + 1) * 256], x1t[:, sub, dg * 256:(dg + 1) * 256], tx[:], ALU.add,
                               [("tmpx", sub % 2), ("x1t", sub)], [("x1t", sub)])
                    for j in range(4):
                        DMA("sp", x1_d[tcol + j * 128:tcol + (j + 1) * 128, :], x1t[:, j, :], [("x1t", j)], ["x1_d"])
                    norm_to_T(bufs, "h2Tt", 4, lambda j: (x1t[:, j, :], [("x1t", j)]), a2, shift2, h2Tt)
                    DMA("act", h2T_d[:, :, 2 + tcol:2 + tcol + TA], h2Tt[:], [("h2Tt", dk) for dk in range(16)], ["h2T_d"])
                    if ti == NT - 1:
                        MEMSET("pool", h2l[:], 0.0, ["h2l"])
                        CP("dve", h2l[:, 0:32].rearrange("p (k t) -> p k t", t=2), h2Tt[:, :, 510:512], [("h2Tt", dk) for dk in range(16)] + ["h2l"], ["h2l"])
                        DMA("sp", ex2_in.ap(), h2l[:], ["h2l"], ["ex2_in"])
                issue_cast(1000)
                S.drain_bg = True
                if dbg:
                    DMA("sp", dbg_out("x1", [T, D])[:, :], x1_d, ["x1_d"], ["dbg0"])
                    DMA("sp", dbg_out("h2T", [128, 16, 2 + T], BF16)[:, :, :], h2T_d, ["h2T_d"], ["dbg1"])
                emit()

        with _Phase() as P:
            stop_check()
            r2 = sb(P, "r2", [128, 7, 32], F32)
            hacc = sb(P, "hacc", [128, 32], F32)
            hout = sb(P, "hout", [128, 16, 2], BF16)

            def cc2(e):
                return e.collective_compute("AllGather", ALU.bypass, replica_groups=[list(range(NCORES))],
                                            ins=[ex2_in.ap().opt()], outs=[ex2_out.ap().opt()])
            if not int(os.environ.get("K_NOX", 0)):
                S.op("pool", cc2, (), ["ex2_out"])

                def ccf2(e):
                    return e.collective_compute("AllReduce", ALU.add, replica_groups=[list(range(NCORES))],
                                                ins=[fence_in.ap().opt()], outs=[fence_out[1].ap().opt()])
                S.op("pool", ccf2, ["ex2_out"], ["fence2"])
                DMA("sp", r2[:], ex2_out.ap()[0:7 * 128, 0:32].rearrange("(r p) w -> p r w", p=128), ["ex2_out", "fence2"], ["r2"])
            else:
                MEMSET("pool", r2[:], 0.0, ["r2"])
            MEMSET("pool", hacc[:], 0.0, ["hacc"])
            for r in range(7):
                STT(hacc[:], r2[:, r, :], flags[:, 8 + r:9 + r], hacc[:], ALU.mult, ALU.add, ["r2", "hacc"], ["hacc"])
            if int(os.environ.get("K_NOX", 0)) == 2:
                DMA("sp", hacc[:], ex2_in.ap()[:, 0:32], ["hacc"], ["hacc"])
            CP("dve", hout[:].rearrange("p k t -> p (k t)"), hacc[:], ["hacc"], ["hout"])
            DMA("sp", h2T_d[:, :, 0:2], hout[:], ["hout"], ["h2T_d"])
            if dbg:
                DMA("sp", dbg_out("r2", [128, 7, 32])[:, :, :], r2[:], ["r2"], ["dbg0"])
                DMA("sp", dbg_out("hacc", [128, 32])[:, :], hacc[:], ["hacc"], ["dbg1"])
                DMA("sp", dbg_out("h2Tb", [128, 16, 2 + T], BF16)[:, :, :], h2T_d, ["h2T_d"], ["dbg2"])
            emit()

        with _Phase() as P:
            stop_check()
            g2b = sb(P, "g2b", [128, 2048], F32)
            fwb = sb(P, "fwb", [128, 2048], F32)
            fcw = sb(P, "fcw", [128, 88, 3], F32)
            fcb = sb(P, "fcb", [128, 88], F32)
            uhalo = sb(P, "uhalo", [128, 88, 2], F32)
            h2Tt = [sb(P, "h2Tt0", [128, 16, 514], BF16)] * 2
            a_half = sb(P, "a_half", [128, 22, 512], BF16)
            wu = [sb(P, f"wu{i}", [128, 16, 2, 256], BF16) for i in range(3)]
            wd = [sb(P, f"wd{i}", [128, 11, 512], BF16) for i in range(4)]
            ur = [sb(P, f"ur{i}", [128, 514], F32) for i in range(2)]
            ac2 = [sb(P, f"ac2_{i}", [128, 512], F32) for i in range(2)]
            sgf = sb(P, "sgf", [128, 512], F32)
            x2t = sb(P, "x2t", [128, 4, 2048], F32)
            tmpx = [sb(P, f"tmpx{i}", [128, 512], F32) for i in range(2)]
            ssf = sb(P, "ssf", [128, 4], F32)
            junk = sb(P, "junk", [128, 2048], BF16)
            ppu = [ps(P, f"ppu{i}", [128, 512], F32) for i in range(4)]
            ppd = [ps(P, f"ppd{i}", [128, 512], F32) for i in range(2)]
            pph = ps(P, "pph", [128, 512], F32)
            w_up_v = ffn_w_up[0].rearrange("(k p) c -> p k c", p=128)
            w_dn_v = ffn_w_down[0].rearrange("(i p) d -> p i d", p=128)

            DMA("sp", g2b[:], mod_d[0:1, 5 * 2048:6 * 2048].partition_broadcast(128), (), ["g2b"])
            DMA("act", fwb[:], b_mod[0:1, 5 * 2048:6 * 2048].partition_broadcast(128), (), ["fwb"])
            TT("pool", g2b[:], g2b[:], fwb[:], ALU.add, ["g2b", "fwb"], ["g2b"])
            DMA("sp", fwb[:], final_w.rearrange("(o d) -> o d", o=1).partition_broadcast(128), ["fwb"], ["fwb"])
            for j in range(3):
                DMA("sp", fcw[:, :, j], ffn_conv_w[0, j:j + 1, :].rearrange("o (c p) -> p (o c)", p=128), (), ["fcw"])
            DMA("sp", fcb[:], ffn_conv_b.rearrange("o (c p) -> p (o c)", p=128), (), ["fcb"])
            wcu = 0
            wcd = 0
            for ti in range(NT):
                tcol = ti * TA
                hb = h2Tt[0]
                hk = ("h2Tt", 0)
                DMA("sp", hb[:], h2T_d[:, :, tcol:tcol + 514], (), [hk])
                for j in range(4):
                    DMA("act", x2t[:, j, :], x1_d[tcol + j * 128:tcol + (j + 1) * 128, :], (), [("x2t", j)])
                for hf in range(2):
                    for gi in range(11):
                        wi = wcu % 3
                        wcu += 1
                        DMA("sp" if wcu % 2 else "act", wu[wi][:].rearrange("p k g c -> p (k g c)"), wup_t[hf * 11 + gi], (), [("wu", wi)])
                        for half in range(2):
                            fl = gi * 2 + half
                            fc = hf * 22 + fl
                            accs = []
                            for gv in range(2):
                                ch = gv * 44 + fc
                                pp = ppu[(fl % 2) * 2 + gv]
                                pk = ("ppu", (fl % 2) * 2 + gv)
                                for k in range(16):
                                    MM(pp[:], wu[wi][:, k, gv, half * 128:(half + 1) * 128], hb[:, k, 2:514], k == 0, k == 15,
                                       [("wu", wi), hk], [pk])
                                u = ur[gv]
                                uk = ("ur", gv)
                                if ti == 0:
                                    for k in range(16):
                                        MM(pph[:, gv * 2:gv * 2 + 2], wu[wi][:, k, gv, half * 128:(half + 1) * 128], hb[:, k, 0:2], k == 0, k == 15,
                                           [("wu", wi), hk], [("pph", gv)])
                                    CP("act", u[:, 0:2], pph[:, gv * 2:gv * 2 + 2], [("pph", gv)], [uk])
                                else:
                                    CP("pool", u[:, 0:2], uhalo[:, ch, :], [("uhalo", ch)], [uk])
                                ACT(u[:, 2:514], pp[:], AF.Copy, [pk, uk], [uk])
                                CP("pool", uhalo[:, ch, :], u[:, 512:514], [uk], [("uhalo", ch)])
                                ac = ac2[gv]
                                ak = ("ac2", gv)
                                TS("dve", ac[:], u[:, 2:514], fcw[:, ch, 2:3], fcb[:, ch:ch + 1], ALU.mult, ALU.add, [uk, "fcw", "fcb"], [ak])
                                STT(ac[:], u[:, 1:513], fcw[:, ch, 1:2], ac[:], ALU.mult, ALU.add, [uk, ak, "fcw"], [ak])
                                STT(ac[:], u[:, 0:512], fcw[:, ch, 0:1], ac[:], ALU.mult, ALU.add, [uk, ak, "fcw"], [ak])
                                accs.append((ac, ak))
                            ACT(sgf[:], accs[0][0][:], AF.Silu, [accs[0][1]], ["sgf"])
                            TT("pool", a_half[:, fl, :], sgf[:], accs[1][0][:], ALU.mult, ["sgf", accs[1][1]], [("a_half", fl)])
                    for dg in range(4):
                        wis = []
                        for j in range(2):
                            wi = wcd % 4
                            wcd += 1
                            wis.append(wi)
                            DMA("sp" if wcd % 2 else "act", wd[wi][:].rearrange("p i d -> p (i d)"), wdn_t[(hf * 4 + dg) * 2 + j], (), [("wd", wi)])
                        for sub in range(4):
                            pq = ppd[sub % 2]
                            for i in range(22):
                                MM(pq[:], a_half[:, i, sub * 128:(sub + 1) * 128], wd[wis[i // 11]][:, i % 11, :], i == 0, i == 21,
                                   [("a_half", i), ("wd", wis[i // 11])], [("ppd", sub % 2)])
                            tx = tmpx[sub % 2]
                            TT("dve", tx[:], pq[:], g2b[:, dg * 512:(dg + 1) * 512], ALU.mult, [("ppd", sub % 2), "g2b"], [("tmpx", sub % 2)])
                            TT("pool", x2t[:, sub, dg * 512:(dg + 1) * 512], x2t[:, sub, dg * 512:(dg + 1) * 512], tx[:], ALU.add,
                               [("tmpx", sub % 2), ("x2t", sub)], [("x2t", sub)])
                for j in range(4):
                    ACT(junk[:], x2t[:, j, :], AF.Square, [("x2t", j)], [("ssf", j)], accum_out=ssf[:, j:j + 1])
                    TS("pool", ssf[:, j:j + 1], ssf[:, j:j + 1], 1.0 / D, EPS, ALU.mult, ALU.add, [("ssf", j)], [("ssf", j)])
                    TT("pool", ssf[:, j:j + 1], ssf[:, j:j + 1], negh[:], ALU.pow, [("ssf", j)], [("ssf", j)])
                    STT(x2t[:, j, :], x2t[:, j, :], ssf[:, j:j + 1], fwb[:], ALU.mult, ALU.mult, [("x2t", j), ("ssf", j), "fwb"], [("x2t", j)])
                    DMA("sp", y_out[tcol + j * 128:tcol + (j + 1) * 128, :], x2t[:, j, :], [("x2t", j)], ["y_out"])
            emit()
    return nc


def _consts():
    c = np.zeros((128, C_W), np.float32)
    c[:, C_ID:C_ID + 128] = np.eye(128, dtype=np.float32)
    sm = np.ones(512, np.float32)
    sm[::64] = 0.0
    c[:, C_SCAN:C_SCAN + 512] = sm[None, :]
    s = np.arange(64)[:, None]
    t = np.arange(64)[None, :]
    m01 = (t >= s).astype(np.float32)
    c[0:64, C_ATT:C_ATT + 512] = np.tile(m01, (1, 8))
    neg = np.where(t >= s, 0.0, -30000.0).astype(np.float32)
    c[0:64, C_NEG:C_NEG + 1024] = np.tile(neg, (1, 16))
    for h in range(16):
        c[h, C_BLK + h * 64:C_BLK + (h + 1) * 64] = 1.0
        c[h, C_SEL + h * 64:C_SEL + (h + 1) * 64] = 1.0
    c[0:64, C_DIAG:C_DIAG + 1024] = np.tile(np.eye(64, dtype=np.float32), (1, 16))
    lm = np.zeros((96, 64), np.float32)
    lm[0:16, :] = 1.0
    lm[32:96, :] = np.eye(64, dtype=np.float32)
    c[0:96, C_LM:C_LM + 64] = lm
    return c


_NC_CACHE = {}


def kernel(**inputs):
    x = np.ascontiguousarray(np.asarray(inputs["x"], dtype=np.float32))[0]
    if "nc" not in _NC_CACHE:
        _NC_CACHE["nc"] = build_program()
    nc = _NC_CACHE["nc"]
    cst = _consts()
    shared = {}
    for k in ("c", "w_mod", "b_mod", "norm1_w", "w_in", "hgrn_lb", "hgrn_gnorm_w", "ssd_conv_w", "ssd_conv_b",
              "ssd_dt_bias", "ssd_a_log", "ssd_d", "ssd_norm_w", "w_out", "norm2_w", "ffn_w_up", "ffn_conv_w",
              "ffn_conv_b", "ffn_w_down", "final_norm_w"):
        shared[k] = np.ascontiguousarray(np.asarray(inputs[k], dtype=np.float32))
    in_maps = []
    for r in range(NCORES):
        xc = np.zeros((T + 3, D), np.float32)
        if r > 0:
            xc[0:3] = x[r * T - 3:r * T]
        xc[3:] = x[r * T:(r + 1) * T]
        fl = np.zeros((128, 16), np.float32)
        fl[:, 0] = 1.0 if r > 0 else 0.0
        for q in range(7):
            fl[:, 1 + q] = 1.0 if q < r else 0.0
            fl[:, 8 + q] = 1.0 if q == r - 1 else 0.0
        m = dict(shared)
        m["x_c"] = xc
        m["flags"] = fl
        m["cst"] = cst
        in_maps.append(m)
    res = run_bass_kernel_spmd(nc, in_maps, core_ids=list(range(NCORES)))
    out = np.concatenate([res.results[r]["y_out"] for r in range(NCORES)], axis=0)
    return out.reshape(1, NCORES * T, D).astype(np.float32)
```

```python
import os
import numpy as np
from contextlib import ExitStack
import concourse.bass as bass
import concourse.mybir as mybir
from concourse.bass_utils import run_bass_kernel_spmd

F32 = mybir.dt.float32
BF16 = mybir.dt.bfloat16
ALU = mybir.AluOpType
AF = mybir.ActivationFunctionType

NCORES = 8
D = 2048
T = 2048
TA = 512
NT = T // TA
DIN = 7184
DFF = 5632
EPS = 1e-6

ENGS = ("pe", "act", "dve", "pool", "sp")
NDMA = 12
NBG = 16

C_ID = 0
C_SCAN = 128
C_ATT = 640
C_NEG = 1152
C_BLK = 2176
C_SEL = 3200
C_DIAG = 4224
C_LM = 5248
C_W = 5312


class _Op:
    __slots__ = ("eng", "fn", "deps", "signal", "count", "is_dma", "dslot", "dval", "bg")

    def __init__(self, eng, fn, is_dma):
        self.eng = eng
        self.fn = fn
        self.deps = []
        self.signal = False
        self.count = 0
        self.is_dma = is_dma
        self.dslot = 0
        self.dval = 0
        self.bg = False


class Sched:
    def __init__(self):
        self.cbase = {e: 0 for e in ENGS}
        self.ndma = {e: 0 for e in ENGS}
        self.nbg = 0
        self.drain_bg = False
        self.phase = 0
        self.begin()

    def begin(self):
        self.ops = {e: [] for e in ENGS}
        self.res = {}

    def op(self, eng, fn, r=(), w=(), dma=False, bg=False):
        o = _Op(eng, fn, dma)
        o.bg = bg
        deps = []
        for k in r:
            st = self.res.get(k)
            if st is not None and st[0] is not None:
                deps.append(st[0])
        for k in w:
            st = self.res.get(k)
            if st is not None:
                if st[0] is not None:
                    deps.append(st[0])
                deps.extend(st[1])
        seen = set()
        for d in deps:
            if id(d) in seen or d is o:
                continue
            seen.add(id(d))
            if d.eng == eng and eng == "pe" and not d.is_dma:
                continue
            o.deps.append(d)
            if not d.is_dma:
                d.signal = True
        if dma and bg:
            n = self.nbg
            o.dslot = n % NBG
            o.dval = 16 * (n // NBG + 1)
            self.nbg = n + 1
        elif dma:
            n = self.ndma[eng]
            o.dslot = n % NDMA
            o.dval = 16 * (n // NDMA + 1)
            self.ndma[eng] = n + 1
        self.ops[eng].append(o)
        for k in r:
            st = self.res.setdefault(k, [None, []])
            st[1].append(o)
        for k in w:
            self.res[k] = [o, []]
        return o

    def emit(self, nc, block, sems, dsems, barsem, bgsems=None):
        last_c = {}
        for e in ENGS:
            for o in reversed(self.ops[e]):
                if not o.is_dma:
                    o.signal = True
                    last_c[e] = o
                    break
        for e in ENGS:
            c = self.cbase[e]
            for o in self.ops[e]:
                if o.signal and not o.is_dma:
                    c += 1
                    o.count = c
            self.cbase[e] = c
        sched = self
        ndma_end = dict(self.ndma)
        nbg_end = self.nbg
        drain_bg = self.drain_bg
        self.drain_bg = False
        phase = self.phase

        def run(e, eng):
            waited = {}

            def wait(sem, val, key):
                if waited.get(key, 0) >= val:
                    return
                waited[key] = val
                eng.wait_ge(sem, val)

            if phase > 0:
                eng.wait_ge(barsem, len(ENGS) * phase)
            for o in sched.ops[e]:
                for d in o.deps:
                    if d.is_dma:
                        wait(dsems[d.eng][d.dslot], d.dval, ("d", d.eng, d.dslot))
                    else:
                        wait(sems[d.eng], d.count, ("c", d.eng))
                if o.is_dma and o.bg:
                    if o.dval > 16:
                        wait(bgsems[o.dslot], o.dval - 16, ("b", o.dslot))
                    ins = o.fn(eng)
                    ins.then_inc(bgsems[o.dslot], 16)
                elif o.is_dma:
                    if o.dval > 16:
                        wait(dsems[e][o.dslot], o.dval - 16, ("d", e, o.dslot))
                    ins = o.fn(eng)
                    ins.then_inc(dsems[e][o.dslot], 16)
                else:
                    ins = o.fn(eng)
                    if o.signal:
                        ins.then_inc(sems[e], 1)
            if drain_bg:
                for s_ in range(min(nbg_end, NBG)):
                    last = ((nbg_end - 1 - s_) // NBG) * NBG + s_
                    wait(bgsems[s_], 16 * (last // NBG + 1), ("b", s_))
            n = ndma_end[e]
            for s in range(min(n, NDMA)):
                last = ((n - 1 - s) // NDMA) * NDMA + s
                wait(dsems[e][s], 16 * (last // NDMA + 1), ("d", e, s))
            if e in last_c:
                wait(sems[e], last_c[e].count, ("c", e))
            eng.sem_inc(barsem, 1)

        block.tensor(lambda eng: run("pe", eng))
        block.scalar(lambda eng: run("act", eng))
        block.vector(lambda eng: run("dve", eng))
        block.gpsimd(lambda eng: run("pool", eng))
        block.sync(lambda eng: run("sp", eng))
        self.phase += 1
        self.begin()


class _Stop(Exception):
    pass


class _Phase(ExitStack):
    def __exit__(self, et, ev, tb):
        r = super().__exit__(et, ev, tb)
        return bool(r) or (et is not None and issubclass(et, _Stop))


class _Lenient(ExitStack):
    truncated = False

    def __exit__(self, et, ev, tb):
        try:
            return super().__exit__(et, ev, tb)
        except AssertionError:
            if not _Lenient.truncated:
                raise
            return False


class _LazyIn:
    def __init__(self, nc, name, shape, used):
        self._nc, self._name, self._shape, self._used, self._apv = nc, name, list(shape), used, None

    def _ap(self):
        if self._apv is None:
            self._apv = self._nc.dram_tensor(self._name, self._shape, F32, kind="ExternalInput").ap()
            self._used.append(self._name)
        return self._apv

    def __getitem__(self, k):
        return self._ap()[k]

    def __getattr__(self, a):
        return getattr(self._ap(), a)


def build_program(upto=99, dbg=None):
    nc = bass.Bass("TRN2", target_bir_lowering=False)
    used_inputs = []
    nc._used_inputs = used_inputs
    dbg_outs = {}
    nc._dbg_outs = dbg_outs

    def din(name, shape):
        return _LazyIn(nc, name, shape, used_inputs)

    def dbg_out(name, shape, dt=F32):
        t = nc.dram_tensor("dbg_" + name, list(shape), dt, kind="ExternalOutput").ap()
        dbg_outs[name] = t
        return t

    x_c = din("x_c", [T + 3, D])
    flags_d = din("flags", [128, 16])
    cst = din("cst", [128, C_W])
    c_in = din("c", [1, D])
    w_mod_c = din("w_mod_c", [D, 1536])
    b_mod = din("b_mod", [1, 6 * D])
    norm1_w = din("norm1_w", [1, D])
    w_in = din("w_in", [1, D, DIN])
    hgrn_lb = din("hgrn_lb", [2, 1024])
    hgrn_gnw = din("hgrn_gnorm_w", [1, 1024])
    ssd_conv_w = din("ssd_conv_w", [1, 4, 2048])
    ssd_conv_b = din("ssd_conv_b", [1, 2048])
    ssd_dt_bias = din("ssd_dt_bias", [1, 16])
    ssd_a_log = din("ssd_a_log", [1, 16])
    ssd_d = din("ssd_d", [1, 16])
    ssd_norm_w = din("ssd_norm_w", [1, 1024])
    w_out = din("w_out", [1, D, D])
    norm2_w = din("norm2_w", [1, D])
    ffn_w_up = din("ffn_w_up", [1, D, 2 * DFF])
    ffn_conv_w = din("ffn_conv_w", [1, 3, 2 * DFF])
    ffn_conv_b = din("ffn_conv_b", [1, 2 * DFF])
    ffn_w_down = din("ffn_w_down", [1, DFF, D])
    final_w = din("final_norm_w", [D])
    y_out = nc.dram_tensor("y_out", [T, D], F32, kind="ExternalOutput").ap() if upto >= 7 else None

    mod_d = nc.dram_tensor("mod_d", [1, 6 * D], F32).ap()
    hT_d = nc.dram_tensor("hT_d", [128, 16, 3 + T], BF16).ap()
    hq_d = nc.dram_tensor("hq_d", [128, 8, T], BF16).ap()
    ho_d = nc.dram_tensor("ho_d", [128, 8, T], BF16).ap()
    hg_d = nc.dram_tensor("hg_d", [128, 8, T], BF16).ap()
    sz_d = nc.dram_tensor("sz_d", [128, 8, T], BF16).ap()
    yl_d = nc.dram_tensor("yl_d", [128, 8, T], BF16).ap()
    x1_d = nc.dram_tensor("x1_d", [T, D], F32).ap()
    h2T_d = nc.dram_tensor("h2T_d", [128, 16, 2 + T], BF16).ap()
    EXW = 3072
    EX_W = [1024, 1024, 1024]
    ex_in = [nc.dram_tensor(f"ex_in{i}", [128, w], F32) for i, w in enumerate(EX_W)]
    ex_out = [nc.dram_tensor(f"ex_out{i}", [NCORES * 128, w], F32) for i, w in enumerate(EX_W)]
    agm_in = nc.dram_tensor("agm_in", [128, 1024], F32)
    agm_out = nc.dram_tensor("agm_out", [NCORES * 128, 1024], F32)
    fence_in = nc.dram_tensor("fence_in", [128, 64], F32)
    fence_out0 = nc.dram_tensor("fence_out_p0", [128, 64], F32)
    fence_out = [nc.dram_tensor(f"fence_out{i}", [128, 64], F32) for i in range(2)]
    ex2_in = nc.dram_tensor("ex2_in", [128, 1024], F32)
    ex2_out = nc.dram_tensor("ex2_out", [NCORES * 128, 1024], F32)

    wup_t = nc.dram_tensor("wup_t", [22, 128, 16 * 2 * 256], BF16).ap()
    wdn_t = nc.dram_tensor("wdn_t", [16, 128, 11 * 512], BF16).ap()
    S = Sched()
    cast_jobs = []

    uid = [0]

    def sb(stack, name, shape, dt):
        uid[0] += 1
        if stack is None:
            return nc.alloc_sbuf_tensor(f"{name}_u{uid[0]}", list(shape), dt, side="right")
        return stack.enter_context(nc.sbuf_tensor(f"{name}_u{uid[0]}", list(shape), dt))

    def ps(stack, name, shape, dt):
        uid[0] += 1
        return stack.enter_context(nc.psum_tensor(f"{name}_u{uid[0]}", list(shape), dt))

    def ACT(out, in_, func, r, w, bias=None, scale=None, accum_out=None):
        kw = {}
        if bias is not None:
            kw["bias"] = bias
        if scale is not None:
            kw["scale"] = scale
        if accum_out is not None:
            kw["accum_out"] = accum_out
        S.op("act", lambda e: e.activation(out=out, in_=in_, func=func, **kw), r, w)

    def TT(eng, out, in0, in1, op, r, w):
        S.op(eng, lambda e: e.tensor_tensor(out=out, in0=in0, in1=in1, op=op), r, w)

    def TS(eng, out, in0, s1, s2, op0, op1, r, w):
        if s2 is None:
            S.op(eng, lambda e: e.tensor_scalar(out=out, in0=in0, scalar1=s1, scalar2=None, op0=op0), r, w)
        else:
            S.op(eng, lambda e: e.tensor_scalar(out=out, in0=in0, scalar1=s1, scalar2=s2, op0=op0, op1=op1), r, w)

    def STT(out, in0, scalar, in1, op0, op1, r, w):
        S.op("dve", lambda e: e.scalar_tensor_tensor(out=out, in0=in0, scalar=scalar, in1=in1, op0=op0, op1=op1), r, w)

    def CP(eng, out, in_, r, w):
        if eng == "act":
            ACT(out, in_, AF.Copy, r, w)
        else:
            S.op(eng, lambda e: e.tensor_copy(out=out, in_=in_), r, w)

    def MM(out, lhsT, rhs, start, stop, r, w):
        S.op("pe", lambda e: e.matmul(out, lhsT=lhsT, rhs=rhs, start=start, stop=stop), r, w)

    def TR(out, in_, ident, r, w):
        S.op("pe", lambda e: e.transpose(out=out, in_=in_, identity=ident), r, w)

    def DMA(eng, out, in_, r, w):
        S.op(eng, lambda e: e.dma_start(out=out, in_=in_), r, w, dma=True)

    def SCAN(out, d0, d1, init, r, w):
        S.op("dve", lambda e: e.tensor_tensor_scan(out=out, data0=d0, data1=d1, initial=init,
                                                    op0=ALU.mult, op1=ALU.add), r, w)

    def MEMSET(eng, ap, val, w):
        S.op(eng, lambda e: e.memset(ap, val), (), w)

    def RECIP(out, in_, r, w):
        S.op("dve", lambda e: e.reciprocal(out=out, in_=in_), r, w)

    def bc_last(ap, n):
        sh = list(ap.shape)
        return ap.unsqueeze(len(sh)).to_broadcast(sh + [n])

    if True:
     def make_cast_jobs():
        w_up_v = ffn_w_up[0].rearrange("(k p) c -> p k c", p=128)
        w_dn_v = ffn_w_down[0].rearrange("(i p) d -> p i d", p=128)
        for g in range(22):
            f0 = g * 256
            dst = wup_t[g].rearrange("p (k gv c) -> p k gv c", gv=2, c=256)
            for gv in range(2):
                cast_jobs.append((dst[:, :, gv, :], w_up_v[:, :, gv * DFF + f0:gv * DFF + f0 + 256]))
        for hf in range(2):
            for dg in range(4):
                for j in range(2):
                    blk = (hf * 4 + dg) * 2 + j
                    i0 = hf * 22 + j * 11
                    dst = wdn_t[blk].rearrange("p (i d) -> p i d", d=512)
                    cast_jobs.append((dst, w_dn_v[:, i0:i0 + 11, dg * 512:(dg + 1) * 512]))

    def issue_cast(n=1):
        for _ in range(n):
            if cast_jobs:
                dst, src = cast_jobs.pop(0)
                S.op("pool", lambda e, dst=dst, src=src: e.dma_start(out=dst, in_=src), (), (), dma=True, bg=True)

    with ExitStack() as G:
        sems = {e: G.enter_context(nc.semaphore("s_" + e)) for e in ENGS}
        dsems = {e: [G.enter_context(nc.semaphore(f"d_{e}_{i}")) for i in range(NDMA)] for e in ENGS}
        barsem = G.enter_context(nc.semaphore("barsem"))
        bgsems = [G.enter_context(nc.semaphore(f"bg_{i}")) for i in range(NBG)]
        G.enter_context(nc.allow_non_contiguous_dma(reason="small strided parameter loads"))

        phase_no = [0]

        def emit():
            S.emit(nc, block, sems, dsems, barsem, bgsems)
            phase_no[0] += 1
            if phase_no[0] > upto:
                stopped[0] = True
                _Lenient.truncated = True
                raise _Stop()

        stopped = [False]

        def stop_check():
            if stopped[0]:
                raise _Stop()

        ident_f = sb(None, "ident_f", [128, 128], F32)
        ident_b = sb(None, "ident_b", [128, 128], BF16)
        ones_b = sb(None, "ones_b", [128, 128], BF16)
        flags = sb(None, "flags_sb", [128, 16], F32)
        one1 = sb(None, "one1", [128, 1], F32)
        epsb = sb(None, "epsb", [128, 1], F32)
        negh = sb(None, "negh", [128, 1], F32)
        modF = sb(None, "modF", [128, 96], F32)
        bmodF = sb(None, "bmodF", [128, 96], F32)
        nw1F = sb(None, "nw1F", [128, 16], F32)
        nw2F = sb(None, "nw2F", [128, 16], F32)
        a1 = sb(None, "a1", [128, 16], F32)
        a2 = sb(None, "a2", [128, 16], F32)
        block = G.enter_context(nc.Block())

        with _Phase() as P:
            stop_check()
            wp = [sb(P, f"wp{i}", [128, 1536], F32) for i in range(4)]
            modrow = sb(P, "modrow", [1, 1536], F32)
            zf = sb(P, "zf", [128, 64], F32)
            c_sb = sb(P, "c_sb", [128, 16], F32)
            cE = sb(P, "cE", [128, 16], F32)
            scv = sb(P, "scv", [128, 16], F32)
            psm = [ps(P, f"psm{n}", [128, 512], F32) for n in range(3)]

            DMA("sp", ident_f[:], cst[:, C_ID:C_ID + 128], (), ["ident_f"])
            DMA("pool", ident_b[:], cst[:, C_ID:C_ID + 128], (), ["ident_b"])
            DMA("sp", flags[:], flags_d[:, :], (), ["flags"])
            MEMSET("pool", ones_b[:], 1.0, ["ones_b"])
            MEMSET("pool", one1[:], 1.0, ["one1"])
            MEMSET("pool", epsb[:], EPS, ["epsb"])
            MEMSET("pool", negh[:], -0.5, ["negh"])
            MEMSET("pool", zf[:], 0.0, ["zf"])
            DMA("act", fence_in.ap(), zf[:], ["zf"], ["fence_in"])
            DMA("sp", c_sb[:], c_in.rearrange("o (k p) -> p (o k)", p=128), (), ["c_sb"])
            ACT(cE[:], c_sb[:], AF.Exp, ["c_sb"], ["cE"], scale=-1.0)
            TS("dve", cE[:], cE[:], 1.0, None, ALU.add, None, ["cE"], ["cE"])
            RECIP(cE[:], cE[:], ["cE"], ["cE"])
            TT("dve", scv[:], c_sb[:], cE[:], ALU.mult, ["c_sb", "cE"], ["scv"])
            for k in range(16):
                b = k % 4
                DMA("sp" if k % 2 else "act", wp[b][:], w_mod_c[k * 128:(k + 1) * 128, :], (), [("wp", b)])
                for n in range(3):
                    MM(psm[n][0:1, :], scv[:, k:k + 1], wp[b][:, n * 512:(n + 1) * 512], k == 0, k == 15,
                       ["scv", ("wp", b)], [("psm", n)])
            for n in range(3):
                CP("act" if n % 2 else "dve", modrow[0:1, n * 512:(n + 1) * 512], psm[n][0:1, :], [("psm", n)], [("mr", n)])
            DMA("sp", agm_in.ap().rearrange("p w -> (p w)")[0:1536].rearrange("(o n) -> o n", o=1), modrow[0:1, :],
                [("mr", n) for n in range(3)], ["agm_in"])

            def ccm(e):
                return e.collective_compute("AllGather", ALU.bypass, replica_groups=[list(range(NCORES))],
                                            ins=[agm_in.ap().opt()], outs=[agm_out.ap().opt()])
            S.op("pool", ccm, ["agm_in"], ["agm_out"])

            def ccmf(e):
                return e.collective_compute("AllReduce", ALU.add, replica_groups=[list(range(NCORES))],
                                            ins=[fence_in.ap().opt()], outs=[fence_out0.ap().opt()])
            S.op("pool", ccmf, ["agm_out", "fence_in"], ["fence0"])
            DMA("sp", mod_d[0:1, :].rearrange("o (r n) -> (o r) n", r=NCORES),
                agm_out.ap().rearrange("(r p) w -> r (p w)", p=128)[:, 0:1536], ["agm_out", "fence0"], ["mod_d"])
            for s in range(6):
                DMA("sp", modF[:, s * 16:(s + 1) * 16],
                    mod_d[0:1, s * 2048:(s + 1) * 2048].rearrange("o (k p) -> p (o k)", p=128), ["mod_d"], ["modF"])
                DMA("act", bmodF[:, s * 16:(s + 1) * 16],
                    b_mod[0:1, s * 2048:(s + 1) * 2048].rearrange("o (k p) -> p (o k)", p=128), (), ["bmodF"])
            DMA("sp", nw1F[:], norm1_w.rearrange("o (k p) -> p (o k)", p=128), (), ["nw1F"])
            DMA("sp", nw2F[:], norm2_w.rearrange("o (k p) -> p (o k)", p=128), (), ["nw2F"])
            TT("dve", modF[:], modF[:], bmodF[:], ALU.add, ["modF", "bmodF"], ["modF"])
            STT(a1[:], modF[:, 16:32], 1.0, nw1F[:], ALU.add, ALU.mult, ["modF", "nw1F"], ["a1"])
            STT(a2[:], modF[:, 64:80], 1.0, nw2F[:], ALU.add, ALU.mult, ["modF", "nw2F"], ["a2"])
            if dbg:
                DMA("sp", dbg_out("modF", [128, 96])[:, :], modF[:], ["modF"], ["dbg0"])
                DMA("sp", dbg_out("a1", [128, 16])[:, :], a1[:], ["a1"], ["dbg1"])
                DMA("sp", dbg_out("a2", [128, 16])[:, :], a2[:], ["a2"], ["dbg2"])
            emit()
        shift1 = modF[:, 0:16]
        shift2 = modF[:, 48:64]

        def norm_to_T(P, tag, nsub, src_loader, avec, shvec, dstT, halo=False, flagmul=False):
            np_ = 3 if halo else 128
            junk = P["junk"]
            ss = P["ss"]
            xn = P["xn"]
            tp = P["tp"]
            for j in range(nsub):
                src, rk = src_loader(j)
                ACT(junk[0:np_, :], src, AF.Square, rk, [("ss", j)], accum_out=ss[0:np_, j:j + 1])
                TS("pool", ss[0:np_, j:j + 1], ss[0:np_, j:j + 1], 1.0 / D, EPS, ALU.mult, ALU.add, [("ss", j)], [("ss", j)])
                TT("pool", ss[0:np_, j:j + 1], ss[0:np_, j:j + 1], negh[0:np_, :], ALU.pow, [("ss", j), "negh"], [("ss", j)])
                ACT(xn[0:np_, j, :], src, AF.Copy, rk + [("ss", j)], [("xn", j)], scale=ss[0:np_, j:j + 1])
            ncol = nsub * np_ if not halo else 3
            for dk in range(16):
                tpb = tp[dk % 2]
                for j in range(nsub):
                    TR(tpb[:, j * np_:(j + 1) * np_] if not halo else tpb[:, 0:3],
                       xn[0:np_, j, dk * 128:(dk + 1) * 128],
                       ident_b[0:np_, 0:np_], [("xn", j), "ident_b"], [("tp", dk % 2)])
                if dk % 2 == 0:
                    ACT(dstT[:, dk, 0:ncol], tpb[:, 0:ncol], AF.Identity, [("tp", dk % 2), "a1", "a2", "modF"], [(tag, dk)],
                        bias=shvec[:, dk:dk + 1], scale=avec[:, dk:dk + 1])
                else:
                    TS("dve", dstT[:, dk, 0:ncol], tpb[:, 0:ncol], avec[:, dk:dk + 1], shvec[:, dk:dk + 1], ALU.mult, ALU.add,
                       [("tp", dk % 2), "a1", "a2", "modF"], [(tag, dk)])
                if flagmul:
                    TS("dve", dstT[:, dk, 0:ncol], dstT[:, dk, 0:ncol], flags[:, 0:1], None, ALU.mult, None,
                       [(tag, dk), "flags"], [(tag, dk)])

        with _Phase() as P:
            stop_check()
            bufs = {
                "junk": sb(P, "junk", [128, 2048], BF16),
                "ss": sb(P, "ss", [128, 4], F32),
                "xn": sb(P, "xn", [128, 4, 2048], BF16),
                "tp": [ps(P, f"tp{i}", [128, 512], BF16) for i in range(2)],
            }
            xt = [sb(P, f"xt{i}", [128, 2048], F32) for i in range(4)]
            hTt = [sb(P, f"hTt{i}", [128, 16, 512], BF16) for i in range(2)]
            hTh = sb(P, "hTh", [128, 16, 4], BF16)
            DMA("sp", xt[0][0:3, :], x_c[0:3, :], (), [("xt", 0)])
            norm_to_T(bufs, "hTh", 1, lambda j: (xt[0][0:3, :], [("xt", 0)]), a1, shift1, hTh, halo=True, flagmul=True)
            DMA("act", hT_d[:, :, 0:3], hTh[:, :, 0:3], [("hTh", dk) for dk in range(16)], ["hT_d"])
            for ti in range(NT):
                def loader(j, ti=ti):
                    return xt[j][:], [("xt", j)]
                for j in range(4):
                    r0 = 3 + ti * TA + j * 128
                    DMA("sp", xt[j][:], x_c[r0:r0 + 128, :], (), [("xt", j)])
                hb = hTt[ti % 2]
                norm_to_T(bufs, ("hTt", ti % 2), 4, loader, a1, shift1, hb)
                DMA("act", hT_d[:, :, 3 + ti * TA:3 + (ti + 1) * TA], hb[:], [(("hTt", ti % 2), dk) for dk in range(16)], ["hT_d"])
            if dbg:
                DMA("sp", dbg_out("hT", [128, 16, 3 + T], BF16)[:, :, :], hT_d, ["hT_d"], ["dbg0"])
            emit()

        w_in_v = w_in[0].rearrange("(k p) c -> p k c", p=128)

        with _Lenient() as M:
            lbv = sb(M, "lbv", [128, 8], F32)
            oml = sb(M, "oml", [128, 8], F32)
            gnw = sb(M, "gnw", [128, 8], F32)
            S_hg = sb(M, "S_hg", [128, 8, 128], F32)
            Bprev = sb(M, "Bprev", [128, 8], F32)
            Gall = sb(M, "Gall", [128, 8, 32], F32)
            Hs = sb(M, "Hs", [128, 1024], F32)
            acumG = sb(M, "acumG", [16, T], F32)
            CFall = sb(M, "CFall", [128, 4, T], BF16)
            Sin_hb = sb(M, "Sin_hb", [128, 8, 128], BF16)
            Sin_sb = sb(M, "Sin_sb", [128, 1024], BF16)
            snw = sb(M, "snw", [128, 8], F32)
            scanmask = sb(M, "scanmask", [128, 512], F32)

            with _Phase() as P:
                stop_check()
                lbr = sb(P, "lbr", [128, 2, 8], F32)
                hTt = [sb(P, f"hTt{i}", [128, 16, 512], BF16) for i in range(2)]
                wb = [sb(P, f"wb{i}", [128, 16, 256], BF16) for i in range(3)]
                qs_all = sb(P, "qs_all", [128, 8, 512], BF16)
                qt_all = sb(P, "qt_all", [128, 8, 512], BF16)
                kt_all = sb(P, "kt_all", [128, 8, 512], BF16)
                t0 = [sb(P, f"t0_{i}", [128, 512], F32) for i in range(2)]
                tE = sb(P, "tE", [128, 512], F32)
                tL1 = sb(P, "tL1", [128, 512], F32)
                tL2 = sb(P, "tL2", [128, 512], F32)
                tb = sb(P, "tb", [128, 512], F32)
                tq = sb(P, "tq", [128, 512], F32)
                tk = sb(P, "tk", [128, 512], F32)
                vF = [sb(P, f"vF{i}", [128, 512], BF16) for i in range(2)]
                kT_sb = sb(P, "kT_sb", [64, 1024], BF16)
                vT_sb = sb(P, "vT_sb", [64, 1024], BF16)
                attm = sb(P, "attm", [64, 512], F32)
                attT_sb = sb(P, "attT_sb", [64, 512], BF16)
                tmpU = sb(P, "tmpU", [128, 128], F32)
                Sp = sb(P, "Sp", [128, 128], BF16)
                o_sb = [sb(P, f"o_sb{i}", [128, 512], BF16) for i in range(2)]
                sgt = [sb(P, f"sgt{i}", [128, 512], BF16) for i in range(2)]
                sc_em = sb(P, "sc_em", [128, 8, 8], F32)
                sc_c1 = sb(P, "sc_c1", [128, 8, 8], F32)
                sc_c2 = sb(P, "sc_c2", [128, 8, 8], F32)
                bcum = sb(P, "bcum", [128, 8], F32)
                gex = sb(P, "gex", [128, 8], F32)
                ones8 = sb(P, "ones8", [128, 8], F32)
                pin = [ps(P, f"pin{i}", [128, 512], F32) for i in range(2)]
                kTp = ps(P, "kTp", [128, 1024], BF16)
                vTp = ps(P, "vTp", [128, 1024], BF16)
                attp = ps(P, "attp", [128, 512], F32)
                Up = ps(P, "Up", [128, 512], F32)
                op_ = ps(P, "op_", [128, 512], F32)

                DMA("sp", lbr[:], hgrn_lb.rearrange("r (h p) -> p r h", p=128), (), ["lbr"])
                DMA("sp", gnw[:], hgrn_gnw.rearrange("o (h p) -> p (o h)", p=128), (), ["gnw"])
                DMA("sp", scanmask[:], cst[:, C_SCAN:C_SCAN + 512], (), ["scanmask"])
                DMA("sp", attm[:], cst[0:64, C_ATT:C_ATT + 512], (), ["attm"])
                TT("dve", lbv[:], lbr[:, 1, :], lbr[:, 0, :], ALU.subtract, ["lbr"], ["lbv"])
                ACT(oml[:], lbv[:], AF.Exp, ["lbv"], ["oml"])
                TS("dve", lbv[:], oml[:], 1.0, None, ALU.add, None, ["oml"], ["lbv"])
                RECIP(lbv[:], lbv[:], ["lbv"], ["lbv"])
                TT("dve", oml[:], oml[:], lbv[:], ALU.mult, ["oml", "lbv"], ["oml"])
                MEMSET("pool", S_hg[:], 0.0, [("S_hg", h) for h in range(8)])
                MEMSET("pool", Bprev[:], 0.0, ["Bprev"])
                MEMSET("pool", ones8[:], 1.0, ["ones8"])

                wcnt = [0]
                make_cast_jobs()

                def head_pipeline(ti, h):
                    tcol = ti * TA
                    qt = qt_all[:, h, :]
                    kt = kt_all[:, h, :]
                    vf = vF[h % 2]
                    for c in range(8):
                        TR(kTp[0:64, c * 128:(c + 1) * 128], kt[:, c * 64:(c + 1) * 64], ident_b[:], [("kt", h), "ident_b"], ["kTp"])
                    for c in range(8):
                        TR(vTp[0:64, c * 128:(c + 1) * 128], vf[:, c * 64:(c + 1) * 64], ident_b[:], [("vF", h % 2), "ident_b"], ["vTp"])
                    CP("act", kT_sb[:], kTp[0:64, :], ["kTp"], ["kT_sb"])
                    CP("dve", vT_sb[:], vTp[0:64, :], ["vTp"], ["vT_sb"])
                    for c in range(8):
                        MM(attp[0:64, c * 64:(c + 1) * 64], kt[:, c * 64:(c + 1) * 64], qt[:, c * 64:(c + 1) * 64], True, True,
                           [("kt", h), ("qt", h)], ["attp"])
                    TT("dve", attT_sb[:], attp[0:64, :], attm[:], ALU.mult, ["attp", "attm"], ["attT_sb"])
                    ob = o_sb[h % 2]
                    for c in range(8):
                        ACT(Sp[:], S_hg[:, h, :], AF.Copy, [("S_hg", h), ("sc", h)], ["Sp"], scale=sc_em[:, h, c:c + 1])
                        MM(op_[:, c * 64:(c + 1) * 64], vT_sb[:, c * 128:(c + 1) * 128], attT_sb[:, c * 64:(c + 1) * 64], True, False,
                           ["vT_sb", "attT_sb"], [("op", c)])
                        MM(op_[:, c * 64:(c + 1) * 64], Sp[:], qt[:, c * 64:(c + 1) * 64], False, True,
                           ["Sp", ("qt", h)], [("op", c)])
                        us = Up[:, (c % 4) * 128:(c % 4 + 1) * 128]
                        MM(us, kT_sb[:, c * 128:(c + 1) * 128], vT_sb[:, c * 128:(c + 1) * 128], True, True,
                           ["kT_sb", "vT_sb"], [("Up", c % 4)])
                        TS("dve", tmpU[:], us, sc_c2[:, h, c:c + 1], None, ALU.mult, None, [("Up", c % 4), ("sc", h)], ["tmpU"])
                        STT(S_hg[:, h, :], S_hg[:, h, :], sc_c1[:, h, c:c + 1], tmpU[:], ALU.mult, ALU.add,
                            [("S_hg", h), "tmpU", ("sc", h)], [("S_hg", h)])
                    CP("act", ob[:], op_[:], [("op", c) for c in range(8)], [("o_sb", h % 2)])
                    DMA("sp", ho_d[:, h, tcol:tcol + TA], ob[:], [("o_sb", h % 2)], ["ho_d"])
                    DMA("sp", hq_d[:, h, tcol:tcol + TA], qt, [("qt", h)], ["hq_d"])

                for ti in range(NT):
                    hb = hTt[ti % 2]
                    DMA("sp", hb[:], hT_d[:, :, 3 + ti * TA:3 + (ti + 1) * TA], (), [("hTt", ti % 2)])
                    for gi in range(16):
                        wi = wcnt[0] % 3
                        wcnt[0] += 1
                        DMA("pool", wb[wi][:], w_in_v[:, :, gi * 256:(gi + 1) * 256], (), [("wb", wi)])
                        issue_cast(1)
                        for half in range(2):
                            cc = gi * 2 + half
                            kind, h = cc // 8, cc % 8
                            pp = pin[cc % 2]
                            pk = ("pin", cc % 2)
                            for k in range(16):
                                MM(pp[:], wb[wi][:, k, half * 128:(half + 1) * 128], hb[:, k, :], k == 0, k == 15,
                                   [("wb", wi), ("hTt", ti % 2)], [pk])
                            if kind == 0 or kind == 3:
                                tt = t0[cc % 2]
                                tk0 = ("t0", cc % 2)
                                ACT(tt[:], pp[:], AF.Exp, [pk], [tk0], scale=-1.0)
                                ACT(tt[:], tt[:], AF.Ln, [tk0], [tk0], bias=one1[:])
                                ACT(tt[:], tt[:], AF.Exp, [tk0], [tk0], scale=-1.0)
                                if kind == 0:
                                    TT("dve", qs_all[:, h, :], pp[:], tt[:], ALU.mult, [pk, tk0], [("qs", h)])
                                else:
                                    sg = sgt[h % 2]
                                    STT(sg[:], pp[:], gnw[:, h:h + 1], tt[:], ALU.mult, ALU.mult, [pk, tk0, "gnw"], [("sgt", h % 2)])
                                    DMA("act", hg_d[:, h, ti * TA:(ti + 1) * TA], sg[:], [("sgt", h % 2)], ["hg_d"])
                            elif kind == 1:
                                ACT(tE[:], pp[:], AF.Exp, [pk], ["tE"], scale=-1.0)
                                ACT(tL2[:], tE[:], AF.Ln, ["tE"], ["tL2"], bias=one1[:])
                                ACT(tL1[:], tE[:], AF.Ln, ["tE", "lbv"], ["tL1"], bias=one1[:], scale=lbv[:, h:h + 1])
                                TT("pool", tL1[:], tL1[:], tL2[:], ALU.subtract, ["tL1", "tL2"], ["tL1"])
                                SCAN(tb[:], scanmask[:], tL1[:], 0.0, ["scanmask", "tL1"], ["tb"])
                                tb3 = tb[:].rearrange("p (c t) -> p c t", t=64)
                                TT("dve", tL1[:].rearrange("p (c t) -> p c t", t=64), tb3,
                                   tb3[:, :, 31:32].to_broadcast([128, 8, 64]), ALU.subtract, ["tb"], ["tL1"])
                                ACT(tq[:], tL1[:], AF.Exp, ["tL1"], ["tq"])
                                ACT(tk[:], tL1[:], AF.Exp, ["tL1"], ["tk"], scale=-1.0)
                                ACT(tL2[:], tL2[:], AF.Exp, ["tL2"], ["tL2"], scale=-1.0)
                                STT(tE[:], tE[:], oml[:, h:h + 1], tL2[:], ALU.mult, ALU.mult, ["tE", "tL2", "oml"], ["tE"])
                                TT("dve", qt_all[:, h, :], qs_all[:, h, :], tq[:], ALU.mult, [("qs", h), "tq"], [("qt", h)])
                                TT("pool", kt_all[:, h, :], tE[:], tk[:], ALU.mult, ["tE", "tk"], [("kt", h)])
                                m_v = tb3[:, :, 31]
                                bl_v = tb3[:, :, 63]
                                ACT(sc_em[:, h, :], m_v, AF.Exp, ["tb"], [("sc", h)])
                                ACT(sc_c1[:, h, :], bl_v, AF.Exp, ["tb"], [("sc", h)])
                                TT("pool", gex[:], bl_v, m_v, ALU.subtract, ["tb"], ["gex"])
                                ACT(sc_c2[:, h, :], gex[:], AF.Exp, ["gex"], [("sc", h)])
                                SCAN(bcum[:], ones8[:], bl_v, Bprev[:, h:h + 1], ["ones8", "tb", ("Bprev", h), "Bprev"], ["bcum"])
                                TT("pool", gex[:], bcum[:], bl_v, ALU.subtract, ["bcum", "tb"], ["gex"])
                                TT("pool", gex[:], gex[:], m_v, ALU.add, ["gex", "tb"], ["gex"])
                                ACT(Gall[:, h, ti * 8:(ti + 1) * 8], gex[:], AF.Exp, ["gex"], ["Gall"])
                                CP("pool", Bprev[:, h:h + 1], bcum[:, 7:8], ["bcum"], [("Bprev", h)])
                            else:
                                CP("act", vF[h % 2][:], pp[:], [pk], [("vF", h % 2)])
                                head_pipeline(ti, h)
                if dbg:
                    DMA("sp", dbg_out("ho", [128, 8, T], BF16)[:, :, :], ho_d, ["ho_d"], ["dbg0"])
                    DMA("sp", dbg_out("hq", [128, 8, T], BF16)[:, :, :], hq_d, ["hq_d"], ["dbg1"])
                    DMA("sp", dbg_out("hg", [128, 8, T], BF16)[:, :, :], hg_d, ["hg_d"], ["dbg2"])
                    DMA("sp", dbg_out("S_hg", [128, 1024])[:, :], S_hg[:].rearrange("p h v -> p (h v)"), [("S_hg", h) for h in range(8)], ["dbg3"])
                    DMA("sp", dbg_out("Gall", [128, 256])[:, :], Gall[:].rearrange("p h c -> p (h c)"), ["Gall"], ["dbg4"])
                    DMA("sp", dbg_out("Bprev", [128, 8])[:, :], Bprev[:], [("Bprev", h) for h in range(8)], ["dbg5"])
                emit()

            with _Phase() as P:
                stop_check()
                hTt = [sb(P, f"hTt{i}", [128, 16, 512], BF16) for i in range(2)]
                hTh = sb(P, "hTh", [128, 16, 4], BF16)
                wb = [sb(P, f"wb{i}", [128, 16, 256], BF16) for i in range(3)]
                wdt = sb(P, "wdt", [128, 16, 16], BF16)
                convw = sb(P, "convw", [128, 16, 4], F32)
                convb = sb(P, "convb", [128, 16], F32)
                dtb = sb(P, "dtb", [16, 1], F32)
                a_h = sb(P, "a_h", [16, 1], F32)
                dB = sb(P, "dB", [128, 16], F32)
                dgm = sb(P, "dgm", [128, 1024], F32)
                blockm = sb(P, "blockm", [16, 1024], F32)
                ones16 = sb(P, "ones16", [16, 128], F32)
                rhs_m = sb(P, "rhs_m", [96, 1024], F32)
                lhsT_m = sb(P, "lhsT_m", [96, 64], F32)
                xhalo = sb(P, "xhalo", [128, 16, 3], F32)
                xr = [sb(P, f"xr{i}", [128, 515], F32) for i in range(2)]
                acc = [sb(P, f"acc{i}", [128, 512], F32) for i in range(2)]
                t0 = [sb(P, f"t0_{i}", [128, 512], F32) for i in range(2)]
                szt = [sb(P, f"szt{i}", [128, 512], BF16) for i in range(2)]
                XF = sb(P, "XF", [128, 8, 512], BF16)
                BFm = sb(P, "BFm", [128, 4, 512], BF16)
                dtE = sb(P, "dtE", [16, 512], F32)
                dtv = sb(P, "dtv", [16, 512], F32)
                dAv = sb(P, "dAv", [16, 512], F32)
                acum = sb(P, "acum", [16, 512], F32)
                ones512 = sb(P, "ones512", [16, 512], F32)
                acT = sb(P, "acT", [64, 16], F32)
                dtx = sb(P, "dtx", [128, 16], F32)
                edt = sb(P, "edt", [64, 16], F32)
                eatot2 = [sb(P, f"eatot{i}", [128, 16], F32) for i in range(2)]
                eatot = eatot2[0]
                diff = sb(P, "diff", [64, 1024], F32)
                Maug2 = [sb(P, f"Maug{i}", [128, 1024], BF16) for i in range(2)]
                xaug2 = [sb(P, f"xaug{i}", [128, 1024], BF16) for i in range(2)]
                xdec2 = [sb(P, f"xdec{i}", [64, 1024], BF16) for i in range(2)]
                btok2 = [sb(P, f"btok{i}", [64, 512], BF16) for i in range(2)]
                Maug, xaug, xdec, btok = Maug2[0], xaug2[0], xdec2[0], btok2[0]
                eR = sb(P, "eR", [128, 1024], BF16)
                Ct2 = [sb(P, f"Ct{i}", [128, 1024], BF16) for i in range(2)]
                Ct = Ct2[0]
                Hb = sb(P, "Hb", [128, 1024], BF16)
                y_sb = [sb(P, f"y_sb{i}", [128, 8, 512], BF16) for i in range(2)]
                pinU = ps(P, "pinU", [128, 1024], F32)
                repm = ps(P, "repm", [128, 512], F32)
                rep1 = ps(P, "rep1", [128, 512], F32)
                xTp = ps(P, "xTp", [128, 1024], BF16)
                yp = ps(P, "yp", [128, 512], F32)
                psml = ps(P, "psml", [128, 512], F32)
                bTp = ps(P, "bTp", [128, 1024], BF16)
                pin = [pinU[:, 0:512], pinU[:, 512:1024]]

                SETUPN = int(os.environ.get("K_A3_SETUP", 999))
                if SETUPN > 0:
                    for j in range(4):
                        DMA("sp", convw[:, :, j], ssd_conv_w[0, j:j + 1, :].rearrange("o (c p) -> p (o c)", p=128), (), ["convw"])
                if SETUPN > 1:
                    DMA("sp", convb[:], ssd_conv_b.rearrange("o (c p) -> p (o c)", p=128), (), ["convb"])
                if SETUPN > 2:
                    DMA("sp", dtb[:], ssd_dt_bias.rearrange("o h -> h o"), (), ["dtb"])
                if SETUPN > 3:
                    DMA("sp", a_h[:], ssd_a_log.rearrange("o h -> h o"), (), ["a_h"])
                if SETUPN > 4:
                    ACT(a_h[:], a_h[:], AF.Exp, ["a_h"], ["a_h"])
                if SETUPN > 5:
                    TS("dve", a_h[:], a_h[:], -1.0, None, ALU.mult, None, ["a_h"], ["a_h"])
                if SETUPN > 6:
                    DMA("sp", dB[:], ssd_d[0:1, :].partition_broadcast(128), (), ["dB"])
                if SETUPN > 7:
                    DMA("sp", dgm[64:128, :], cst[0:64, C_DIAG:C_DIAG + 1024], (), ["dgm"])
                if SETUPN > 8:
                    DMA("sp", blockm[:], cst[0:16, C_BLK:C_BLK + 1024], (), ["blockm"])
                if SETUPN > 9:
                    MEMSET("pool", ones16[:], 1.0, ["ones16"])
                if SETUPN > 10:
                    MEMSET("pool", ones512[:], 1.0, ["ones512"])
                if SETUPN > 11:
                    MEMSET("pool", rhs_m[:], 0.0, ["rhs_m"])
                if SETUPN > 12:
                    DMA("sp", rhs_m[32:96, :], cst[0:64, C_NEG:C_NEG + 1024], ["rhs_m"], ["rhs_m"])
                if SETUPN > 13:
                    DMA("sp", lhsT_m[:], cst[0:96, C_LM:C_LM + 64], (), ["lhsT_m"])
                if SETUPN > 14:
                    DMA("pool", wdt[:], w_in_v[:, :, 7168:7184], (), ["wdt"])
                if SETUPN > 15:
                    DMA("sp", hTh[:, :, 0:3], hT_d[:, :, 0:3], (), ["hTh"])
                if SETUPN > 16:
                    DMA("sp", snw[:], ssd_norm_w.rearrange("o (h p) -> p (o h)", p=128), (), ["snw"])
                if SETUPN > 17:
                    for Mq in Maug2:
                        TT("dve", Mq[64:128, :].rearrange("p (h t) -> p h t", t=64), dgm[64:128, :].rearrange("p (h t) -> p h t", t=64),
                           bc_last(dB[64:128, :], 64), ALU.mult, ["dgm", "dB"], ["MaugC"])
                if SETUPN > 18:
                    MEMSET("pool", Hs[:], 0.0, ["Hs"])
                if SETUPN > 19:
                    MEMSET("pool", Hb[:], 0.0, ["Hb"])
                if SETUPN > 20:
                    MEMSET("pool", dtx[64:128, :], 1.0, ["dtxC"])


                wcnt = [0]
                A3_TILES = int(os.environ.get("K_A3_TILES", NT))
                A3_CHUNKS = int(os.environ.get("K_A3_CHUNKS", 8))
                A3_STAGE = int(os.environ.get("K_A3_STAGE", 99))

                def silu_evac(src, srck, tt, ttk, dst, dstk, eng="dve"):
                    ACT(tt, src, AF.Exp, [srck], [ttk], scale=-1.0)
                    ACT(tt, tt, AF.Ln, [ttk], [ttk], bias=one1[:])
                    ACT(tt, tt, AF.Exp, [ttk], [ttk], scale=-1.0)
                    TT(eng, dst, src, tt, ALU.mult, [srck, ttk], [dstk])

                for ti in range(A3_TILES):
                    tcol = ti * TA
                    hb = hTt[ti % 2]
                    DMA("sp", hb[:], hT_d[:, :, 3 + tcol:3 + tcol + TA], (), [("hTt", ti % 2)])
                    ys = y_sb[ti % 2]
                    for gi in range(16, 28):
                        wi = wcnt[0] % 3
                        wcnt[0] += 1
                        DMA("pool", wb[wi][:], w_in_v[:, :, gi * 256:(gi + 1) * 256], (), [("wb", wi)])
                        issue_cast(1)
                        for half in range(2):
                            cc = gi * 2 + half
                            pp = pin[cc % 2]
                            pk = ("pin", cc % 2)
                            for k in range(16):
                                MM(pp, wb[wi][:, k, half * 128:(half + 1) * 128], hb[:, k, :], k == 0, k == 15,
                                   [("wb", wi), ("hTt", ti % 2)], [pk])
                            if cc < 40:
                                j = cc - 32
                                silu_evac(pp, pk, t0[cc % 2][:], ("t0", cc % 2), szt[cc % 2][:], ("szt", cc % 2))
                                DMA("act", sz_d[:, j, tcol:tcol + TA], szt[cc % 2][:], [("szt", cc % 2)], ["sz_d"])
                            else:
                                j = cc - 40
                                xx = xr[j % 2]
                                xk = ("xr", j % 2)
                                if ti == 0:
                                    for k in range(16):
                                        MM(psml[:, 400:403], wb[wi][:, k, half * 128:(half + 1) * 128], hTh[:, k, 0:3], k == 0, k == 15,
                                           [("wb", wi), "hTh"], ["psml_h"])
                                    CP("act", xx[:, 0:3], psml[:, 400:403], ["psml_h"], [xk])
                                else:
                                    CP("pool", xx[:, 0:3], xhalo[:, j, :], [("xhalo", j)], [xk])
                                ACT(xx[:, 3:515], pp, AF.Copy, [pk, xk], [xk])
                                CP("pool", xhalo[:, j, :], xx[:, 512:515], [xk], [("xhalo", j)])
                                ac = acc[j % 2]
                                ak = ("acc", j % 2)
                                TS("dve", ac[:], xx[:, 3:515], convw[:, j, 3:4], convb[:, j:j + 1], ALU.mult, ALU.add, [xk, "convw", "convb"], [ak])
                                for tap in (2, 1, 0):
                                    STT(ac[:], xx[:, tap:tap + 512], convw[:, j, tap:tap + 1], ac[:], ALU.mult, ALU.add, [xk, ak, "convw"], [ak])
                                if j < 8:
                                    dst, dk_ = XF[:, j, :], ("XF", j)
                                elif j < 12:
                                    dst, dk_ = BFm[:, j - 8, :], ("BF", j - 8)
                                else:
                                    dst, dk_ = CFall[:, j - 12, tcol:tcol + TA], ("CF", j - 12)
                                silu_evac(ac[:], ak, t0[cc % 2][:], ("t0", cc % 2), dst, dk_, eng="pool")
                    for k in range(16):
                        MM(repm[0:16, :], wdt[:, k, :], hb[:, k, :], k == 0, k == 15, ["wdt", ("hTt", ti % 2)], ["repm"])
                    ACT(dtE[:], repm[0:16, :], AF.Exp, ["repm", "dtb"], ["dtE"], bias=dtb[:])
                    ACT(dtv[:], dtE[:], AF.Ln, ["dtE"], ["dtv"], bias=one1[0:16, :])
                    TS("dve", dAv[:], dtv[:], a_h[:, 0:1], None, ALU.mult, None, ["dtv", "a_h"], ["dAv"])
                    SCAN(acum[:], scanmask[0:16, :], dAv[:], 0.0, ["scanmask", "dAv"], ["acum"])
                    SCAN(acumG[:, tcol:tcol + TA], ones512[:], dAv[:], 0.0 if ti == 0 else acumG[:, tcol - 1:tcol],
                         ["ones512", "dAv", "acumG"], ["acumG"])
                    def front(c, tcol=tcol):
                        cs = c * 64
                        pb = c % 2
                        Mg, xa, Cc, xd, bt, ea = Maug2[pb], xaug2[pb], Ct2[pb], xdec2[pb], btok2[pb], eatot2[pb]
                        TT("pool", rhs_m[0:16, :].rearrange("p (h t) -> p h t", t=64),
                           blockm[:].rearrange("p (h t) -> p h t", t=64),
                           acum[:, cs:cs + 64].unsqueeze(1).to_broadcast([16, 16, 64]), ALU.mult,
                           ["blockm", "acum"], ["Dg"])
                        MM(repm[0:64, 0:16], acum[:, cs:cs + 64], ident_f[0:16, 0:16], True, True, ["acum", "ident_f"], ["repm"])
                        MM(rep1[0:64, 0:16], dtv[:, cs:cs + 64], ident_f[0:16, 0:16], True, True, ["dtv", "ident_f"], ["rep1"])
                        CP("act", acT[:], repm[0:64, 0:16], ["repm"], ["acT"])
                        CP("act", dtx[0:64, :], rep1[0:64, 0:16], ["rep1"], ["dtx"])
                        for g in range(4):
                            MM(psml[0:64, 64 + g * 64:64 + (g + 1) * 64], BFm[:, g, cs:cs + 64], CFall[:, g, tcol + cs:tcol + cs + 64], True, True,
                               [("BF", g), ("CF", g)], ["psml_cb"])
                        for j in range(8):
                            TR(xTp[0:64, j * 128:(j + 1) * 128], XF[:, j, cs:cs + 64], ident_b[:], [("XF", j), "ident_b"], ["xTp"])
                            TR(xTp[64:128, j * 128:(j + 1) * 128], XF[:, j, cs:cs + 64], ident_b[:], [("XF", j), "ident_b"], ["xTp"])
                        TT("dve", xa[:].rearrange("p (h q) -> p h q", q=64), xTp[:].rearrange("p (h q) -> p h q", q=64),
                           bc_last(dtx[:], 64), ALU.mult, ["xTp", "dtx", "dtxC"], [("xaug", pb)])
                        for g in range(4):
                            TR(bTp[0:64, g * 128:(g + 1) * 128], BFm[:, g, cs:cs + 64], ident_b[:], [("BF", g), "ident_b"], ["bTp"])
                        CP("act", bt[:], bTp[0:64, 0:512], ["bTp"], [("btok", pb)])
                        for hf in range(2):
                            hs = slice(hf * 512, (hf + 1) * 512)
                            MM(repm[0:64, :], lhsT_m[:], rhs_m[:, hs], True, True, ["lhsT_m", "rhs_m", "Dg"], ["repm"])
                            MM(rep1[:], ones16[:], rhs_m[0:16, hs], True, True, ["ones16", "Dg"], ["rep1"])
                            r3 = repm[0:64, :].rearrange("p (h t) -> p h t", t=64)
                            TT("dve", diff[:, hs].rearrange("p (h t) -> p h t", t=64), r3,
                               bc_last(acT[:, hf * 8:(hf + 1) * 8], 64), ALU.subtract, ["repm", "acT"], [("diff", hf)])
                            TT("dve", edt[:, hf * 8:(hf + 1) * 8], r3[:, :, 63], acT[:, hf * 8:(hf + 1) * 8], ALU.subtract,
                               ["repm", "acT"], ["edt"])
                            ACT(diff[:, hs], diff[:, hs], AF.Exp, [("diff", hf)], [("diff", hf)])
                            ACT(eR[:, hs], rep1[:], AF.Exp, ["rep1"], [("eR", hf)])
                            ACT(ea[:, hf * 8:(hf + 1) * 8], rep1[:].rearrange("p (h t) -> p h t", t=64)[:, :, 63], AF.Exp,
                                ["rep1"], [("eatot", pb)])
                            TT("dve", Mg[0:64, hs].rearrange("p (g r t) -> p g r t", r=4, t=64),
                               diff[:, hs].rearrange("p (g r t) -> p g r t", r=4, t=64),
                               psml[0:64, 64 + hf * 128:64 + (hf + 1) * 128].rearrange("p (g t) -> p g t", t=64).unsqueeze(2).to_broadcast([64, 2, 4, 64]),
                               ALU.mult, [("diff", hf), "psml_cb"], [("Maug", pb)])
                            TT("pool", Cc[:, hs].rearrange("p (g r t) -> p g r t", r=4, t=64),
                               eR[:, hs].rearrange("p (g r t) -> p g r t", r=4, t=64),
                               CFall[:, hf * 2:(hf + 1) * 2, tcol + cs:tcol + cs + 64].unsqueeze(2).to_broadcast([128, 2, 4, 64]),
                               ALU.mult, [("eR", hf), ("CF", 2 * hf), ("CF", 2 * hf + 1)], [("Ct", pb)])
                        ACT(edt[:], edt[:], AF.Exp, ["edt"], ["edt"])
                        TT("pool", xd[:].rearrange("p (h q) -> p h q", q=64), xa[0:64, :].rearrange("p (h q) -> p h q", q=64),
                           bc_last(edt[:], 64), ALU.mult, [("xaug", pb), "edt"], [("xdec", pb)])

                    def back(c, ys=ys, ti=ti):
                        cs = c * 64
                        pb = c % 2
                        Mg, xa, Cc, xd, bt, ea = Maug2[pb], xaug2[pb], Ct2[pb], xdec2[pb], btok2[pb], eatot2[pb]
                        for h in range(16):
                            jp, hh = h // 2, h % 2
                            o_ap = yp[hh * 64:(hh + 1) * 64, jp * 64:(jp + 1) * 64]
                            MM(o_ap, xa[:, h * 64:(h + 1) * 64], Mg[:, h * 64:(h + 1) * 64], True, False,
                               [("xaug", pb), ("Maug", pb), "MaugC"], ["yp"])
                            MM(o_ap, Hb[:, h * 64:(h + 1) * 64], Cc[:, h * 64:(h + 1) * 64], False, True,
                               ["Hb", ("Ct", pb)], ["yp"])
                        CP("act", ys[:, :, cs:cs + 64], yp[:].rearrange("p (j t) -> p j t", t=64), ["yp"], [("y_sb", ti % 2)])
                        for g in range(4):
                            MM(pinU[:, g * 256:(g + 1) * 256], bt[:, g * 128:(g + 1) * 128], xd[:, g * 256:(g + 1) * 256], True, True,
                               [("btok", pb), ("xdec", pb)], [("pin", g // 2)])
                        TT("dve", Hs[:].rearrange("p (h q) -> p h q", q=64), Hs[:].rearrange("p (h q) -> p h q", q=64),
                           bc_last(ea[:], 64), ALU.mult, ["Hs", ("eatot", pb)], ["Hs"])
                        TT("dve", Hs[:], Hs[:], pinU[:], ALU.add, ["Hs", ("pin", 0), ("pin", 1)], ["Hs"])
                        CP("act", Hb[:], Hs[:], ["Hs"], ["Hb"])

                    if A3_CHUNKS > 0:
                        front(0)
                    for c in range(A3_CHUNKS):
                        if c + 1 < A3_CHUNKS:
                            front(c + 1)
                        back(c)
                    DMA("sp", yl_d[:, :, tcol:tcol + TA], ys[:], [("y_sb", ti % 2)], ["yl_d"])
                if dbg:
                    DMA("sp", dbg_out("yl", [128, 8, T], BF16)[:, :, :], yl_d, ["yl_d"], ["dbg0"])
                    DMA("sp", dbg_out("sz", [128, 8, T], BF16)[:, :, :], sz_d, ["sz_d"], ["dbg1"])
                    DMA("sp", dbg_out("Hs", [128, 1024])[:, :], Hs[:], ["Hs"], ["dbg2"])
                    DMA("sp", dbg_out("acumG", [16, T])[:, :], acumG[:], ["acumG"], ["dbg3"])
                    DMA("sp", dbg_out("CF", [128, 4, T], BF16)[:, :, :], CFall[:], [("CF", g) for g in range(4)], ["dbg4"])
                    if A3_CHUNKS == 1:
                        DMA("sp", dbg_out("acT", [64, 16])[:, :], acT[:], ["acT"], ["dbg5"])
                        DMA("sp", dbg_out("dtx", [128, 16])[:, :], dtx[:], ["dtx"], ["dbg6"])
                        DMA("sp", dbg_out("Wm", [64, 1024])[:, :], diff[:], [("diff", 0), ("diff", 1)], ["dbg7"])
                        DMA("sp", dbg_out("Maug", [128, 1024], BF16)[:, :], Maug[:], [("Maug", 0), ("Maug", 1)], ["dbg8"])
                        DMA("sp", dbg_out("xaug", [128, 1024], BF16)[:, :], xaug[:], ["xaug"], ["dbg9"])
                        DMA("sp", dbg_out("xdec", [64, 1024], BF16)[:, :], xdec[:], ["xdec"], ["dbg10"])
                        DMA("sp", dbg_out("btok", [64, 512], BF16)[:, :], btok[:], ["btok"], ["dbg11"])
                        DMA("sp", dbg_out("Ct", [128, 1024], BF16)[:, :], Ct[:], [("Ct", 0), ("Ct", 1)], ["dbg12"])
                        DMA("sp", dbg_out("eatot", [128, 16])[:, :], eatot[:], [("eatot", 0), ("eatot", 1)], ["dbg13"])
                        DMA("sp", dbg_out("edt", [64, 16])[:, :], edt[:], ["edt"], ["dbg14"])
                        DMA("sp", dbg_out("acum", [16, 512])[:, :], acum[:], ["acum"], ["dbg15"])
                        DMA("sp", dbg_out("dtv", [16, 512])[:, :], dtv[:], ["dtv"], ["dbg16"])
                emit()

            with _Phase() as P:
                stop_check()
                xbuf = sb(P, "xbuf", [128, EXW], F32)
                rblk = sb(P, "rblk", [128, 7, EXW], F32)
                Sin_h = sb(P, "Sin_h", [128, 1024], F32)
                Sin_s = sb(P, "Sin_s", [128, 1024], F32)
                Tm = sb(P, "Tm", [128, 1024], F32)
                Tm2 = sb(P, "Tm2", [128, 1024], F32)
                agl = sb(P, "agl", [16, 16], F32)
                ones16 = sb(P, "ones16x", [16, 128], F32)
                pr = ps(P, "pr", [128, 512], F32)

                MEMSET("pool", xbuf[:, 2072:3072], 0.0, ["xbuf5"])
                CP("dve", xbuf[:, 0:1024], S_hg[:].rearrange("p h v -> p (h v)"), (), ["xbuf"])
                CP("pool", xbuf[:, 1024:2048], Hs[:], (), ["xbuf2"])
                ACT(xbuf[:, 2048:2056], Bprev[:], AF.Exp, (), ["xbuf3"])
                MEMSET("pool", ones16[:], 1.0, ["ones16x"])
                TT("dve", agl[:], ident_f[0:16, 0:16], acumG[:, T - 1:T].to_broadcast([16, 16]), ALU.mult, (), ["agl"])
                MM(pr[:, 0:16], ones16[:], agl[:], True, True, ["ones16x", "agl"], ["pr"])
                ACT(xbuf[:, 2056:2072], pr[:, 0:16], AF.Exp, ["pr"], ["xbuf4"])
                NOX = int(os.environ.get("K_NOX", 0))
                if not NOX:
                    off = 0
                    for i, w in enumerate(EX_W):
                        DMA("sp", ex_in[i].ap(), xbuf[:, off:off + w], ["xbuf", "xbuf2", "xbuf3", "xbuf4", "xbuf5"], [("ex_in", i)])

                        def cc1(e, i=i):
                            return e.collective_compute("AllGather", ALU.bypass, replica_groups=[list(range(NCORES))],
                                                        ins=[ex_in[i].ap().opt()], outs=[ex_out[i].ap().opt()])
                        S.op("pool", cc1, [("ex_in", i)], [("ex_out", i)])
                        off += w
                    DMA("sp", fence_in.ap(), xbuf[:, 2072:2136], ["xbuf5"], ["fence_in"])

                    def ccf(e):
                        return e.collective_compute("AllReduce", ALU.add, replica_groups=[list(range(NCORES))],
                                                    ins=[fence_in.ap().opt()], outs=[fence_out[0].ap().opt()])
                    S.op("pool", ccf, ["fence_in"] + [("ex_out", i) for i in range(3)], ["fence"])
                    off = 0
                    for i, w in enumerate(EX_W):
                        DMA("sp", rblk[:, :, off:off + w], ex_out[i].ap()[0:7 * 128, :].rearrange("(r p) w -> p r w", p=128),
                            [("ex_out", i), "fence"], ["rblk"])
                        off += w
                else:
                    MEMSET("pool", rblk[:], 0.0, ["rblk"])
                MEMSET("pool", Sin_h[:], 0.0, ["Sin_h"])
                MEMSET("pool", Sin_s[:], 0.0, ["Sin_s"])
                Dp = sb(P, "Dp", [128, 7, 24], F32)
                for r in range(7):
                    mcol = flags[:, 1 + r:2 + r]
                    TS("dve", Dp[:, r, :], rblk[:, r, 2048:2072], -1.0, mcol, ALU.add, ALU.mult, ["rblk", "flags"], [("Dp", r)])
                    TS("dve", Dp[:, r, :], Dp[:, r, :], 1.0, None, ALU.add, None, [("Dp", r)], [("Dp", r)])
                    ACT(Tm[:], rblk[:, r, 0:1024], AF.Copy, ["rblk", "flags"], ["Tm"], scale=mcol)
                    TT("dve", Sin_h[:].rearrange("p (h v) -> p h v", v=128), Sin_h[:].rearrange("p (h v) -> p h v", v=128),
                       bc_last(Dp[:, r, 0:8], 128), ALU.mult, ["Sin_h", ("Dp", r)], ["Sin_h"])
                    TT("dve", Sin_h[:], Sin_h[:], Tm[:], ALU.add, ["Tm", "Sin_h"], ["Sin_h"])
                    ACT(Tm2[:], rblk[:, r, 1024:2048], AF.Copy, ["rblk", "flags"], ["Tm2"], scale=mcol)
                    TT("pool", Sin_s[:].rearrange("p (h q) -> p h q", q=64), Sin_s[:].rearrange("p (h q) -> p h q", q=64),
                       bc_last(Dp[:, r, 8:24], 64), ALU.mult, ["Sin_s", ("Dp", r)], ["Sin_s"])
                    TT("pool", Sin_s[:], Sin_s[:], Tm2[:], ALU.add, ["Tm2", "Sin_s"], ["Sin_s"])
                CP("act", Sin_hb[:].rearrange("p h v -> p (h v)"), Sin_h[:], ["Sin_h"], ["Sin_hb"])
                CP("act", Sin_sb[:], Sin_s[:], ["Sin_s"], ["Sin_sb"])
                if dbg:
                    DMA("sp", dbg_out("rblk", [128, 7, EXW])[:, :, :], rblk[:], ["rblk"], ["dbg0"])
                    DMA("sp", dbg_out("Sin_h", [128, 1024])[:, :], Sin_h[:], ["Sin_h"], ["dbg1"])
                    DMA("sp", dbg_out("Sin_s", [128, 1024])[:, :], Sin_s[:], ["Sin_s"], ["dbg2"])
                    DMA("sp", dbg_out("xbuf", [128, EXW])[:, :], xbuf[:], ["xbuf", "xbuf2", "xbuf3", "xbuf4"], ["dbg3"])
                emit()

            with _Phase() as P:
                stop_check()
                bufs = {
                    "junk": sb(P, "junk", [128, 2048], BF16),
                    "ss": sb(P, "ss", [128, 4], F32),
                    "xn": sb(P, "xn", [128, 4, 2048], BF16),
                    "tp": [ps(P, f"tp{i}", [128, 512], BF16) for i in range(2)],
                }
                g1b = sb(P, "g1b", [128, 2048], F32)
                selm = sb(P, "selm", [16, 1024], F32)
                hq = [sb(P, f"hq{i}", [128, 512], BF16) for i in range(2)]
                ho = [sb(P, f"ho{i}", [128, 512], BF16) for i in range(2)]
                hg = [sb(P, f"hg{i}", [128, 512], BF16) for i in range(2)]
                yl = [sb(P, f"yl{i}", [128, 512], BF16) for i in range(2)]
                szl = [sb(P, f"szl{i}", [128, 512], BF16) for i in range(2)]
                Qg = [sb(P, f"Qg{i}", [128, 512], BF16) for i in range(2)]
                tO = [sb(P, f"tO{i}", [128, 512], F32) for i in range(4)]
                sq = [sb(P, f"sq{i}", [128, 512], BF16) for i in range(4)]
                rs = [sb(P, f"rs{i}", [128, 512], F32) for i in range(2)]
                eP = [sb(P, f"eP{i}", [128, 512], F32) for i in range(2)]
                mixT = sb(P, "mixT", [128, 16, 512], BF16)
                wo = [sb(P, f"wo{i}", [128, 16, 256], BF16) for i in range(2)]
                x1t = sb(P, "x1t", [128, 4, 2048], F32)
                tmpx = [sb(P, f"tmpx{i}", [128, 256], F32) for i in range(2)]
                h2Tt = sb(P, "h2Tt", [128, 16, 512], BF16)
                h2l = sb(P, "h2l", [128, 1024], F32)
                pc = [ps(P, f"pc{i}", [128, 512], F32) for i in range(2)]
                pss = [ps(P, f"pss{i}", [128, 512], F32) for i in range(2)]
                prp = ps(P, "prp", [128, 512], F32)
                w_out_v = w_out[0].rearrange("(k p) d -> p k d", p=128)

                DMA("sp", g1b[:], mod_d[0:1, 2 * 2048:3 * 2048].partition_broadcast(128), (), ["g1b"])
                DMA("act", x1t[:, 0, :], b_mod[0:1, 2 * 2048:3 * 2048].partition_broadcast(128), (), [("x1t", 0)])
                TT("pool", g1b[:], g1b[:], x1t[:, 0, :], ALU.add, ["g1b", ("x1t", 0)], ["g1b"])
                DMA("sp", selm[:], cst[0:16, C_SEL:C_SEL + 1024], (), ["selm"])
                wcnt = 0
                for ti in range(NT):
                    tcol = ti * TA
                    for j in range(4):
                        r0 = 3 + tcol + j * 128
                        DMA("act", x1t[:, j, :], x_c[r0:r0 + 128, :], (), [("x1t", j)])
                    for h in range(8):
                        b2 = h % 2
                        DMA("sp", hq[b2][:], hq_d[:, h, tcol:tcol + TA], (), [("hq", b2)])
                        DMA("sp", ho[b2][:], ho_d[:, h, tcol:tcol + TA], (), [("ho", b2)])
                        DMA("sp", hg[b2][:], hg_d[:, h, tcol:tcol + TA], (), [("hg", b2)])
                        q_ = Qg[b2]
                        TT("pool", q_[:].rearrange("p (c t) -> p c t", t=64), hq[b2][:].rearrange("p (c t) -> p c t", t=64),
                           bc_last(Gall[:, h, ti * 8:(ti + 1) * 8], 64), ALU.mult, [("hq", b2)], [("Qg", b2)])
                        MM(pc[b2][:], Sin_hb[:, h, :], q_[:], True, True, [("Qg", b2)], [("pc", b2)])
                        to = tO[h % 4]
                        tok = ("tO", h % 4)
                        TT("dve", to[:], pc[b2][:], ho[b2][:], ALU.add, [("pc", b2), ("ho", b2)], [tok])
                        ACT(sq[h % 4][:], to[:], AF.Square, [tok], [("sq", h % 4)])
                        MM(pss[b2][:], ones_b[:], sq[h % 4][:], True, True, [("sq", h % 4)], [("pss", b2)])
                        r_ = rs[b2]
                        ACT(r_[:], pss[b2][:], AF.Ln, [("pss", b2)], [("rs", b2)], bias=epsb[:], scale=1.0 / 128)
                        ACT(r_[:], r_[:], AF.Exp, [("rs", b2)], [("rs", b2)], scale=-0.5)
                        TT("dve", to[:], to[:], r_[:], ALU.mult, [tok, ("rs", b2)], [tok])
                        TT("pool", mixT[:, h, :], to[:], hg[b2][:], ALU.mult, [tok, ("hg", b2)], [("mixT", h)])
                    for g in range(4):
                        for jj in range(2):
                            j = 2 * g + jj
                            DMA("sp", yl[jj][:], yl_d[:, j, tcol:tcol + TA], (), [("yl", jj)])
                            DMA("sp", szl[jj][:], sz_d[:, j, tcol:tcol + TA], (), [("szl", jj)])
                            for hh in range(2):
                                hd = 2 * j + hh
                                MM(pc[jj][hh * 64:(hh + 1) * 64, :], Sin_sb[:, hd * 64:(hd + 1) * 64], CFall[:, g, tcol:tcol + TA], True, True,
                                   (), [("pc", jj)])
                            MM(prp[:], selm[:, j * 128:(j + 1) * 128], acumG[:, tcol:tcol + TA], True, True, ["selm"], ["prp"])
                            ACT(eP[jj][:], prp[:], AF.Exp, ["prp"], [("eP", jj)])
                            to = tO[jj]
                            tok = ("tO", jj)
                            TT("dve", to[:], pc[jj][:], eP[jj][:], ALU.mult, [("pc", jj), ("eP", jj)], [tok])
                            TT("pool", to[:], to[:], yl[jj][:], ALU.add, [tok, ("yl", jj)], [tok])
                            TT("pool", to[:], to[:], szl[jj][:], ALU.mult, [tok, ("szl", jj)], [tok])
                            ACT(sq[jj][:], to[:], AF.Square, [tok], [("sq", jj)])
                            MM(pss[0][:], ones_b[:], sq[jj][:], jj == 0, jj == 1, [("sq", jj)], [("pss", 0)])
                        r_ = rs[0]
                        ACT(r_[:], pss[0][:], AF.Ln, [("pss", 0)], [("rs", 0)], bias=epsb[:], scale=1.0 / 256)
                        ACT(r_[:], r_[:], AF.Exp, [("rs", 0)], [("rs", 0)], scale=-0.5)
                        for jj in range(2):
                            j = 2 * g + jj
                            STT(mixT[:, 8 + j, :], tO[jj][:], snw[:, j:j + 1], r_[:], ALU.mult, ALU.mult,
                                [("tO", jj), ("rs", 0), "snw"], [("mixT", 8 + j)])
                    for dg in range(8):
                        wi = wcnt % 2
                        wcnt += 1
                        DMA("pool", wo[wi][:], w_out_v[:, :, dg * 256:(dg + 1) * 256], (), [("wo", wi)])
                        for sub in range(4):
                            pq = pc[sub % 2]
                            for k in range(16):
                                MM(pq[:, 0:256], mixT[:, k, sub * 128:(sub + 1) * 128], wo[wi][:, k, :], k == 0, k == 15,
                                   [("mixT", k), ("wo", wi)], [("pc", sub % 2)])
                            tx = tmpx[sub % 2]
                            TT("dve", tx[:], pq[:, 0:256], g1b[:, dg * 256:(dg + 1) * 256], ALU.mult, [("pc", sub % 2), "g1b"], [("tmpx", sub % 2)])
                            TT("pool", x1t[:, sub, dg * 256:(dg + 1) * 256], x1t[:, sub, dg * 256:(dg + 1) * 256], tx[:], ALU.add,
                               [("tmpx", sub % 2), ("x1t", sub)], [("x1t", sub)])
                    for j in range(4):
                        DMA("sp", x1_d[tcol + j * 128:tcol + (j + 1) * 128, :], x1t[:, j, :], [("x1t", j)], ["x1_d"])
                    norm_to_T(bufs, "h2Tt", 4, lambda j: (x1t[:, j, :], [("x1t", j)]), a2, shift2, h2Tt)
                    DMA("act", h2T_d[:, :, 2 + tcol:2 + tcol + TA], h2Tt[:], [("h2Tt", dk) for dk in range(16)], ["h2T_d"])
                    if ti == NT - 1:
                        MEMSET("pool", h2l[:], 0.0, ["h2l"])
                        CP("dve", h2l[:, 0:32].rearrange("p (k t) -> p k t", t=2), h2Tt[:, :, 510:512], [("h2Tt", dk) for dk in range(16)] + ["h2l"], ["h2l"])
                        DMA("sp", ex2_in.ap(), h2l[:], ["h2l"], ["ex2_in"])
                issue_cast(1000)
                S.drain_bg = True
                if dbg:
                    DMA("sp", dbg_out("x1", [T, D])[:, :], x1_d, ["x1_d"], ["dbg0"])
                    DMA("sp", dbg_out("h2T", [128, 16, 2 + T], BF16)[:, :, :], h2T_d, ["h2T_d"], ["dbg1"])
                emit()

        with _Phase() as P:
            stop_check()
            r2 = sb(P, "r2", [128, 7, 32], F32)
            hacc = sb(P, "hacc", [128, 32], F32)
            hout = sb(P, "hout", [128, 16, 2], BF16)

            def cc2(e):
                return e.collective_compute("AllGather", ALU.bypass, replica_groups=[list(range(NCORES))],
                                            ins=[ex2_in.ap().opt()], outs=[ex2_out.ap().opt()])
            if not int(os.environ.get("K_NOX", 0)):
                S.op("pool", cc2, (), ["ex2_out"])

                def ccf2(e):
                    return e.collective_compute("AllReduce", ALU.add, replica_groups=[list(range(NCORES))],
                                                ins=[fence_in.ap().opt()], outs=[fence_out[1].ap().opt()])
                S.op("pool", ccf2, ["ex2_out"], ["fence2"])
                DMA("sp", r2[:], ex2_out.ap()[0:7 * 128, 0:32].rearrange("(r p) w -> p r w", p=128), ["ex2_out", "fence2"], ["r2"])
            else:
                MEMSET("pool", r2[:], 0.0, ["r2"])
            MEMSET("pool", hacc[:], 0.0, ["hacc"])
            for r in range(7):
                STT(hacc[:], r2[:, r, :], flags[:, 8 + r:9 + r], hacc[:], ALU.mult, ALU.add, ["r2", "hacc"], ["hacc"])
            if int(os.environ.get("K_NOX", 0)) == 2:
                DMA("sp", hacc[:], ex2_in.ap()[:, 0:32], ["hacc"], ["hacc"])
            CP("dve", hout[:].rearrange("p k t -> p (k t)"), hacc[:], ["hacc"], ["hout"])
            DMA("sp", h2T_d[:, :, 0:2], hout[:], ["hout"], ["h2T_d"])
            if dbg:
                DMA("sp", dbg_out("r2", [128, 7, 32])[:, :, :], r2[:], ["r2"], ["dbg0"])
                DMA("sp", dbg_out("hacc", [128, 32])[:, :], hacc[:], ["hacc"], ["dbg1"])
                DMA("sp", dbg_out("h2Tb", [128, 16, 2 + T], BF16)[:, :, :], h2T_d, ["h2T_d"], ["dbg2"])
            emit()

        with _Phase() as P:
            stop_check()
            g2b = sb(P, "g2b", [128, 2048], F32)
            fwb = sb(P, "fwb", [128, 2048], F32)
            fcw = sb(P, "fcw", [128, 88, 3], F32)
            fcb = sb(P, "fcb", [128, 88], F32)
            uhalo = sb(P, "uhalo", [128, 88, 2], F32)
            h2Tt = [sb(P, "h2Tt0", [128, 16, 514], BF16)] * 2
            a_half = sb(P, "a_half", [128, 22, 512], BF16)
            wu = [sb(P, f"wu{i}", [128, 16, 2, 256], BF16) for i in range(3)]
            wd = [sb(P, f"wd{i}", [128, 11, 512], BF16) for i in range(4)]
            ur = [sb(P, f"ur{i}", [128, 514], F32) for i in range(2)]
            ac2 = [sb(P, f"ac2_{i}", [128, 512], F32) for i in range(2)]
            sgf = sb(P, "sgf", [128, 512], F32)
            x2t = sb(P, "x2t", [128, 4, 2048], F32)
            tmpx = [sb(P, f"tmpx{i}", [128, 512], F32) for i in range(2)]
            ssf = sb(P, "ssf", [128, 4], F32)
            junk = sb(P, "junk", [128, 2048], BF16)
            ppu = [ps(P, f"ppu{i}", [128, 512], F32) for i in range(4)]
            ppd = [ps(P, f"ppd{i}", [128, 512], F32) for i in range(2)]
            pph = ps(P, "pph", [128, 512], F32)
            w_up_v = ffn_w_up[0].rearrange("(k p) c -> p k c", p=128)
            w_dn_v = ffn_w_down[0].rearrange("(i p) d -> p i d", p=128)

            DMA("sp", g2b[:], mod_d[0:1, 5 * 2048:6 * 2048].partition_broadcast(128), (), ["g2b"])
            DMA("act", fwb[:], b_mod[0:1, 5 * 2048:6 * 2048].partition_broadcast(128), (), ["fwb"])
            TT("pool", g2b[:], g2b[:], fwb[:], ALU.add, ["g2b", "fwb"], ["g2b"])
            DMA("sp", fwb[:], final_w.rearrange("(o d) -> o d", o=1).partition_broadcast(128), ["fwb"], ["fwb"])
            for j in range(3):
                DMA("sp", fcw[:, :, j], ffn_conv_w[0, j:j + 1, :].rearrange("o (c p) -> p (o c)", p=128), (), ["fcw"])
            DMA("sp", fcb[:], ffn_conv_b.rearrange("o (c p) -> p (o c)", p=128), (), ["fcb"])
            wcu = 0
            wcd = 0
            for ti in range(NT):
                tcol = ti * TA
                hb = h2Tt[0]
                hk = ("h2Tt", 0)
                DMA("sp", hb[:], h2T_d[:, :, tcol:tcol + 514], (), [hk])
                for j in range(4):
                    DMA("act", x2t[:, j, :], x1_d[tcol + j * 128:tcol + (j + 1) * 128, :], (), [("x2t", j)])
                for hf in range(2):
                    for gi in range(11):
                        wi = wcu % 3
                        wcu += 1
                        DMA("sp" if wcu % 2 else "act", wu[wi][:].rearrange("p k g c -> p (k g c)"), wup_t[hf * 11 + gi], (), [("wu", wi)])
                        for half in range(2):
                            fl = gi * 2 + half
                            fc = hf * 22 + fl
                            accs = []
                            for gv in range(2):
                                ch = gv * 44 + fc
                                pp = ppu[(fl % 2) * 2 + gv]
                                pk = ("ppu", (fl % 2) * 2 + gv)
                                for k in range(16):
                                    MM(pp[:], wu[wi][:, k, gv, half * 128:(half + 1) * 128], hb[:, k, 2:514], k == 0, k == 15,
                                       [("wu", wi), hk], [pk])
                                u = ur[gv]
                                uk = ("ur", gv)
                                if ti == 0:
                                    for k in range(16):
                                        MM(pph[:, gv * 2:gv * 2 + 2], wu[wi][:, k, gv, half * 128:(half + 1) * 128], hb[:, k, 0:2], k == 0, k == 15,
                                           [("wu", wi), hk], [("pph", gv)])
                                    CP("act", u[:, 0:2], pph[:, gv * 2:gv * 2 + 2], [("pph", gv)], [uk])
                                else:
                                    CP("pool", u[:, 0:2], uhalo[:, ch, :], [("uhalo", ch)], [uk])
                                ACT(u[:, 2:514], pp[:], AF.Copy, [pk, uk], [uk])
                                CP("pool", uhalo[:, ch, :], u[:, 512:514], [uk], [("uhalo", ch)])
                                ac = ac2[gv]
                                ak = ("ac2", gv)
                                TS("dve", ac[:], u[:, 2:514], fcw[:, ch, 2:3], fcb[:, ch:ch + 1], ALU.mult, ALU.add, [uk, "fcw", "fcb"], [ak])
                                STT(ac[:], u[:, 1:513], fcw[:, ch, 1:2], ac[:], ALU.mult, ALU.add, [uk, ak, "fcw"], [ak])
                                STT(ac[:], u[:, 0:512], fcw[:, ch, 0:1], ac[:], ALU.mult, ALU.add, [uk, ak, "fcw"], [ak])
                                accs.append((ac, ak))
                            ACT(sgf[:], accs[0][0][:], AF.Silu, [accs[0][1]], ["sgf"])
                            TT("pool", a_half[:, fl, :], sgf[:], accs[1][0][:], ALU.mult, ["sgf", accs[1][1]], [("a_half", fl)])
                    for dg in range(4):
                        wis = []
                        for j in range(2):
                            wi = wcd % 4
                            wcd += 1
                            wis.append(wi)
                            DMA("sp" if wcd % 2 else "act", wd[wi][:].rearrange("p i d -> p (i d)"), wdn_t[(hf * 4 + dg) * 2 + j], (), [("wd", wi)])
                        for sub in range(4):
                            pq = ppd[sub % 2]
                            for i in range(22):
                                MM(pq[:], a_half[:, i, sub * 128:(sub + 1) * 128], wd[wis[i // 11]][:, i % 11, :], i == 0, i == 21,
                                   [("a_half", i), ("wd", wis[i // 11])], [("ppd", sub % 2)])
                            tx = tmpx[sub % 2]
                            TT("dve", tx[:], pq[:], g2b[:, dg * 512:(dg + 1) * 512], ALU.mult, [("ppd", sub % 2), "g2b"], [("tmpx", sub % 2)])
                            TT("pool", x2t[:, sub, dg * 512:(dg + 1) * 512], x2t[:, sub, dg * 512:(dg + 1) * 512], tx[:], ALU.add,
                               [("tmpx", sub % 2), ("x2t", sub)], [("x2t", sub)])
                for j in range(4):
                    ACT(junk[:], x2t[:, j, :], AF.Square, [("x2t", j)], [("ssf", j)], accum_out=ssf[:, j:j + 1])
                    TS("pool", ssf[:, j:j + 1], ssf[:, j:j + 1], 1.0 / D, EPS, ALU.mult, ALU.add, [("ssf", j)], [("ssf", j)])
                    TT("pool", ssf[:, j:j + 1], ssf[:, j:j + 1], negh[:], ALU.pow, [("ssf", j)], [("ssf", j)])
                    STT(x2t[:, j, :], x2t[:, j, :], ssf[:, j:j + 1], fwb[:], ALU.mult, ALU.mult, [("x2t", j), ("ssf", j), "fwb"], [("x2t", j)])
                    DMA("sp", y_out[tcol + j * 128:tcol + (j + 1) * 128, :], x2t[:, j, :], [("x2t", j)], ["y_out"])
            emit()
    return nc


def _consts():
    c = np.zeros((128, C_W), np.float32)
    c[:, C_ID:C_ID + 128] = np.eye(128, dtype=np.float32)
    sm = np.ones(512, np.float32)
    sm[::64] = 0.0
    c[:, C_SCAN:C_SCAN + 512] = sm[None, :]
    s = np.arange(64)[:, None]
    t = np.arange(64)[None, :]
    m01 = (t >= s).astype(np.float32)
    c[0:64, C_ATT:C_ATT + 512] = np.tile(m01, (1, 8))
    neg = np.where(t >= s, 0.0, -30000.0).astype(np.float32)
    c[0:64, C_NEG:C_NEG + 1024] = np.tile(neg, (1, 16))
    for h in range(16):
        c[h, C_BLK + h * 64:C_BLK + (h + 1) * 64] = 1.0
        c[h, C_SEL + h * 64:C_SEL + (h + 1) * 64] = 1.0
    c[0:64, C_DIAG:C_DIAG + 1024] = np.tile(np.eye(64, dtype=np.float32), (1, 16))
    lm = np.zeros((96, 64), np.float32)
    lm[0:16, :] = 1.0
    lm[32:96, :] = np.eye(64, dtype=np.float32)
    c[0:96, C_LM:C_LM + 64] = lm
    return c


_NC_CACHE = {}


def kernel(**inputs):
    x = np.ascontiguousarray(np.asarray(inputs["x"], dtype=np.float32))[0]
    if "nc" not in _NC_CACHE:
        _NC_CACHE["nc"] = build_program()
    nc = _NC_CACHE["nc"]
    cst = _consts()
    shared = {}
    for k in ("c", "b_mod", "norm1_w", "w_in", "hgrn_lb", "hgrn_gnorm_w", "ssd_conv_w", "ssd_conv_b",
              "ssd_dt_bias", "ssd_a_log", "ssd_d", "ssd_norm_w", "w_out", "norm2_w", "ffn_w_up", "ffn_conv_w",
              "ffn_conv_b", "ffn_w_down", "final_norm_w"):
        shared[k] = np.ascontiguousarray(np.asarray(inputs[k], dtype=np.float32))
    in_maps = []
    wmod_full = np.asarray(inputs["w_mod"], dtype=np.float32)[0]
    for r in range(NCORES):
        xc = np.zeros((T + 3, D), np.float32)
        if r > 0:
            xc[0:3] = x[r * T - 3:r * T]
        xc[3:] = x[r * T:(r + 1) * T]
        fl = np.zeros((128, 16), np.float32)
        fl[:, 0] = 1.0 if r > 0 else 0.0
        for q in range(7):
            fl[:, 1 + q] = 1.0 if q < r else 0.0
            fl[:, 8 + q] = 1.0 if q == r - 1 else 0.0
        m = dict(shared)
        m["w_mod_c"] = np.ascontiguousarray(wmod_full[:, r * 1536:(r + 1) * 1536])
        m["x_c"] = xc
        m["flags"] = fl
        m["cst"] = cst
        in_maps.append(m)
    res = run_bass_kernel_spmd(nc, in_maps, core_ids=list(range(NCORES)))
    out = np.concatenate([res.results[r]["y_out"] for r in range(NCORES)], axis=0)
    return out.reshape(1, NCORES * T, D).astype(np.float32)
```
